# Optimizing a Trainium2 kernel written in Bass

```python
import math
import jax, jax.numpy as jnp
from jax import lax
import numpy as np

D_MODEL = 2048
BATCH = 2
SEQ = 16384
DEPTH = 2
DEC_BATCH = 2
DEC_SEQ = 4096
PAST_LEN = 128

N_MEM = 256
RW_HEADS = 12
RW_HEAD_DIM = 64
RW_WIDTH = RW_HEADS * RW_HEAD_DIM
RW_DECAY_RANK = 96
RW_AAA_RANK = 96
RW_MV_RANK = 64
RW_GATE_RANK = 256
HG_HEADS = 6
HG_HEAD_DIM = 128
HG_WIDTH = HG_HEADS * HG_HEAD_DIM
HG_CHUNK = 64
XA_HEADS = 4
XA_HEAD_DIM = 128
XA_WIDTH = XA_HEADS * XA_HEAD_DIM
N_BRANCH = 3
MIX_WIDTH = RW_WIDTH + HG_WIDTH + XA_WIDTH
RW_COLS = 3 * RW_WIDTH + 2 * RW_DECAY_RANK + 2 * RW_AAA_RANK + RW_GATE_RANK
HG_COLS = 5 * HG_WIDTH
XA_COLS = XA_WIDTH
GATE_COLS = N_BRANCH * D_MODEL
IN_COLS = RW_COLS + HG_COLS + XA_COLS + GATE_COLS
D_FF = 5504
CONV_WIDTH = 3
NORM_EPS = 1e-6
LNX_EPS = 64e-5
DECAY_SCALE = 0.6065306597126334

kernel_name = "hybrid_rwkv7_hgrn2_memxattn_encoder"


def _cuts(sizes):
    cuts, acc = [], 0
    for s in sizes[:-1]:
        acc += s
        cuts.append(acc)
    return cuts


def _heads(t, n_heads):
    return t.reshape(t.shape[:-1] + (n_heads, t.shape[-1] // n_heads))


def _flip(t):
    return t[:, ::-1]


def rmsnorm(x, g):
    xf = x.astype(jnp.float32)
    y = xf * lax.rsqrt(jnp.mean(xf * xf, axis=-1, keepdims=True) + NORM_EPS)
    return (y * g.astype(jnp.float32)).astype(x.dtype)


def centred_conv3(x, w):
    xp = jnp.pad(x, ((0, 0), (1, 1), (0, 0)))
    return xp[:, :-2] * w[0] + xp[:, 1:-1] * w[1] + xp[:, 2:] * w[2]


def rwkv7_scan(r, w, k, v, kk, a):
    B, S, H, N = r.shape

    def step(state, inp):
        r_t, w_t, k_t, v_t, kk_t, a_t = inp
        sa = jnp.einsum('bhvk,bhk->bhv', state, -kk_t)
        state = (state * w_t[:, :, None, :]
                 + sa[..., None] * (kk_t * a_t)[:, :, None, :]
                 + v_t[..., None] * k_t[:, :, None, :])
        y = jnp.einsum('bhvk,bhk->bhv', state, r_t)
        return state, y

    xs = tuple(jnp.moveaxis(t, 1, 0) for t in (r, w, k, v, kk, a))
    s0 = jnp.zeros((B, H, N, N), jnp.float32)
    _, ys = lax.scan(step, s0, xs)
    return jnp.moveaxis(ys, 0, 1)


def gla_chunk_scan(q, log_f, k, v):
    B, S, H, DK = q.shape
    DV = v.shape[-1]
    n_chunks = S // HG_CHUNK

    def to_chunks(t):
        return t.reshape(B, n_chunks, HG_CHUNK, H, t.shape[-1]).transpose(1, 0, 3, 2, 4)

    qc, fc, kc, vc = (to_chunks(t) for t in (q, log_f, k, v))
    lower = jnp.tril(jnp.ones((HG_CHUNK, HG_CHUNK), bool))[:, :, None]

    def step(state, inp):
        q_c, lf_c, k_c, v_c = inp
        b = jnp.cumsum(lf_c, axis=2)
        o_inter = jnp.einsum('bhtk,bhkv->bhtv', q_c * jnp.exp(b), state)
        rel = jnp.where(lower, b[:, :, :, None, :] - b[:, :, None, :, :], -jnp.inf)
        scores = jnp.einsum('bhtk,bhsk,bhtsk->bhts', q_c, k_c, jnp.exp(rel))
        o = o_inter + jnp.einsum('bhts,bhsv->bhtv', scores, v_c)
        b_last = b[:, :, -1]
        k_dec = k_c * jnp.exp(b_last[:, :, None, :] - b)
        state = state * jnp.exp(b_last)[..., None] + jnp.einsum('bhsk,bhsv->bhkv', k_dec, v_c)
        return state, o

    s0 = jnp.zeros((B, H, DK, DV), jnp.float32)
    _, oc = lax.scan(step, s0, (qc, fc, kc, vc))
    return oc.transpose(1, 0, 3, 2, 4).reshape(B, S, H, DV)


def rwkv7_branch(h, cols, v_first, p, vres):
    f32 = jnp.float32
    B, S = h.shape[:2]
    cols = centred_conv3(cols, p['rw_conv'])
    r, k, v, wd_f, wd_b, ad_f, ad_b, gd = jnp.split(
        cols, _cuts([RW_WIDTH] * 3 + [RW_DECAY_RANK] * 2 + [RW_AAA_RANK] * 2 + [RW_GATE_RANK]), axis=-1)
    if vres is None:
        v_first = v
    else:
        v0, v_down, v_up = vres
        v = v + (v_first - v) * jax.nn.sigmoid(v0 + (h @ v_down) @ v_up)
    w0, w_up, a0, a_up = p['rw_w0'], p['rw_w_up'], p['rw_a0'], p['rw_a_up']
    k_a = p['rw_k_a'].astype(f32)

    def direction_terms(wd, ad, d):
        logit = (w0[d] + jnp.tanh(wd) @ w_up[d]).astype(f32)
        decay = jnp.exp(-DECAY_SCALE * jax.nn.sigmoid(logit))
        a = jax.nn.sigmoid((a0[d] + ad @ a_up[d]).astype(f32))
        k_dir = k.astype(f32) * (1.0 + (a - 1.0) * k_a)
        return _heads(decay, RW_HEADS), _heads(a, RW_HEADS), _heads(k_dir, RW_HEADS)

    dec_f, a_f, k_f = direction_terms(wd_f, ad_f, 0)
    dec_b, a_b, k_b = direction_terms(wd_b, ad_b, 1)
    r_h = _heads(r.astype(f32), RW_HEADS)
    v_h = _heads(v.astype(f32), RW_HEADS)
    kk = _heads(k.astype(f32) * p['rw_k_k'].astype(f32), RW_HEADS)
    kk = kk * lax.rsqrt(jnp.maximum(jnp.sum(kk * kk, axis=-1, keepdims=True), 1e-24))
    y_f = rwkv7_scan(r_h, dec_f, k_f, v_h, kk, a_f)
    y_b = _flip(rwkv7_scan(*(_flip(t) for t in (r_h, dec_b, k_b, v_h, kk, a_b))))
    y = y_f + y_b
    mu = jnp.mean(y, axis=-1, keepdims=True)
    var = jnp.mean(jnp.square(y - mu), axis=-1, keepdims=True)
    y = (y - mu) * lax.rsqrt(var + LNX_EPS)
    bonus = jnp.sum(r_h * (k_f + k_b) * p['rw_r_k'].astype(f32), axis=-1, keepdims=True) * v_h
    y = (y.reshape(B, S, RW_WIDTH) * p['rw_lnx_w'] + p['rw_lnx_b']
         + bonus.reshape(B, S, RW_WIDTH))
    g = jax.nn.sigmoid(gd) @ p['rw_g_up']
    return (y * g).astype(h.dtype), v_first


def hgrn2_branch(cols, lb, norm_g):
    f32 = jnp.float32
    B, S = cols.shape[:2]
    q, fz_f, fz_b, i, og = jnp.split(cols, _cuts([HG_WIDTH] * 5), axis=-1)
    q_h = _heads(jax.nn.silu(q).astype(f32), HG_HEADS)
    i_h = _heads(i.astype(f32), HG_HEADS)
    f_f = lb + (1.0 - lb) * jax.nn.sigmoid(fz_f.astype(f32))
    f_b = lb + (1.0 - lb) * jax.nn.sigmoid(fz_b.astype(f32))
    o_f = gla_chunk_scan(q_h, _heads(jnp.log(f_f), HG_HEADS), _heads(1.0 - f_f, HG_HEADS), i_h)
    o_b = _flip(gla_chunk_scan(_flip(q_h), _flip(_heads(jnp.log(f_b), HG_HEADS)),
                               _flip(_heads(1.0 - f_b, HG_HEADS)), _flip(i_h)))
    o = o_f + o_b
    o = o * lax.rsqrt(jnp.mean(o * o, axis=-1, keepdims=True) + NORM_EPS)
    o = o.reshape(B, S, HG_WIDTH) * norm_g.astype(f32) * jax.nn.silu(og.astype(f32))
    return o.astype(cols.dtype)


def memory_branch(q_cols, mem, mem_norm_g, w_mem_kv):
    B, S = q_cols.shape[:2]
    m = rmsnorm(mem, mem_norm_g)
    mk, mv = jnp.split(m @ w_mem_kv, 2, axis=-1)
    q = _heads(q_cols, XA_HEADS)
    mk = _heads(mk, XA_HEADS)
    mv = _heads(mv, XA_HEADS)
    s = jnp.einsum('bshd,bmhd->bhsm', q.astype(jnp.float32), mk.astype(jnp.float32)) * (XA_HEAD_DIM ** -0.5)
    probs = jax.nn.softmax(s, axis=-1)
    o = jnp.einsum('bhsm,bmhd->bshd', probs.astype(mv.dtype), mv)
    return o.reshape(B, S, XA_WIDTH)


def encoder_layer(x, mem, v_first, lb, p, vres):
    h = rmsnorm(x, p['norm1_g'])
    proj = h @ p['w_in']
    rw_cols, hg_cols, xa_q, gate_logits = jnp.split(
        proj, _cuts([RW_COLS, HG_COLS, XA_COLS, GATE_COLS]), axis=-1)
    y_rw, v_first = rwkv7_branch(h, rw_cols, v_first, p, vres)
    y_hg = hgrn2_branch(hg_cols, lb, p['hg_norm_g'])
    y_xa = memory_branch(xa_q, mem, p['mem_norm_g'], p['w_mem_kv'])
    wb_rw, wb_hg, wb_xa = jnp.split(p['w_branch_out'], _cuts([RW_WIDTH, HG_WIDTH, XA_WIDTH]), axis=0)
    g_rw, g_hg, g_xa = jnp.split(jax.nn.sigmoid(gate_logits), N_BRANCH, axis=-1)
    mixed = g_rw * (y_rw @ wb_rw) + g_hg * (y_hg @ wb_hg) + g_xa * (y_xa @ wb_xa)
    x = x + mixed @ p['w_o']
    h = rmsnorm(x, p['norm2_g'])
    u = centred_conv3(h @ p['ffn_up'], p['ffn_conv'])
    u_act, u_lin = jnp.split(u, 2, axis=-1)
    x = x + (jax.nn.gelu(u_act, approximate=False) * u_lin) @ p['ffn_down']
    return x, v_first


def setup_inputs(seed: int = 0) -> dict:
    key = jax.random.key(seed)
    ks = jax.random.split(key, 32)
    f32 = jnp.float32

    def nrm(k, shape, s):
        return jax.random.normal(k, shape, f32) * s

    def gain(k, shape):
        return 1.0 + 0.02 * jax.random.normal(k, shape, f32)

    conv_base = jnp.array([0.25, 0.5, 0.25], f32)[None, :, None]
    branch_scale = jnp.concatenate([
        jnp.full((RW_WIDTH, 1), RW_WIDTH ** -0.5, f32),
        jnp.full((HG_WIDTH, 1), HG_WIDTH ** -0.5, f32),
        jnp.full((XA_WIDTH, 1), XA_WIDTH ** -0.5, f32)], axis=0)
    return {
        'x_prompt': nrm(ks[0], (BATCH, SEQ, D_MODEL), 1.0),
        'x_sample': nrm(ks[1], (DEC_BATCH, DEC_SEQ, D_MODEL), 1.0),
        'mem_prompt': nrm(ks[2], (BATCH, N_MEM, D_MODEL), 1.0),
        'mem_sample': nrm(ks[3], (DEC_BATCH, N_MEM, D_MODEL), 1.0),
        'norm1_g': gain(ks[4], (DEPTH, D_MODEL)),
        'w_in': nrm(ks[5], (DEPTH, D_MODEL, IN_COLS), D_MODEL ** -0.5),
        'rw_conv': conv_base + nrm(ks[6], (DEPTH, CONV_WIDTH, RW_COLS), 0.05),
        'rw_w0': nrm(ks[7], (DEPTH, 2, RW_WIDTH), 0.5),
        'rw_w_up': nrm(ks[8], (DEPTH, 2, RW_DECAY_RANK, RW_WIDTH), RW_DECAY_RANK ** -0.5),
        'rw_a0': nrm(ks[9], (DEPTH, 2, RW_WIDTH), 0.3),
        'rw_a_up': nrm(ks[10], (DEPTH, 2, RW_AAA_RANK, RW_WIDTH), RW_AAA_RANK ** -0.5),
        'rw_g_up': nrm(ks[11], (DEPTH, RW_GATE_RANK, RW_WIDTH), RW_GATE_RANK ** -0.5),
        'rw_k_k': 0.85 + nrm(ks[12], (DEPTH, RW_WIDTH), 0.05),
        'rw_k_a': 1.0 + nrm(ks[13], (DEPTH, RW_WIDTH), 0.05),
        'rw_r_k': nrm(ks[14], (DEPTH, RW_HEADS, RW_HEAD_DIM), 0.1),
        'rw_lnx_w': gain(ks[15], (DEPTH, RW_WIDTH)),
        'rw_lnx_b': nrm(ks[16], (DEPTH, RW_WIDTH), 0.02),
        'rw_v0': nrm(ks[17], (DEPTH - 1, RW_WIDTH), 0.3),
        'rw_v_down': nrm(ks[18], (DEPTH - 1, D_MODEL, RW_MV_RANK), D_MODEL ** -0.5),
        'rw_v_up': nrm(ks[19], (DEPTH - 1, RW_MV_RANK, RW_WIDTH), RW_MV_RANK ** -0.5),
        'hg_lb_logits': nrm(ks[20], (DEPTH, HG_WIDTH), 0.5),
        'hg_norm_g': gain(ks[21], (DEPTH, HG_WIDTH)),
        'mem_norm_g': gain(ks[22], (DEPTH, D_MODEL)),
        'w_mem_kv': nrm(ks[23], (DEPTH, D_MODEL, 2 * XA_WIDTH), D_MODEL ** -0.5),
        'w_branch_out': jax.random.normal(ks[24], (DEPTH, MIX_WIDTH, D_MODEL), f32) * branch_scale,
        'w_o': nrm(ks[25], (DEPTH, D_MODEL, D_MODEL), D_MODEL ** -0.5),
        'norm2_g': gain(ks[26], (DEPTH, D_MODEL)),
        'ffn_up': nrm(ks[27], (DEPTH, D_MODEL, 2 * D_FF), D_MODEL ** -0.5),
        'ffn_conv': conv_base + nrm(ks[28], (DEPTH, CONV_WIDTH, 2 * D_FF), 0.1),
        'ffn_down': nrm(ks[29], (DEPTH, D_FF, D_MODEL), D_FF ** -0.5),
        'final_norm_g': gain(ks[30], (D_MODEL,)),
    }


def reference(x_prompt, x_sample, mem_prompt, mem_sample, norm1_g, w_in, rw_conv, rw_w0, rw_w_up,
              rw_a0, rw_a_up, rw_g_up, rw_k_k, rw_k_a, rw_r_k, rw_lnx_w, rw_lnx_b, rw_v0, rw_v_down,
              rw_v_up, hg_lb_logits, hg_norm_g, mem_norm_g, w_mem_kv, w_branch_out, w_o, norm2_g,
              ffn_up, ffn_conv, ffn_down, final_norm_g):
    p_lb = jax.nn.softmax(hg_lb_logits.astype(jnp.float32), axis=0)
    lower_bounds = jnp.cumsum(p_lb, axis=0) - p_lb[0]

    def trunk(x, mem):
        v_first = None
        for l in range(DEPTH):
            p = {
                'norm1_g': norm1_g[l], 'w_in': w_in[l], 'rw_conv': rw_conv[l],
                'rw_w0': rw_w0[l], 'rw_w_up': rw_w_up[l], 'rw_a0': rw_a0[l], 'rw_a_up': rw_a_up[l],
                'rw_g_up': rw_g_up[l], 'rw_k_k': rw_k_k[l], 'rw_k_a': rw_k_a[l], 'rw_r_k': rw_r_k[l],
                'rw_lnx_w': rw_lnx_w[l], 'rw_lnx_b': rw_lnx_b[l], 'hg_norm_g': hg_norm_g[l],
                'mem_norm_g': mem_norm_g[l], 'w_mem_kv': w_mem_kv[l], 'w_branch_out': w_branch_out[l],
                'w_o': w_o[l], 'norm2_g': norm2_g[l], 'ffn_up': ffn_up[l], 'ffn_conv': ffn_conv[l],
                'ffn_down': ffn_down[l],
            }
            vres = None if l == 0 else (rw_v0[l - 1], rw_v_down[l - 1], rw_v_up[l - 1])
            x, v_first = encoder_layer(x, mem, v_first, lower_bounds[l], p, vres)
        return rmsnorm(x, final_norm_g)

    y_prompt = trunk(x_prompt, mem_prompt)
    y_sample = trunk(x_sample, mem_sample)
    return (y_prompt, y_sample)
```

```python
import numpy as np
from contextlib import ExitStack
import concourse.bass as bass
import concourse.mybir as mybir
from concourse.bass_utils import run_bass_kernel_spmd

F32 = mybir.dt.float32
BF16 = mybir.dt.bfloat16
AF = mybir.ActivationFunctionType
ALU = mybir.AluOpType

D = 2048
L = 2
NMEM = 256
RWW = 768
RWC = 2944
HGC = 3840
XAC = 512
GC = 6144
INC = RWC + HGC + XAC + GC
DFF = 5504
TB = 512
CH = 64
DECAY_SCALE = 0.6065306597126334


class KB:
    def __init__(self, nc):
        self.nc = nc
        self.eng = {'pe': nc.tensor, 'act': nc.scalar, 'dve': nc.vector, 'pool': nc.gpsimd, 'sp': nc.sync}
        self.sem = {}
        self.cnt = {}
        self.inc = {}
        self.seen = {e: {} for e in self.eng}
        self.bufs = {}
        self._stack = []
        self.free = []
        for e in self.eng:
            self._mk(e, 1)

    def _mk(self, name, inc):
        if inc == 16 and self.free:
            self.sem[name], self.cnt[name] = self.free.pop()
        else:
            self.nsem = getattr(self, 'nsem', 0) + 1
            cm = self.nc.semaphore(f"sem{self.nsem}")
            self.sem[name] = cm.__enter__()
            self._stack.append(cm)
            self.cnt[name] = 0
        self.inc[name] = inc

    def close(self):
        for cm in reversed(self._stack):
            cm.__exit__(None, None, None)

    def _deps(self, e, reads, writes):
        waits = {}

        def need(w):
            if w is not None and w[0] != e:
                waits[w[0]] = max(waits.get(w[0], 0), w[1])
        for b in reads:
            st = self.bufs.get(b)
            if st:
                for e2, n in st['w'].items():
                    need((e2, n))
        for b in writes:
            st = self.bufs.get(b)
            if st:
                for e2, n in st['w'].items():
                    need((e2, n))
                for e2, n in st['r'].items():
                    need((e2, n))
        return waits

    def _wait(self, issuer, waits):
        for e2, n in waits.items():
            if n > self.seen[issuer].get(e2, 0):
                self.eng[issuer].wait_ge(self.sem[e2], n * self.inc[e2])
                self.seen[issuer][e2] = n

    def _mark(self, e, reads, writes):
        n = self.cnt[e]
        for b in reads:
            st = self.bufs.setdefault(b, {'w': {}, 'r': {}})
            st['r'][e] = n
        for b in writes:
            st = self.bufs.setdefault(b, {'w': {}, 'r': {}})
            st['w'][e] = n

    GLOBALS = ('c_', 'ps', 'ident_b', 'ones_b', 'vecs', 'rwsin', 'hgin', 'proj', 'rw_', 'hg_')
    ns = ''

    def _nm(self, names):
        if not self.ns:
            return list(names)
        return [b if b.startswith(self.GLOBALS) else self.ns + b for b in names]

    def op(self, e, fn, reads=(), writes=()):
        reads, writes = self._nm(reads), self._nm(writes)
        writes = list(writes) + [b for b in reads if b.startswith('ps') and b not in writes]
        self._wait(e, self._deps(e, reads, writes))
        ins = fn(self.eng[e])
        self.cnt[e] += 1
        ins.then_inc(self.sem[e], 1)
        self._mark(e, reads, writes)
        return ins

    def dma(self, stream, out, in_, reads=(), writes=(), q='sp', **kw):
        reads, writes = self._nm(reads), self._nm(writes)
        stream = self.ns + stream
        if stream not in self.sem:
            self._mk(stream, 16)
        self._wait(q, self._deps(stream, reads, writes))
        ins = self.eng[q].dma_start(out=out, in_=in_, **kw)
        self.cnt[stream] += 1
        ins.then_inc(self.sem[stream], 16)
        self._mark(stream, reads, writes)
        return ins

    def barrier(self):
        waits = {s: c for s, c in self.cnt.items() if c > 0}
        for e in self.eng:
            self._wait(e, {s: c for s, c in waits.items() if s != e})
        self.bufs = {}
        for s in [s for s in self.sem if self.inc[s] == 16]:
            self.free.append((self.sem.pop(s), self.cnt.pop(s)))
            self.inc.pop(s)
            for e in self.eng:
                self.seen[e].pop(s, None)


class VecPack:
    def __init__(self):
        self.cols = []
        self.idx = {}

    def add(self, name, arr):
        arr = np.asarray(arr, np.float32).reshape(-1)
        n = (arr.size + 127) // 128
        self.idx[name] = (len(self.cols), n)
        for c in range(n):
            col = np.zeros(128, np.float32)
            seg = arr[c * 128:(c + 1) * 128]
            col[:seg.size] = seg
            self.cols.append(col)

    def array(self):
        return np.stack(self.cols, axis=1).copy()


RW_GROUPS = ([(i * 128, 128) for i in range(18)]
             + [(2304, 96), (2400, 96), (2496, 96), (2592, 96), (2688, 128), (2816, 128)])


def vec_layout():
    vp = VecPack()
    z = np.zeros
    for l in range(L):
        vp.add(f'n1g{l}', z(D)); vp.add(f'n2g{l}', z(D)); vp.add(f'mng{l}', z(D))
        for gi, (r0, sz) in enumerate(RW_GROUPS):
            for j in range(3):
                vp.add(f'rwc{l}_{gi}_{j}', z(sz))
        for d in range(2):
            vp.add(f'w0{l}_{d}', z(RWW)); vp.add(f'a0{l}_{d}', z(RWW))
        for nm in ('kk', 'ka', 'rk', 'lnw', 'lnb', 'v0', 'lb0', 'lb1', 'hgn'):
            vp.add(f'{nm}{l}', z(RWW))
        for j in range(3):
            vp.add(f'fc{l}_{j}', z(2 * DFF))
    vp.add('fng', z(D))
    return vp


def pack_vecs(inp):
    vp = VecPack()
    for l in range(L):
        vp.add(f'n1g{l}', inp['norm1_g'][l]); vp.add(f'n2g{l}', inp['norm2_g'][l]); vp.add(f'mng{l}', inp['mem_norm_g'][l])
        for gi, (r0, sz) in enumerate(RW_GROUPS):
            for j in range(3):
                vp.add(f'rwc{l}_{gi}_{j}', inp['rw_conv'][l, j, r0:r0 + sz])
        for d in range(2):
            vp.add(f'w0{l}_{d}', inp['rw_w0'][l, d]); vp.add(f'a0{l}_{d}', inp['rw_a0'][l, d])
        vp.add(f'kk{l}', inp['rw_k_k'][l]); vp.add(f'ka{l}', inp['rw_k_a'][l]); vp.add(f'rk{l}', inp['rw_r_k'][l])
        vp.add(f'lnw{l}', inp['rw_lnx_w'][l]); vp.add(f'lnb{l}', inp['rw_lnx_b'][l])
        vp.add(f'v0{l}', inp['rw_v0'][l - 1] if l > 0 else np.zeros(RWW, np.float32))
        vp.add(f'lb0{l}', inp['hg_lb_logits'][0]); vp.add(f'lb1{l}', inp['hg_lb_logits'][l])
        vp.add(f'hgn{l}', inp['hg_norm_g'][l])
        for j in range(3):
            vp.add(f'fc{l}_{j}', inp['ffn_conv'][l, j])
    vp.add('fng', inp['final_norm_g'])
    return vp.array()


def const_mats():
    i = np.arange(128)
    h = i // 64
    t = i % 64
    same = (h[:, None] == h[None, :])
    c = {}
    c['ident'] = np.eye(128, dtype=np.float32)
    c['ones'] = np.ones((128, 128), np.float32)
    c['bones'] = same.astype(np.float32)
    c['mlt'] = (same & (t[:, None] < t[None, :])).astype(np.float32)
    c['mgt'] = (same & (t[:, None] > t[None, :])).astype(np.float32)
    t64 = np.arange(64)
    c['mle'] = (t[:, None] <= t64[None, :]).astype(np.float32)
    c['mge'] = (t[:, None] >= t64[None, :]).astype(np.float32)
    c['tle'] = (t64[:, None] <= t64[None, :]).astype(np.float32)
    c['tge'] = (t64[:, None] >= t64[None, :]).astype(np.float32)
    r = np.ones((128, TB), np.float32)
    r[:, ::CH] = 0.0
    c['rst'] = r
    return c


class RowSplit:
    def __init__(self, mk, name, R, C, step=3840):
        self.pieces = [(r0, mk(f'{name}_{r0}', [min(step, R - r0), C])) for r0 in range(0, R, step)]

    def __getitem__(self, key):
        rs, cs = key
        for base, ap in reversed(self.pieces):
            if rs.start >= base:
                assert rs.stop - base <= ap.shape[0]
                return ap[rs.start - base:rs.stop - base, cs]
        raise IndexError


BIGW = [('w_in', D, INC), ('w_mem_kv', D, 2 * XAC), ('w_branch_out', D, D), ('w_o', D, D),
        ('ffn_up', D, 2 * DFF), ('ffn_down', DFF, D)]


def build(TP, debug=(), stop_after=None):
    _uid = [0]

    def un(name):
        _uid[0] += 1
        return f's{_uid[0]}_{name}'
    NBLK = TP // TB
    nc = bass.Bass("TRN2", target_bir_lowering=False)
    VL = vec_layout()
    NV = len(VL.cols)
    dbg = set(debug)

    def din(name, shape, dt=F32):
        return nc.dram_tensor(name, list(shape), dt, kind="ExternalInput").ap()

    def dscr(name, shape, dt=F32):
        kind = "ExternalOutput" if name in dbg else "Internal"
        return nc.dram_tensor(name, list(shape), dt, kind=kind).ap()

    xT = din('xT', [D, TP])
    memT = din('memT', [D, NMEM])
    maskd = din('mask', [128, TP + 2])
    vecs_d = din('vecs', [128, NV])
    cm_d = {k: din('c_' + k, v.shape) for k, v in const_mats().items()}
    wd = {}
    for nm, r, c in BIGW:
        wd[nm] = din(nm, [L, r, c])
    w_up_d = din('rw_w_up', [L * 2, 96, RWW])
    a_up_d = din('rw_a_up', [L * 2, 96, RWW])
    g_up_d = din('rw_g_up', [L, 256, RWW])
    v_down_d = din('rw_v_down', [D, 64])
    v_up_d = din('rw_v_up', [64, RWW])
    yT = nc.dram_tensor('yT', [D, TP], F32, kind="ExternalOutput").ap()

    wb = {nm: dscr('b_' + nm, [L, r, c], BF16) for nm, r, c in BIGW}
    proj = RowSplit(dscr, 'proj', INC, TP + 2)

    kb = KB(nc)
    es = ExitStack()

    def sb(name, shape, dt=F32):
        return es.enter_context(nc.sbuf_tensor(un(name), list(shape), dt)).ap()

    PS = [es.enter_context(nc.psum_tensor(f'ps{i}', [128, 512], F32)).ap() for i in range(8)]

    vecs = sb('vecs', [128, NV])
    kb.dma('ldc', vecs, vecs_d, writes=['vecs'])
    cm = {}
    for k, d_ap in cm_d.items():
        cm[k] = sb('c_' + k, d_ap.shape)
        kb.dma('ldc', cm[k], d_ap, writes=['c_' + k])
    ones_b = sb('ones_b', [128, 128], BF16)
    ident_b = sb('ident_b', [128, 128], BF16)
    kb.op('dve', lambda e: e.tensor_copy(out=ones_b, in_=cm['ones']), reads=['c_ones'], writes=['ones_b'])
    kb.op('dve', lambda e: e.tensor_copy(out=ident_b, in_=cm['ident']), reads=['c_ident'], writes=['ident_b'])
    zt = sb('zt', [128, 2])
    kb.op('dve', lambda e: e.memset(zt, 0.0), writes=['zt'])
    kb.barrier()

    def V(name, c=0, n=1):
        c0, nn = VL.idx[name]
        return vecs[:, c0 + c:c0 + c + n]

    with ExitStack() as es0:
        CW = 4096
        stg = [es0.enter_context(nc.sbuf_tensor(un(f'p0f{i}'), [128, CW], F32)).ap() for i in range(3)]
        stb = [es0.enter_context(nc.sbuf_tensor(un(f'p0b{i}'), [128, CW], BF16)).ap() for i in range(3)]
        it = 0
        for nm, r, c in BIGW:
            tot = L * r * c
            per = tot // 128
            assert per * 128 == tot
            src = wd[nm].rearrange("l r c -> (l r c)").rearrange("(p f) -> p f", p=128)
            dst = wb[nm].rearrange("l r c -> (l r c)").rearrange("(p f) -> p f", p=128)
            o = 0
            while o < per:
                w = min(CW, per - o)
                s = it % 3
                kb.dma(f'p0l{s}', stg[s][:, :w], src[:, o:o + w], writes=[f'p0f{s}'])
                ce = ('dve', 'act', 'pool')[it % 3]
                if ce == 'act':
                    kb.op('act', lambda e: e.copy(out=stb[s][:, :w], in_=stg[s][:, :w]), reads=[f'p0f{s}'], writes=[f'p0b{s}'])
                else:
                    kb.op(ce, lambda e: e.tensor_copy(out=stb[s][:, :w], in_=stg[s][:, :w]), reads=[f'p0f{s}'], writes=[f'p0b{s}'])
                kb.dma(f'p0s{s}', dst[:, o:o + w], stb[s][:, :w], reads=[f'p0b{s}'], writes=['wb_' + nm], q='pool')
                o += w
                it += 1
        for r0 in range(0, RWC, 128):
            kb.dma('zp', proj[r0:r0 + 128, 0:1], zt[:, 0:1], reads=['zt'], writes=['proj'], q='pool', allow_slow_non_contiguous=True)
            kb.dma('zp', proj[r0:r0 + 128, TP + 1:TP + 2], zt[:, 0:1], reads=['zt'], writes=['proj'], q='pool', allow_slow_non_contiguous=True)
        kb.barrier()

    def rsqrt_(ap, name):
        kb.op('act', lambda e: e.activation(out=ap, in_=ap, func=AF.Sqrt), reads=[name], writes=[name])
        kb.op('dve', lambda e: e.reciprocal(out=ap, in_=ap), reads=[name], writes=[name])

    def rmsnorm_block(xall, xname, gname, hT, hname, ps, psname, sq, rstd, N=TB):
        for c in range(16):
            s = c % 2
            kb.op('act', lambda e: e.activation(out=sq[s][:, :N], in_=xall[:, c, :N], func=AF.Square),
                  reads=[xname], writes=[f'sq{s}'])
            kb.op('pe', lambda e: e.matmul(ps[:, :N], lhsT=ones_b, rhs=sq[s][:, :N], start=(c == 0), stop=(c == 15)),
                  reads=[f'sq{s}', 'ones_b'], writes=[psname])
        kb.op('dve', lambda e: e.tensor_scalar(out=rstd[:, :N], in0=ps[:, :N], scalar1=1.0 / D, scalar2=1e-6, op0=ALU.mult, op1=ALU.add),
              reads=[psname], writes=['rstd'])
        rsqrt_(rstd[:, :N], 'rstd')
        for c in range(16):
            kb.op('dve', lambda e: e.scalar_tensor_tensor(out=hT[:, c, :N], in0=xall[:, c, :N], scalar=V(gname, c), in1=rstd[:, :N],
                                                          op0=ALU.mult, op1=ALU.mult),
                  reads=[xname, 'rstd', 'vecs'], writes=[hname])

    lin_state = {'g': 0, 'ps': 0}

    def linear(Wb2d, wname, KC, M, rhs_fn, rhs_names, wtiles, evac, N=TB, G=512):
        Wv = Wb2d.rearrange("(kc p) m -> p kc m", p=128)
        for g0 in range(0, M, G):
            gw = min(G, M - g0)
            s = lin_state['g'] % len(wtiles)
            lin_state['g'] += 1
            wt = wtiles[s]
            kb.dma(f'lw{s}', wt[:, :KC, :gw], Wv[:, :, g0:g0 + gw], reads=[wname], writes=[f'wt{s}'])
            for j0 in range(0, gw, 128):
                msz = min(128, gw - j0)
                pi = 4 + lin_state['ps'] % 4
                lin_state['ps'] += 1
                ps = PS[pi]
                for kc in range(KC):
                    kb.op('pe', lambda e: e.matmul(ps[:msz, :N], lhsT=wt[:, kc, j0:j0 + msz], rhs=rhs_fn(kc),
                                                   start=(kc == 0), stop=(kc == KC - 1)),
                          reads=[f'wt{s}'] + rhs_names, writes=[f'ps{pi}'])
                evac((g0 + j0) // 128, msz, ps, f'ps{pi}')


    RWN = ['r', 'v', 'kk', 'lw0', 'lw1', 'ka0', 'ka1', 'kd0', 'kd1', 'bonus', 'g', 'y0', 'y1']
    rws = {n: dscr('rw_' + n, [RWW, TP]) for n in RWN}
    vfirst = dscr('vfirst', [RWW, TP])
    vdd = dscr('vd', [64, TP])
    HGN = ['q', 'lf0', 'lf1', 'kf0', 'kf1', 'o0', 'o1']
    hgs = {n: dscr('hg_' + n, [RWW, TP]) for n in HGN}
    xm = dscr('xm', [D, TP])
    xs = dscr('xs', [D, TP])
    ud = RowSplit(dscr, 'u', 2 * DFF, TP + 2)
    for r0 in range(0, 2 * DFF, 128):
        kb.dma('zp', ud[r0:r0 + 128, 0:1], zt[:, 0:1], reads=['zt'], writes=['u'], q='pool', allow_slow_non_contiguous=True)
        kb.dma('zp', ud[r0:r0 + 128, TP + 1:TP + 2], zt[:, 0:1], reads=['zt'], writes=['u'], q='pool', allow_slow_non_contiguous=True)

    def xview(ap):
        return ap.rearrange("(c p) t -> p c t", p=128)

    psr = {'n': 0}

    def pr(w=128):
        i = psr['n'] % 8
        psr['n'] += 1
        return PS[i][:, :w], f'ps{i}'

    cp_rr = {'n': 0}

    def cpy(out, in_, reads, writes, eng=None):
        if eng is None:
            eng = ('dve', 'act')[cp_rr['n'] % 2]
            cp_rr['n'] += 1
        if eng == 'act':
            kb.op('act', lambda e: e.copy(out=out, in_=in_), reads=reads, writes=writes)
        else:
            kb.op(eng, lambda e: e.tensor_copy(out=out, in_=in_), reads=reads, writes=writes)

    def tt(out, a, b, op, reads, writes, eng='dve'):
        kb.op(eng, lambda e: e.tensor_tensor(out=out, in0=a, in1=b, op=op), reads=reads, writes=writes)

    def conv3(out, oname, win, wname_, vname, N=TB, wts=None):
        rows = out.shape[0]
        if wts is not None:
            w0_, w1_, w2_ = wts
            kb.op('act', lambda e: e.activation(out=out, in_=win[:rows, 1:N + 1], func=AF.Copy, scale=w1_[:rows]),
                  reads=[wname_, 'vecs'], writes=[oname])
            kb.op('dve', lambda e: e.scalar_tensor_tensor(out=out, in0=win[:rows, 0:N], scalar=w0_[:rows], in1=out,
                                                          op0=ALU.mult, op1=ALU.add), reads=[wname_, oname, 'vecs'], writes=[oname])
            kb.op('dve', lambda e: e.scalar_tensor_tensor(out=out, in0=win[:rows, 2:N + 2], scalar=w2_[:rows], in1=out,
                                                          op0=ALU.mult, op1=ALU.add), reads=[wname_, oname, 'vecs'], writes=[oname])
            return
        kb.op('act', lambda e: e.activation(out=out, in_=win[:rows, 1:N + 1], func=AF.Copy, scale=V(vname + '_1')[:rows]),
              reads=[wname_, 'vecs'], writes=[oname])
        kb.op('dve', lambda e: e.scalar_tensor_tensor(out=out, in0=win[:rows, 0:N], scalar=V(vname + '_0')[:rows], in1=out,
                                                      op0=ALU.mult, op1=ALU.add), reads=[wname_, oname, 'vecs'], writes=[oname])
        kb.op('dve', lambda e: e.scalar_tensor_tensor(out=out, in0=win[:rows, 2:N + 2], scalar=V(vname + '_2')[:rows], in1=out,
                                                      op0=ALU.mult, op1=ALU.add), reads=[wname_, oname, 'vecs'], writes=[oname])

    def drive(gens):
        act = list(enumerate(gens))
        while act:
            for k, g in list(act):
                kb.ns = f'{k}:'
                try:
                    next(g)
                except StopIteration:
                    act.remove((k, g))
        kb.ns = ''

    xin = xT
    for l in range(L):
        last = (l == L - 1)
        xout = xs
        esl = ExitStack()

        def sbl(name, shape, dt=F32):
            return esl.enter_context(nc.sbuf_tensor(un(name), list(shape), dt)).ap()
        wup_b = sbl('wup_b', [96, 2, RWW], BF16)
        aup_b = sbl('aup_b', [96, 2, RWW], BF16)
        gup_b = sbl('gup_b', [128, 2, RWW], BF16)
        vup_b = sbl('vup_b', [64, RWW], BF16)
        vdn_b = sbl('vdn_b', [128, 16, 64], BF16)
        lbv = sbl('lbv', [128, 6])
        omlb = sbl('omlb', [128, 6])
        omka = sbl('omka', [128, 6])
        mkT_b = sbl('mkT_b', [128, 4, NMEM], BF16)
        mv_b = sbl('mv_b', [128, 2, XAC], BF16)
        with ExitStack() as e0:
            tmpf = e0.enter_context(nc.sbuf_tensor(un('tmpf'), [128, 2 * RWW], F32)).ap()
            memx = e0.enter_context(nc.sbuf_tensor(un('memx'), [128, 16, NMEM], F32)).ap()
            memh = e0.enter_context(nc.sbuf_tensor(un('memh'), [128, 16, NMEM], BF16)).ap()
            sqm = [e0.enter_context(nc.sbuf_tensor(un(f'sqm{i}'), [128, TB], BF16)).ap() for i in range(2)]
            rstdm = e0.enter_context(nc.sbuf_tensor(un('rstdm'), [128, TB], F32)).ap()
            wkv = e0.enter_context(nc.sbuf_tensor(un('wkv'), [128, 16, 2 * XAC], BF16)).ap()
            for d in range(2):
                kb.dma('lsm', tmpf[:96, :RWW], w_up_d[l * 2 + d], writes=['tmpf'])
                cpy(wup_b[:, d, :], tmpf[:96, :RWW], ['tmpf'], ['wup_b'], 'dve')
                kb.dma('lsm', tmpf[:96, :RWW], a_up_d[l * 2 + d], writes=['tmpf'])
                cpy(aup_b[:, d, :], tmpf[:96, :RWW], ['tmpf'], ['aup_b'], 'dve')
            kb.dma('lsm', tmpf.rearrange("p (a c) -> p a c", a=2), g_up_d[l].rearrange("(a p) c -> p a c", p=128), writes=['tmpf'])
            cpy(gup_b, tmpf.rearrange("p (a c) -> p a c", a=2), ['tmpf'], ['gup_b'], 'dve')
            kb.dma('lsm', tmpf[:64, :RWW], v_up_d, writes=['tmpf'])
            cpy(vup_b, tmpf[:64, :RWW], ['tmpf'], ['vup_b'], 'dve')
            kb.dma('lsm', tmpf[:, :1024].rearrange("p (c m) -> p c m", c=16), v_down_d.rearrange("(c p) m -> p c m", p=128), writes=['tmpf'])
            cpy(vdn_b, tmpf[:, :1024].rearrange("p (c m) -> p c m", c=16), ['tmpf'], ['vdn_b'], 'dve')
            if l == 0:
                kb.op('dve', lambda e: e.memset(lbv, 0.0), writes=['lbv'])
            else:
                tt(lbv, V(f'lb1{l}', 0, 6), V(f'lb0{l}', 0, 6), ALU.subtract, ['vecs'], ['lbv'])
                kb.op('act', lambda e: e.activation(out=lbv, in_=lbv, func=AF.Sigmoid), reads=['lbv'], writes=['lbv'])
            kb.op('dve', lambda e: e.tensor_scalar(out=omlb, in0=lbv, scalar1=-1.0, scalar2=1.0, op0=ALU.mult, op1=ALU.add),
                  reads=['lbv'], writes=['omlb'])
            kb.op('dve', lambda e: e.tensor_scalar(out=omka, in0=V(f'ka{l}', 0, 6), scalar1=-1.0, scalar2=1.0, op0=ALU.mult, op1=ALU.add),
                  reads=['vecs'], writes=['omka'])
            kb.dma('lmem', memx, xview(memT), writes=['memx'])
            rmsnorm_block(memx, 'memx', f'mng{l}', memh, 'memh', PS[0], 'ps0', sqm, rstdm, N=NMEM)
            kb.dma('lsm2', wkv, wb['w_mem_kv'][l].rearrange("(kc p) m -> p kc m", p=128), reads=['wb_w_mem_kv'], writes=['wkv'])
            for hd in range(4):
                ps, pn = pr()
                ps = PS[1][:, :NMEM]
                for kc in range(16):
                    kb.op('pe', lambda e: e.matmul(ps, lhsT=wkv[:, kc, hd * 128:(hd + 1) * 128], rhs=memh[:, kc, :],
                                                   start=(kc == 0), stop=(kc == 15)), reads=['wkv', 'memh'], writes=['ps1'])
                cpy(mkT_b[:, hd, :], ps, ['ps1'], ['mkT_b'])
            for mc in range(2):
                ps = PS[2]
                for kc in range(16):
                    kb.op('pe', lambda e: e.matmul(ps, lhsT=memh[:, kc, mc * 128:(mc + 1) * 128], rhs=wkv[:, kc, XAC:2 * XAC],
                                                   start=(kc == 0), stop=(kc == 15)), reads=['wkv', 'memh'], writes=['ps2'])
                cpy(mv_b[:, mc, :], ps, ['ps2'], ['mv_b'])
            kb.barrier()

        with ExitStack() as e1:
            def sb1(name, shape, dt=F32):
                return e1.enter_context(nc.sbuf_tensor(un(name), list(shape), dt)).ap()
            xall = [sb1(f'xall{i}', [128, 16, TB]) for i in range(2)]
            hT = sb1('hT', [128, 16, TB], BF16)
            sq = [sb1(f'sq{i}', [128, TB], BF16) for i in range(2)]
            rstd = sb1('rstd', [128, TB])
            wtiles = [sb1(f'wt{i}', [128, 16, 512], BF16) for i in range(2)]
            stage = [sb1(f'stage{i}', [128, TB]) for i in range(4)]
            ev = {'n': 0}
            for blk in range(NBLK):
                t0 = blk * TB
                xsl = blk % 2
                kb.dma(f'lx{xsl}', xall[xsl], xview(xin)[:, :, t0:t0 + TB], reads=['xin'], writes=[f'xall{xsl}'])
                rmsnorm_block(xall[xsl], f'xall{xsl}', f'n1g{l}', hT, 'hT', PS[0], 'ps0', sq, rstd)

                def evac(mi, msz, ps, psname):
                    s_ = ev['n'] % 4
                    ev['n'] += 1
                    if mi >= (RWC + HGC + XAC) // 128:
                        kb.op('act', lambda e: e.activation(out=stage[s_][:msz], in_=ps[:msz], func=AF.Sigmoid),
                              reads=[psname], writes=[f'stage{s_}'])
                    else:
                        cpy(stage[s_][:msz], ps[:msz], [psname], [f'stage{s_}'])
                    kb.dma(f'st{s_}', proj[mi * 128:mi * 128 + msz, 1 + t0:1 + t0 + TB], stage[s_][:msz],
                           reads=[f'stage{s_}'], writes=['proj'], q='pool')
                linear(wb['w_in'][l], 'wb_w_in', 16, INC, lambda kc: hT[:, kc, :], ['hT'], wtiles, evac)
                if l > 0:
                    ps = PS[1]
                    for kc in range(16):
                        kb.op('pe', lambda e: e.matmul(ps[:64], lhsT=vdn_b[:, kc, :], rhs=hT[:, kc, :], start=(kc == 0), stop=(kc == 15)),
                              reads=['vdn_b', 'hT'], writes=['ps1'])
                    s_ = ev['n'] % 4
                    ev['n'] += 1
                    cpy(stage[s_][:64], ps[:64], ['ps1'], [f'stage{s_}'])
                    kb.dma(f'st{s_}', vdd[:, t0:t0 + TB], stage[s_][:64], reads=[f'stage{s_}'], writes=['vd'], q='pool')
            kb.barrier()
        if stop_after == f'P1_{l}':
            esl.close()
            break

        with ExitStack() as e2:
            def sb2(name, shape, dt=F32):
                return e2.enter_context(nc.sbuf_tensor(un(name), list(shape), dt)).ap()
            W_ = TB + 2
            win = [sb2(f'win{i}', [128, W_]) for i in range(4)]
            wn = {'n': 0}

            def loadwin(row0, rows, t0):
                s_ = wn['n'] % 4
                wn['n'] += 1
                kb.dma(f'lwin{s_}', win[s_][:rows], proj[row0:row0 + rows, t0:t0 + W_], reads=['proj'], writes=[f'win{s_}'])
                return win[s_], f'win{s_}'
            cvt = sb2('cvt', [128, TB])
            twd = sb2('twd', [96, 2, TB], BF16)
            tad = sb2('tad', [96, 2, TB], BF16)
            sgd = sb2('sgd', [128, 2, TB], BF16)
            vdf = sb2('vdf', [64, TB])
            vdb_ = sb2('vdb', [64, TB], BF16)
            maskt = sb2('maskt', [128, TB])
            names = ['r', 'k', 'v', 'kk', 'lw0', 'lw1', 'a0', 'a1', 'ka0', 'ka1', 'kd0', 'kd1', 'bonus', 'g', 't1', 't2', 'vf', 'sg']
            T2 = [{n: sb2(f'{n}_{i}', [128, TB]) for n in names} for i in range(2)]
            hgt = [{n: sb2(f'h{n}_{i}', [128, TB]) for n in ['q', 'fz0', 'fz1', 'f0', 'f1', 'lf0', 'lf1', 'kf0', 'kf1']} for i in range(2)]
            for blk in range(NBLK):
                t0 = blk * TB
                kb.dma('lmask', maskt, maskd[:, 1 + t0:1 + t0 + TB], writes=['maskt'])
                for d in range(2):
                    w_, wn_ = loadwin(2304 + 96 * d, 96, t0)
                    conv3(cvt[:96], 'cvt', w_, wn_, f'rwc{l}_{18 + d}')
                    kb.op('act', lambda e: e.activation(out=twd[:, d, :], in_=cvt[:96], func=AF.Tanh), reads=['cvt'], writes=['twd'])
                    w_, wn_ = loadwin(2496 + 96 * d, 96, t0)
                    conv3(cvt[:96], 'cvt', w_, wn_, f'rwc{l}_{20 + d}')
                    cpy(tad[:, d, :], cvt[:96], ['cvt'], ['tad'], 'dve')
                for a in range(2):
                    w_, wn_ = loadwin(2688 + 128 * a, 128, t0)
                    conv3(cvt, 'cvt', w_, wn_, f'rwc{l}_{22 + a}')
                    kb.op('act', lambda e: e.activation(out=sgd[:, a, :], in_=cvt, func=AF.Sigmoid), reads=['cvt'], writes=['sgd'])
                if l > 0:
                    kb.dma('lvd', vdf, vdd[:, t0:t0 + TB], reads=['vd'], writes=['vdf'])
                    cpy(vdb_, vdf, ['vdf'], ['vdb'], 'dve')
                for c in range(6):
                    i_ = c % 2
                    Tt = T2[i_]

                    def nm(n):
                        return f'{n}_{i_}'
                    cs = slice(c * 128, (c + 1) * 128)
                    for gi, n in ((c, 'r'), (6 + c, 'k'), (12 + c, 'v')):
                        w_, wn_ = loadwin(gi * 128, 128, t0)
                        conv3(Tt[n], nm(n), w_, wn_, f'rwc{l}_{gi}')
                    tt(Tt['k'], Tt['k'], maskt, ALU.mult, [nm('k'), 'maskt'], [nm('k')])
                    tt(Tt['v'], Tt['v'], maskt, ALU.mult, [nm('v'), 'maskt'], [nm('v')])
                    if l == 0:
                        kb.dma(f'sv{i_}', vfirst[cs, t0:t0 + TB], Tt['v'], reads=[nm('v')], writes=['vfirst'], q='pool')
                    else:
                        ps, pn = PS[1], 'ps1'
                        kb.op('pe', lambda e: e.matmul(ps, lhsT=vup_b[:, cs], rhs=vdb_, start=True, stop=True), reads=['vup_b', 'vdb'], writes=[pn])
                        kb.op('act', lambda e: e.activation(out=Tt['sg'], in_=ps, func=AF.Sigmoid, bias=V(f'v0{l}', c)), reads=[pn, 'vecs'], writes=[nm('sg')])
                        kb.dma(f'lvf{i_}', Tt['vf'], vfirst[cs, t0:t0 + TB], reads=['vfirst'], writes=[nm('vf')])
                        tt(Tt['vf'], Tt['vf'], Tt['v'], ALU.subtract, [nm('vf'), nm('v')], [nm('vf')])
                        tt(Tt['vf'], Tt['vf'], Tt['sg'], ALU.mult, [nm('vf'), nm('sg')], [nm('vf')])
                        tt(Tt['v'], Tt['v'], Tt['vf'], ALU.add, [nm('vf'), nm('v')], [nm('v')])
                    for d in range(2):
                        ps, pn = PS[2], 'ps2'
                        kb.op('pe', lambda e: e.matmul(ps, lhsT=wup_b[:, d, cs], rhs=twd[:, d, :], start=True, stop=True), reads=['wup_b', 'twd'], writes=[pn])
                        kb.op('act', lambda e: e.activation(out=Tt[f'lw{d}'], in_=ps, func=AF.Sigmoid, bias=V(f'w0{l}_{d}', c)), reads=[pn, 'vecs'], writes=[nm(f'lw{d}')])
                        kb.op('dve', lambda e: e.tensor_scalar_mul(out=Tt[f'lw{d}'], in0=Tt[f'lw{d}'], scalar1=-DECAY_SCALE), reads=[nm(f'lw{d}')], writes=[nm(f'lw{d}')])
                        ps, pn = PS[3], 'ps3'
                        kb.op('pe', lambda e: e.matmul(ps, lhsT=aup_b[:, d, cs], rhs=tad[:, d, :], start=True, stop=True), reads=['aup_b', 'tad'], writes=[pn])
                        kb.op('act', lambda e: e.activation(out=Tt[f'a{d}'], in_=ps, func=AF.Sigmoid, bias=V(f'a0{l}_{d}', c)), reads=[pn, 'vecs'], writes=[nm(f'a{d}')])
                        kb.op('dve', lambda e: e.tensor_scalar(out=Tt['t1'], in0=Tt[f'a{d}'], scalar1=V(f'ka{l}', c), scalar2=omka[:, c:c + 1], op0=ALU.mult, op1=ALU.add),
                              reads=[nm(f'a{d}'), 'vecs', 'omka'], writes=[nm('t1')])
                        tt(Tt[f'kd{d}'], Tt['t1'], Tt['k'], ALU.mult, [nm('t1'), nm('k')], [nm(f'kd{d}')])
                    kb.op('dve', lambda e: e.tensor_scalar_mul(out=Tt['kk'], in0=Tt['k'], scalar1=V(f'kk{l}', c)), reads=[nm('k'), 'vecs'], writes=[nm('kk')])
                    tt(Tt['t1'], Tt['kk'], Tt['kk'], ALU.mult, [nm('kk')], [nm('t1')])
                    ps, pn = PS[1], 'ps1'
                    kb.op('pe', lambda e: e.matmul(ps, lhsT=cm['bones'], rhs=Tt['t1'], start=True, stop=True), reads=['c_bones', nm('t1')], writes=[pn])
                    kb.op('dve', lambda e: e.tensor_scalar_max(out=Tt['t2'], in0=ps, scalar1=1e-24), reads=[pn], writes=[nm('t2')])
                    rsqrt_(Tt['t2'], nm('t2'))
                    tt(Tt['kk'], Tt['kk'], Tt['t2'], ALU.mult, [nm('kk'), nm('t2')], [nm('kk')])
                    for d in range(2):
                        tt(Tt[f'ka{d}'], Tt['kk'], Tt[f'a{d}'], ALU.mult, [nm('kk'), nm(f'a{d}')], [nm(f'ka{d}')])
                    tt(Tt['t1'], Tt['kd0'], Tt['kd1'], ALU.add, [nm('kd0'), nm('kd1')], [nm('t1')])
                    tt(Tt['t1'], Tt['t1'], Tt['r'], ALU.mult, [nm('t1'), nm('r')], [nm('t1')])
                    kb.op('dve', lambda e: e.tensor_scalar_mul(out=Tt['t1'], in0=Tt['t1'], scalar1=V(f'rk{l}', c)), reads=[nm('t1'), 'vecs'], writes=[nm('t1')])
                    ps, pn = PS[2], 'ps2'
                    kb.op('pe', lambda e: e.matmul(ps, lhsT=cm['bones'], rhs=Tt['t1'], start=True, stop=True), reads=['c_bones', nm('t1')], writes=[pn])
                    tt(Tt['bonus'], ps, Tt['v'], ALU.mult, [pn, nm('v')], [nm('bonus')])
                    ps, pn = PS[3], 'ps3'
                    for a in range(2):
                        kb.op('pe', lambda e: e.matmul(ps, lhsT=gup_b[:, a, cs], rhs=sgd[:, a, :], start=(a == 0), stop=(a == 1)), reads=['gup_b', 'sgd'], writes=[pn])
                    cpy(Tt['g'], ps, [pn], [nm('g')], 'act')
                    for n in ('r', 'v', 'kk', 'lw0', 'lw1', 'ka0', 'ka1', 'kd0', 'kd1', 'bonus', 'g'):
                        kb.dma(f's2_{n}{i_}', rws[n][cs, t0:t0 + TB], Tt[n], reads=[nm(n)], writes=['rw_' + n], q='pool')
                    Ht = hgt[i_]

                    def hn(n):
                        return f'h{n}_{i_}'
                    hb = RWC + c * 128
                    kb.dma(f'lh0{i_}', Ht['q'], proj[hb:hb + 128, 1 + t0:1 + t0 + TB], reads=['proj'], writes=[hn('q')])
                    kb.op('act', lambda e: e.activation(out=Ht['q'], in_=Ht['q'], func=AF.Silu), reads=[hn('q')], writes=[hn('q')])
                    kb.dma(f's2_hq{i_}', hgs['q'][cs, t0:t0 + TB], Ht['q'], reads=[hn('q')], writes=['hg_q'], q='pool')
                    for d in range(2):
                        fb = RWC + RWW * (1 + d) + c * 128
                        kb.dma(f'lh{1 + d}{i_}', Ht[f'fz{d}'], proj[fb:fb + 128, 1 + t0:1 + t0 + TB], reads=['proj'], writes=[hn(f'fz{d}')])
                        kb.op('act', lambda e: e.activation(out=Ht[f'f{d}'], in_=Ht[f'fz{d}'], func=AF.Exp, scale=-1.0), reads=[hn(f'fz{d}')], writes=[hn(f'f{d}')])
                        kb.op('dve', lambda e: e.tensor_scalar_add(out=Ht[f'f{d}'], in0=Ht[f'f{d}'], scalar1=1.0), reads=[hn(f'f{d}')], writes=[hn(f'f{d}')])
                        kb.op('dve', lambda e: e.reciprocal(out=Ht[f'f{d}'], in_=Ht[f'f{d}']), reads=[hn(f'f{d}')], writes=[hn(f'f{d}')])
                        kb.op('dve', lambda e: e.tensor_scalar(out=Ht[f'f{d}'], in0=Ht[f'f{d}'], scalar1=omlb[:, c:c + 1], scalar2=lbv[:, c:c + 1], op0=ALU.mult, op1=ALU.add),
                              reads=[hn(f'f{d}'), 'omlb', 'lbv'], writes=[hn(f'f{d}')])
                        kb.op('act', lambda e: e.activation(out=Ht[f'lf{d}'], in_=Ht[f'f{d}'], func=AF.Ln), reads=[hn(f'f{d}')], writes=[hn(f'lf{d}')])
                        kb.op('dve', lambda e: e.tensor_scalar(out=Ht[f'kf{d}'], in0=Ht[f'f{d}'], scalar1=-1.0, scalar2=1.0, op0=ALU.mult, op1=ALU.add),
                              reads=[hn(f'f{d}')], writes=[hn(f'kf{d}')])
                        kb.dma(f's2_hlf{d}{i_}', hgs[f'lf{d}'][cs, t0:t0 + TB], Ht[f'lf{d}'], reads=[hn(f'lf{d}')], writes=[f'hg_lf{d}'], q='pool')
                        kb.dma(f's2_hkf{d}{i_}', hgs[f'kf{d}'][cs, t0:t0 + TB], Ht[f'kf{d}'], reads=[hn(f'kf{d}')], writes=[f'hg_kf{d}'], q='pool')
            kb.barrier()
        if stop_after == f'P2a_{l}':
            esl.close()
            break

        def v3(ap):
            return ap.rearrange("p (c t) -> p c t", t=CH)
        NCH = TB // CH
        with ExitStack() as e3:
            def sb3(name, shape, dt=F32):
                return e3.enter_context(nc.sbuf_tensor(un(name), list(shape), dt)).ap()
            def rw_factory(k):
                kb.ns = f'{k}:'
                LN_ = ['lw', 'ka', 'kd', 'kk', 'r', 'v']
                ld = [{n: sb3(f'l{n}_{i}', [128, TB]) for n in LN_} for i in range(2)]
                bcum = sb3('bcum', [128, TB]); bcur_t = sb3('bcur', [128, TB]); excl = sb3('excl', [128, TB])
                eex = sb3('eex', [128, TB]); enb = sb3('enb', [128, TB])
                ebs = [sb3(f'eb{i}', [128, TB]) for i in range(2)]
                EX = {n: [sb3(f'E{n}_{i}', [128, NCH, 2, CH], BF16) for i in range(2)] for n in 'abkrv'}
                rbar = [sb3(f'rbar{i}', [128, TB], BF16) for i in range(2)]
                ybuf = [sb3(f'ybuf{i}', [128, TB]) for i in range(2)]
                S = sb3('S', [128, 128]); Sb = sb3('Sb', [128, 128], BF16)
                Pm = [sb3(f'Pm{i}', [128, 128]) for i in range(2)]
                PTm = [sb3(f'PTm{i}', [128, 128]) for i in range(2)]
                TTm = [sb3(f'TTm{i}', [128, 128]) for i in range(2)]
                AakT = sb3('AakT', [128, 128], BF16); ArbT = sb3('ArbT', [128, CH], BF16); ArkT = sb3('ArkT', [128, CH], BF16)
                Btok = sb3('Btok', [128, 128], BF16); Ktok = sb3('Ktok', [128, 128], BF16); Vbd = sb3('Vbd', [128, 128], BF16)
                Xf = sb3('Xf', [128, 128]); Ub = sb3('Ub', [128, 128], BF16)
                for n in 'abkrv':
                    for i in range(2):
                        kb.op('pool', lambda e: e.memset(EX[n][i].rearrange("p c h t -> p (c h t)"), 0.0), writes=[f'E{n}_{i}'])
                kb.ns = ''
                bi = 0

                def chain(c, d):
                    nonlocal bi
                    cs = slice(c * 128, (c + 1) * 128)
                    kb.op('dve', lambda e: e.memset(S, 0.0), reads=[], writes=['S'])
                    kb.op('dve', lambda e: e.memset(Sb, 0.0), reads=[], writes=['Sb'])
                    mAT, mA, mInc = (cm['mlt'], cm['mgt'], cm['mle']) if d == 0 else (cm['mgt'], cm['mlt'], cm['mge'])
                    mATn, mAn, mIncn = ('c_mlt', 'c_mgt', 'c_mle') if d == 0 else ('c_mgt', 'c_mlt', 'c_mge')
                    blocks = list(range(NBLK)) if d == 0 else list(reversed(range(NBLK)))
                    for blk in blocks:
                        i_ = bi % 2
                        bi += 1
                        t0 = blk * TB
                        Lt = ld[i_]
                        for n, src in (('lw', rws[f'lw{d}']), ('ka', rws[f'ka{d}']), ('kd', rws[f'kd{d}']), ('kk', rws['kk']), ('r', rws['r']), ('v', rws['v'])):
                            kb.dma(f'l3{n}{i_}', Lt[n], src[cs, t0:t0 + TB], reads=['rwsin'], writes=[f'l{n}_{i_}'])

                        def ln(n):
                            return f'l{n}_{i_}'
                        kb.op('dve', lambda e: e.tensor_tensor_scan(out=bcum, data0=cm['rst'], data1=Lt['lw'], initial=0.0, op0=ALU.mult, op1=ALU.add),
                              reads=['c_rst', ln('lw')], writes=['bcum'])
                        if d == 0:
                            bcur, bcn = bcum, 'bcum'
                            tt(excl, bcum, Lt['lw'], ALU.subtract, ['bcum', ln('lw')], ['excl'])
                            yield
                        else:
                            tot = v3(bcum)[:, :, CH - 1:CH].broadcast_to([128, NCH, CH])
                            tt(v3(excl), tot, v3(bcum), ALU.subtract, ['bcum'], ['excl'])
                            yield
                            tt(bcur_t, excl, Lt['lw'], ALU.add, ['excl', ln('lw')], ['bcur'])
                            yield
                            bcur, bcn = bcur_t, 'bcur'
                        eb = ebs[i_]
                        kb.op('act', lambda e: e.activation(out=eex, in_=excl, func=AF.Exp), reads=['excl'], writes=['eex'])
                        kb.op('act', lambda e: e.activation(out=eb, in_=bcur, func=AF.Exp), reads=[bcn], writes=[f'eb{i_}'])
                        kb.op('act', lambda e: e.activation(out=enb, in_=bcur, func=AF.Exp, scale=-1.0), reads=[bcn], writes=['enb'])
                        for hf in range(2):
                            rs = slice(64 * hf, 64 * hf + 64)
                            kb.op('dve', lambda e: e.scalar_tensor_tensor(out=EX['a'][i_][rs, :, hf, :], in0=v3(Lt['kk'])[rs], scalar=-1.0, in1=v3(eex)[rs],
                                                                          op0=ALU.mult, op1=ALU.mult), reads=[ln('kk'), 'eex'], writes=[f'Ea_{i_}'])
                            tt(EX['r'][i_][rs, :, hf, :], v3(Lt['r'])[rs], v3(eb)[rs], ALU.mult, [ln('r'), f'eb{i_}'], [f'Er_{i_}'], 'pool')
                            yield
                            tt(EX['b'][i_][rs, :, hf, :], v3(Lt['ka'])[rs], v3(enb)[rs], ALU.mult, [ln('ka'), 'enb'], [f'Eb_{i_}'])
                            yield
                            tt(EX['k'][i_][rs, :, hf, :], v3(Lt['kd'])[rs], v3(enb)[rs], ALU.mult, [ln('kd'), 'enb'], [f'Ek_{i_}'], 'pool')
                            yield
                            kb.op('act', lambda e: e.copy(out=EX['v'][i_][rs, :, hf, :], in_=v3(Lt['v'])[rs]), reads=[ln('v')], writes=[f'Ev_{i_}'])
                        tt(rbar[i_], Lt['r'], eb, ALU.mult, [ln('r'), f'eb{i_}'], [f'rbar{i_}'], 'pool')
                        yield
                        chunks = list(range(NCH)) if d == 0 else list(reversed(range(NCH)))
                        for ch in chunks:
                            Ea, Eb, Ek, Er, Ev = (EX[n][i_][:, ch].rearrange("p h t -> p (h t)") for n in 'abkrv')
                            En = {n: f'E{n}_{i_}' for n in 'abkrv'}
                            rb = rbar[i_][:, ch * CH:(ch + 1) * CH]
                            ec = eb[:, ch * CH + CH - 1:ch * CH + CH] if d == 0 else eb[:, ch * CH:ch * CH + 1]
                            ps, pn = pr()
                            kb.op('pe', lambda e: e.matmul(ps, lhsT=Eb, rhs=Ea, start=True, stop=True), reads=[En['b'], En['a']], writes=[pn])
                            tt(PTm[0], ps, mAT, ALU.mult, [pn, mATn], ['PTm0'])
                            yield
                            ps, pn = pr()
                            kb.op('pe', lambda e: e.matmul(ps, lhsT=Ea, rhs=Eb, start=True, stop=True), reads=[En['b'], En['a']], writes=[pn])
                            tt(Pm[0], ps, mA, ALU.mult, [pn, mAn], ['Pm0'])
                            yield
                            tt(TTm[0], PTm[0], cm['ident'], ALU.add, ['PTm0', 'c_ident'], ['TTm0'], 'pool')
                            yield
                            g_ = 0
                            for j in range(5):
                                g2 = 1 - g_
                                ps, pn = pr()
                                kb.op('pe', lambda e: e.matmul(ps, lhsT=PTm[g_], rhs=Pm[g_], start=True, stop=True), reads=[f'PTm{g_}', f'Pm{g_}'], writes=[pn])
                                cpy(Pm[g2], ps, [pn], [f'Pm{g2}'])
                                yield
                                if j < 4:
                                    ps2, pn2 = pr()
                                    kb.op('pe', lambda e: e.matmul(ps2, lhsT=Pm[g_], rhs=PTm[g_], start=True, stop=True), reads=[f'PTm{g_}', f'Pm{g_}'], writes=[pn2])
                                    cpy(PTm[g2], ps2, [pn2], [f'PTm{g2}'])
                                    yield
                                ps3, pn3 = pr()
                                kb.op('pe', lambda e: e.matmul(ps3, lhsT=Pm[g2], rhs=TTm[g_], start=True, stop=True), reads=[f'Pm{g2}', f'TTm{g_}'], writes=[pn3])
                                tt(TTm[g2], ps3, TTm[g_], ALU.add, [pn3, f'TTm{g_}'], [f'TTm{g2}'])
                                yield
                                g_ = g2
                            TT, TTn = TTm[g_], f'TTm{g_}'
                            ps, pn = pr()
                            kb.op('pe', lambda e: e.matmul(ps, lhsT=Ek, rhs=Ea, start=True, stop=True), reads=[En['k'], En['a']], writes=[pn])
                            tt(AakT, ps, mAT, ALU.mult, [pn, mATn], ['AakT'])
                            yield
                            ps, pn = pr(CH)
                            kb.op('pe', lambda e: e.matmul(ps, lhsT=Eb, rhs=rb, start=True, stop=True), reads=[En['b'], f'rbar{i_}'], writes=[pn])
                            tt(ArbT, ps, mInc, ALU.mult, [pn, mIncn], ['ArbT'])
                            yield
                            ps, pn = pr(CH)
                            kb.op('pe', lambda e: e.matmul(ps, lhsT=Ek, rhs=rb, start=True, stop=True), reads=[En['k'], f'rbar{i_}'], writes=[pn])
                            tt(ArkT, ps, mInc, ALU.mult, [pn, mIncn], ['ArkT'])
                            yield
                            for src, sn, dst, dn in ((Eb, En['b'], Btok, 'Btok'), (Ek, En['k'], Ktok, 'Ktok'), (Ev, En['v'], Vbd, 'Vbd')):
                                ps, pn = pr()
                                kb.op('pe', lambda e: e.matmul(ps, lhsT=src, rhs=ident_b, start=True, stop=True), reads=[sn, 'ident_b'], writes=[pn])
                                cpy(dst, ps, [pn], [dn])
                                yield
                            ps, pn = pr()
                            kb.op('pe', lambda e: e.matmul(ps, lhsT=Ea, rhs=Sb, start=True, stop=False), reads=[En['a'], 'Sb'], writes=[pn])
                            kb.op('pe', lambda e: e.matmul(ps, lhsT=AakT, rhs=Vbd, start=False, stop=True), reads=['AakT', 'Vbd'], writes=[pn])
                            cpy(Xf, ps, [pn], ['Xf'])
                            yield
                            ps, pn = pr()
                            kb.op('pe', lambda e: e.matmul(ps, lhsT=TT, rhs=Xf, start=True, stop=True), reads=[TTn, 'Xf'], writes=[pn])
                            cpy(Ub, ps, [pn], ['Ub'])
                            yield
                            ps, pn = pr(CH)
                            kb.op('pe', lambda e: e.matmul(ps, lhsT=Sb, rhs=rb, start=True, stop=False), reads=['Sb', f'rbar{i_}'], writes=[pn])
                            kb.op('pe', lambda e: e.matmul(ps, lhsT=Ub, rhs=ArbT, start=False, stop=False), reads=['Ub', 'ArbT'], writes=[pn])
                            kb.op('pe', lambda e: e.matmul(ps, lhsT=Vbd, rhs=ArkT, start=False, stop=True), reads=['Vbd', 'ArkT'], writes=[pn])
                            cpy(ybuf[i_][:, ch * CH:(ch + 1) * CH], ps, [pn], [f'ybuf{i_}'])
                            yield
                            ps, pn = pr()
                            kb.op('pe', lambda e: e.matmul(ps, lhsT=Btok, rhs=Ub, start=True, stop=False), reads=['Btok', 'Ub'], writes=[pn])
                            kb.op('pe', lambda e: e.matmul(ps, lhsT=Ktok, rhs=Vbd, start=False, stop=True), reads=['Ktok', 'Vbd'], writes=[pn])
                            tt(S, S, ps, ALU.add, ['S', pn], ['S'])
                            yield
                            kb.op('dve', lambda e: e.tensor_scalar_mul(out=S, in0=S, scalar1=ec), reads=['S', f'eb{i_}'], writes=['S'])
                            cpy(Sb, S, ['S'], ['Sb'], 'act')
                            yield
                            yield
                        kb.dma(f's3y{i_}', rws[f'y{d}'][cs, t0:t0 + TB], ybuf[i_], reads=[f'ybuf{i_}'], writes=['rw_y'], q='pool')
                return chain
            rw_chains = [rw_factory(k) for k in range(2)]
            for c in range(6):
                drive([rw_chains[0](c, 0), rw_chains[1](c, 1)])
            kb.barrier()
        if stop_after == f'P2b_{l}':
            esl.close()
            break

        with ExitStack() as e4:
            def sb4(name, shape, dt=F32):
                return e4.enter_context(nc.sbuf_tensor(un(name), list(shape), dt)).ap()
            def hg_factory(k):
                ld = [{n: sb4(f'g{n}_{i}', [128, TB]) for n in ['q', 'lf', 'kf', 'iv']} for i in range(2)]
                bcum = sb4('hbcum', [128, TB]); bcur_t = sb4('hbcur', [128, TB]); bm = sb4('hbm', [128, TB]); bc = sb4('hbc', [128, TB])
                ex1 = sb4('hex1', [128, TB]); ex2 = sb4('hex2', [128, TB]); ex3 = sb4('hex3', [128, TB])
                ebs = [sb4(f'heb{i}', [128, TB]) for i in range(2)]
                qn = [sb4(f'qn{i}', [128, TB], BF16) for i in range(2)]
                kn = [sb4(f'kn{i}', [128, TB], BF16) for i in range(2)]
                qh = [sb4(f'qh{i}', [128, TB], BF16) for i in range(2)]
                kh = [sb4(f'kh{i}', [128, TB], BF16) for i in range(2)]
                ivb = [sb4(f'ivb{i}', [128, TB], BF16) for i in range(2)]
                obuf = [sb4(f'obuf{i}', [128, TB]) for i in range(2)]
                S = sb4('hS', [128, 128]); Sb = sb4('hSb', [128, 128], BF16)
                scb = sb4('scb', [CH, CH], BF16); Vtok = sb4('Vtok', [CH, 128], BF16); Ktk = sb4('Ktk', [CH, 128], BF16)
                bi = 0

                def chain(c, d):
                    nonlocal bi
                    cs = slice(c * 128, (c + 1) * 128)
                    ivrow = RWC + 3 * RWW + c * 128
                    kb.op('dve', lambda e: e.memset(S, 0.0), writes=['hS'])
                    kb.op('dve', lambda e: e.memset(Sb, 0.0), writes=['hSb'])
                    msk, mskn = (cm['tle'], 'c_tle') if d == 0 else (cm['tge'], 'c_tge')
                    blocks = list(range(NBLK)) if d == 0 else list(reversed(range(NBLK)))
                    for blk in blocks:
                        i_ = bi % 2
                        bi += 1
                        t0 = blk * TB
                        Lt = ld[i_]
                        kb.dma(f'l4q{i_}', Lt['q'], hgs['q'][cs, t0:t0 + TB], reads=['hgin'], writes=[f'gq_{i_}'])
                        kb.dma(f'l4lf{i_}', Lt['lf'], hgs[f'lf{d}'][cs, t0:t0 + TB], reads=['hgin'], writes=[f'glf_{i_}'])
                        kb.dma(f'l4kf{i_}', Lt['kf'], hgs[f'kf{d}'][cs, t0:t0 + TB], reads=['hgin'], writes=[f'gkf_{i_}'])
                        kb.dma(f'l4iv{i_}', Lt['iv'], proj[ivrow:ivrow + 128, 1 + t0:1 + t0 + TB], reads=['proj'], writes=[f'giv_{i_}'])

                        def gn(n):
                            return f'g{n}_{i_}'
                        kb.op('dve', lambda e: e.tensor_tensor_scan(out=bcum, data0=cm['rst'], data1=Lt['lf'], initial=0.0, op0=ALU.mult, op1=ALU.add),
                              reads=['c_rst', gn('lf')], writes=['hbcum'])
                        if d == 0:
                            bcur, bcn = bcum, 'hbcum'
                            cidx = CH - 1
                        else:
                            tot = v3(bcum)[:, :, CH - 1:CH].broadcast_to([128, NCH, CH])
                            tt(v3(bcur_t), tot, v3(bcum), ALU.subtract, ['hbcum'], ['hbcur'])
                            yield
                            tt(bcur_t, bcur_t, Lt['lf'], ALU.add, ['hbcur', gn('lf')], ['hbcur'])
                            yield
                            bcur, bcn = bcur_t, 'hbcur'
                            cidx = 0
                        ref = v3(bcur)[:, :, 31:32].broadcast_to([128, NCH, CH])
                        tt(v3(bm), v3(bcur), ref, ALU.subtract, [bcn], ['hbm'])
                        bC = v3(bcur)[:, :, cidx:cidx + 1].broadcast_to([128, NCH, CH])
                        tt(v3(bc), v3(bcur), bC, ALU.subtract, [bcn], ['hbc'], 'pool')
                        eb = ebs[i_]
                        kb.op('act', lambda e: e.activation(out=ex1, in_=bm, func=AF.Exp), reads=['hbm'], writes=['hex1'])
                        kb.op('act', lambda e: e.activation(out=ex2, in_=bm, func=AF.Exp, scale=-1.0), reads=['hbm'], writes=['hex2'])
                        kb.op('act', lambda e: e.activation(out=eb, in_=bcur, func=AF.Exp), reads=[bcn], writes=[f'heb{i_}'])
                        kb.op('act', lambda e: e.activation(out=ex3, in_=bc, func=AF.Exp, scale=-1.0), reads=['hbc'], writes=['hex3'])
                        tt(qn[i_], Lt['q'], ex1, ALU.mult, [gn('q'), 'hex1'], [f'qn{i_}'])
                        tt(kn[i_], Lt['kf'], ex2, ALU.mult, [gn('kf'), 'hex2'], [f'kn{i_}'], 'pool')
                        tt(qh[i_], Lt['q'], eb, ALU.mult, [gn('q'), f'heb{i_}'], [f'qh{i_}'])
                        tt(kh[i_], Lt['kf'], ex3, ALU.mult, [gn('kf'), 'hex3'], [f'kh{i_}'], 'pool')
                        cpy(ivb[i_], Lt['iv'], [gn('iv')], [f'ivb{i_}'], 'act')
                        yield
                        chunks = list(range(NCH)) if d == 0 else list(reversed(range(NCH)))
                        for ch in chunks:
                            sl = slice(ch * CH, (ch + 1) * CH)
                            ec = eb[:, ch * CH + cidx:ch * CH + cidx + 1]
                            ps, pn = pr(CH)
                            kb.op('pe', lambda e: e.matmul(ps[:CH], lhsT=kn[i_][:, sl], rhs=qn[i_][:, sl], start=True, stop=True), reads=[f'kn{i_}', f'qn{i_}'], writes=[pn])
                            tt(scb, ps[:CH], msk, ALU.mult, [pn, mskn], ['scb'])
                            yield
                            ps, pn = pr()
                            kb.op('pe', lambda e: e.matmul(ps[:CH], lhsT=ivb[i_][:, sl], rhs=ident_b, start=True, stop=True), reads=[f'ivb{i_}', 'ident_b'], writes=[pn])
                            cpy(Vtok, ps[:CH], [pn], ['Vtok'])
                            yield
                            ps, pn = pr()
                            kb.op('pe', lambda e: e.matmul(ps[:CH], lhsT=kh[i_][:, sl], rhs=ident_b, start=True, stop=True), reads=[f'kh{i_}', 'ident_b'], writes=[pn])
                            cpy(Ktk, ps[:CH], [pn], ['Ktk'])
                            yield
                            ps, pn = pr(CH)
                            kb.op('pe', lambda e: e.matmul(ps, lhsT=Sb, rhs=qh[i_][:, sl], start=True, stop=False), reads=['hSb', f'qh{i_}'], writes=[pn])
                            kb.op('pe', lambda e: e.matmul(ps, lhsT=Vtok, rhs=scb, start=False, stop=True), reads=['Vtok', 'scb'], writes=[pn])
                            cpy(obuf[i_][:, sl], ps, [pn], [f'obuf{i_}'])
                            yield
                            ps, pn = pr()
                            kb.op('pe', lambda e: e.matmul(ps, lhsT=Ktk, rhs=Vtok, start=True, stop=True), reads=['Ktk', 'Vtok'], writes=[pn])
                            kb.op('dve', lambda e: e.scalar_tensor_tensor(out=S, in0=S, scalar=ec, in1=ps, op0=ALU.mult, op1=ALU.add),
                                  reads=['hS', pn, f'heb{i_}'], writes=['hS'])
                            cpy(Sb, S, ['hS'], ['hSb'], 'act')
                            yield
                            yield
                        kb.dma(f's4o{i_}', hgs[f'o{d}'][cs, t0:t0 + TB], obuf[i_], reads=[f'obuf{i_}'], writes=['hg_o'], q='pool')
                return chain
            hg_chains = [hg_factory(k) for k in range(3)]
            units = [(c, d) for c in range(6) for d in range(2)]
            for u0 in range(0, 12, 3):
                drive([hg_chains[k](*units[u0 + k]) for k in range(3)])
            kb.barrier()
        if stop_after == f'P2c_{l}':
            esl.close()
            break

        with ExitStack() as e5:
            def sb5(name, shape, dt=F32):
                return e5.enter_context(nc.sbuf_tensor(un(name), list(shape), dt)).ap()
            xall_ = sb5('xall', [128, 16, TB])
            hT = sb5('hT', [128, 16, TB], BF16)
            sq = [sb5(f'sq{i}', [128, TB], BF16) for i in range(2)]
            rstd = sb5('rstd', [128, TB])
            wtiles = [sb5(f'wt{i}', [128, 16, 512], BF16) for i in range(2)]
            stage = [sb5(f'stage{i}', [128, TB]) for i in range(4)]
            yrw = sb5('yrw', [128, 6, TB], BF16)
            yhg = sb5('yhg', [128, 6, TB], BF16)
            yxa = sb5('yxa', [128, 4, TB], BF16)
            mixed = sb5('mixed', [128, 16, TB], BF16)
            maskt = sb5('maskt', [128, TB])
            WK = [{n: sb5(f'k{n}_{i}', [128, TB]) for n in ['a', 'b', 'c', 'd', 'y', 'yc', 't', 'rs']} for i in range(2)]
            pT = [sb5(f'pT{i}', [128, TB], BF16) for i in range(2)]
            qb = sb5('qb', [128, TB], BF16)
            ev = {'n': 0}
            wkn = {'n': 0}
            GB = RWC + HGC + XAC
            for blk in range(NBLK):
                t0 = blk * TB
                kb.dma('lmask', maskt, maskd[:, 1 + t0:1 + t0 + TB], writes=['maskt'])
                kb.dma('lx0', xall_, xview(xin)[:, :, t0:t0 + TB], reads=['xin'], writes=['xall'])
                for c in range(6):
                    cs = slice(c * 128, (c + 1) * 128)
                    i_ = wkn['n'] % 2
                    wkn['n'] += 1
                    W = WK[i_]

                    def kn_(n):
                        return f'k{n}_{i_}'
                    for n, src in (('a', rws['y0']), ('b', rws['y1']), ('c', rws['bonus']), ('d', rws['g'])):
                        kb.dma(f'l5{n}{i_}', W[n], src[cs, t0:t0 + TB], reads=['rwsin'], writes=[kn_(n)])
                    tt(W['y'], W['a'], W['b'], ALU.add, [kn_('a'), kn_('b')], [kn_('y')])
                    ps, pn = PS[1], 'ps1'
                    kb.op('pe', lambda e: e.matmul(ps, lhsT=cm['bones'], rhs=W['y'], start=True, stop=True), reads=['c_bones', kn_('y')], writes=[pn])
                    kb.op('dve', lambda e: e.scalar_tensor_tensor(out=W['yc'], in0=ps, scalar=-1.0 / 64, in1=W['y'], op0=ALU.mult, op1=ALU.add),
                          reads=[pn, kn_('y')], writes=[kn_('yc')])
                    tt(W['t'], W['yc'], W['yc'], ALU.mult, [kn_('yc')], [kn_('t')], 'pool')
                    ps, pn = PS[2], 'ps2'
                    kb.op('pe', lambda e: e.matmul(ps, lhsT=cm['bones'], rhs=W['t'], start=True, stop=True), reads=['c_bones', kn_('t')], writes=[pn])
                    kb.op('dve', lambda e: e.tensor_scalar(out=W['rs'], in0=ps, scalar1=1.0 / 64, scalar2=64e-5, op0=ALU.mult, op1=ALU.add),
                          reads=[pn], writes=[kn_('rs')])
                    rsqrt_(W['rs'], kn_('rs'))
                    tt(W['yc'], W['yc'], W['rs'], ALU.mult, [kn_('yc'), kn_('rs')], [kn_('yc')])
                    kb.op('dve', lambda e: e.tensor_scalar(out=W['yc'], in0=W['yc'], scalar1=V(f'lnw{l}', c), scalar2=V(f'lnb{l}', c), op0=ALU.mult, op1=ALU.add),
                          reads=[kn_('yc'), 'vecs'], writes=[kn_('yc')])
                    tt(W['yc'], W['yc'], W['c'], ALU.add, [kn_('yc'), kn_('c')], [kn_('yc')], 'pool')
                    tt(yrw[:, c, :], W['yc'], W['d'], ALU.mult, [kn_('yc'), kn_('d')], ['yrw'])
                for c in range(6):
                    cs = slice(c * 128, (c + 1) * 128)
                    i_ = wkn['n'] % 2
                    wkn['n'] += 1
                    W = WK[i_]
                    ogr = RWC + 4 * RWW + c * 128
                    kb.dma(f'l5a{i_}', W['a'], hgs['o0'][cs, t0:t0 + TB], reads=['hgin'], writes=[kn_('a')])
                    kb.dma(f'l5b{i_}', W['b'], hgs['o1'][cs, t0:t0 + TB], reads=['hgin'], writes=[kn_('b')])
                    kb.dma(f'l5c{i_}', W['c'], proj[ogr:ogr + 128, 1 + t0:1 + t0 + TB], reads=['proj'], writes=[kn_('c')])
                    tt(W['y'], W['a'], W['b'], ALU.add, [kn_('a'), kn_('b')], [kn_('y')])
                    tt(W['t'], W['y'], W['y'], ALU.mult, [kn_('y')], [kn_('t')], 'pool')
                    ps, pn = PS[1], 'ps1'
                    kb.op('pe', lambda e: e.matmul(ps, lhsT=cm['ones'], rhs=W['t'], start=True, stop=True), reads=['c_ones', kn_('t')], writes=[pn])
                    kb.op('dve', lambda e: e.tensor_scalar(out=W['rs'], in0=ps, scalar1=1.0 / 128, scalar2=1e-6, op0=ALU.mult, op1=ALU.add),
                          reads=[pn], writes=[kn_('rs')])
                    rsqrt_(W['rs'], kn_('rs'))
                    tt(W['y'], W['y'], W['rs'], ALU.mult, [kn_('y'), kn_('rs')], [kn_('y')])
                    kb.op('act', lambda e: e.activation(out=W['d'], in_=W['c'], func=AF.Silu), reads=[kn_('c')], writes=[kn_('d')])
                    kb.op('dve', lambda e: e.scalar_tensor_tensor(out=yhg[:, c, :], in0=W['y'], scalar=V(f'hgn{l}', c), in1=W['d'], op0=ALU.mult, op1=ALU.mult),
                          reads=[kn_('y'), kn_('d'), 'vecs'], writes=['yhg'])
                for hd in range(4):
                    i_ = wkn['n'] % 2
                    wkn['n'] += 1
                    W = WK[i_]
                    qr = RWC + HGC + hd * 128
                    kb.dma(f'l5a{i_}', W['a'], proj[qr:qr + 128, 1 + t0:1 + t0 + TB], reads=['proj'], writes=[kn_('a')])
                    cpy(qb, W['a'], [kn_('a')], ['qb'], 'dve')
                    for mc in range(2):
                        ps, pn = PS[1 + mc], f'ps{1 + mc}'
                        kb.op('pe', lambda e: e.matmul(ps, lhsT=mkT_b[:, hd, mc * 128:(mc + 1) * 128], rhs=qb, start=True, stop=True), reads=['mkT_b', 'qb'], writes=[pn])
                        kb.op('act', lambda e: e.activation(out=pT[mc], in_=ps, func=AF.Exp, scale=float(128 ** -0.5)), reads=[pn], writes=[f'pT{mc}'])
                    ps, pn = PS[3], 'ps3'
                    for mc in range(2):
                        kb.op('pe', lambda e: e.matmul(ps, lhsT=ones_b, rhs=pT[mc], start=(mc == 0), stop=(mc == 1)), reads=['ones_b', f'pT{mc}'], writes=[pn])
                    kb.op('dve', lambda e: e.reciprocal(out=W['rs'], in_=ps), reads=[pn], writes=[kn_('rs')])
                    ps, pn = PS[1], 'ps1'
                    for mc in range(2):
                        kb.op('pe', lambda e: e.matmul(ps, lhsT=mv_b[:, mc, hd * 128:(hd + 1) * 128], rhs=pT[mc], start=(mc == 0), stop=(mc == 1)), reads=['mv_b', f'pT{mc}'], writes=[pn])
                    tt(yxa[:, hd, :], ps, W['rs'], ALU.mult, [pn, kn_('rs')], ['yxa'])
                Wv = wb['w_branch_out'][l].rearrange("(kc p) m -> p kc m", p=128)
                for g0 in range(0, D, 512):
                    s_ = lin_state['g'] % 2
                    lin_state['g'] += 1
                    wt = wtiles[s_]
                    kb.dma(f'lw{s_}', wt, Wv[:, :, g0:g0 + 512], reads=['wb_wbo'], writes=[f'wt{s_}'])
                    for j0 in range(0, 512, 128):
                        oc = (g0 + j0) // 128
                        i_ = wkn['n'] % 2
                        wkn['n'] += 1
                        W = WK[i_]
                        pss = []
                        for br, (k0, k1, src, srcn) in enumerate(((0, 6, yrw, 'yrw'), (6, 12, yhg, 'yhg'), (12, 16, yxa, 'yxa'))):
                            pi = 4 + lin_state['ps'] % 4
                            lin_state['ps'] += 1
                            ps = PS[pi]
                            for kc in range(k0, k1):
                                kb.op('pe', lambda e: e.matmul(ps, lhsT=wt[:, kc, j0:j0 + 128], rhs=src[:, kc - k0, :], start=(kc == k0), stop=(kc == k1 - 1)),
                                      reads=[f'wt{s_}', srcn], writes=[f'ps{pi}'])
                            pss.append((ps, f'ps{pi}'))
                            gr = GB + br * D + oc * 128
                            kb.dma(f'l5{"abc"[br]}{i_}', W['abc'[br]], proj[gr:gr + 128, 1 + t0:1 + t0 + TB], reads=['proj'], writes=[kn_('abc'[br])])
                        tt(W['y'], pss[0][0], W['a'], ALU.mult, [pss[0][1], kn_('a')], [kn_('y')])
                        tt(W['t'], pss[1][0], W['b'], ALU.mult, [pss[1][1], kn_('b')], [kn_('t')])
                        tt(W['y'], W['y'], W['t'], ALU.add, [kn_('y'), kn_('t')], [kn_('y')], 'pool')
                        tt(W['t'], pss[2][0], W['c'], ALU.mult, [pss[2][1], kn_('c')], [kn_('t')])
                        tt(mixed[:, oc, :], W['y'], W['t'], ALU.add, [kn_('y'), kn_('t')], ['mixed'], 'pool')

                def evac_o(mi, msz, ps, psname):
                    i2 = wkn['n'] % 2
                    wkn['n'] += 1
                    tt(WK[i2]['t'], ps, xall_[:, mi, :], ALU.add, [psname, 'xall'], [f'kt_{i2}'])
                    tt(xall_[:, mi, :], WK[i2]['t'], maskt, ALU.mult, [f'kt_{i2}', 'maskt'], ['xall'], 'pool')
                linear(wb['w_o'][l], 'wb_w_o', 16, D, lambda kc: mixed[:, kc, :], ['mixed'], wtiles, evac_o)
                kb.dma('sxm', xview(xm)[:, :, t0:t0 + TB], xall_, reads=['xall'], writes=['xm'], q='pool')
                rmsnorm_block(xall_, 'xall', f'n2g{l}', hT, 'hT', PS[0], 'ps0', sq, rstd)

                def evac_u(mi, msz, ps, psname):
                    s2 = ev['n'] % 4
                    ev['n'] += 1
                    cpy(stage[s2], ps, [psname], [f'stage{s2}'])
                    kb.dma(f'st{s2}', ud[mi * 128:(mi + 1) * 128, 1 + t0:1 + t0 + TB], stage[s2], reads=[f'stage{s2}'], writes=['u'], q='pool')
                linear(wb['ffn_up'][l], 'wb_ffn_up', 16, 2 * DFF, lambda kc: hT[:, kc, :], ['hT'], wtiles, evac_u)
            kb.barrier()
        if stop_after == f'P3_{l}':
            esl.close()
            break

        with ExitStack() as e6:
            def sb6(name, shape, dt=F32):
                return e6.enter_context(nc.sbuf_tensor(un(name), list(shape), dt)).ap()
            xall_ = sb6('xall', [128, 16, TB])
            gT = sb6('gT', [128, 43, TB], BF16)
            wt2 = [sb6(f'wt{i}', [128, 43, 256], BF16) for i in range(2)]
            W_ = TB + 2
            win = [sb6(f'win{i}', [128, W_]) for i in range(4)]
            at = [sb6(f'at{i}', [128, TB]) for i in range(2)]
            lt_ = [sb6(f'lt{i}', [128, TB]) for i in range(2)]
            maskt = sb6('maskt', [128, TB])
            if last:
                hfin = sb6('hfin', [128, 16, TB])
                sq = [sb6(f'sq{i}', [128, TB], BF16) for i in range(2)]
                rstd = sb6('rstd', [128, TB])
            wn = {'n': 0}
            for blk in range(NBLK):
                t0 = blk * TB
                kb.dma('lx0', xall_, xview(xm)[:, :, t0:t0 + TB], reads=['xm'], writes=['xall'])
                kb.dma('lmask', maskt, maskd[:, 1 + t0:1 + t0 + TB], writes=['maskt'])
                for j in range(43):
                    i_ = j % 2
                    outs = []
                    for tile_, tn, chn in ((at[i_], f'at{i_}', j), (lt_[i_], f'lt{i_}', 43 + j)):
                        s_ = wn['n'] % 4
                        wn['n'] += 1
                        kb.dma(f'lwin{s_}', win[s_], ud[chn * 128:(chn + 1) * 128, t0:t0 + W_], reads=['u'], writes=[f'win{s_}'])
                        conv3(tile_, tn, win[s_], f'win{s_}', None, wts=[V(f'fc{l}_{k_}', chn) for k_ in range(3)])
                    kb.op('act', lambda e: e.activation(out=at[i_], in_=at[i_], func=AF.Gelu), reads=[f'at{i_}'], writes=[f'at{i_}'])
                    tt(gT[:, j, :], at[i_], lt_[i_], ALU.mult, [f'at{i_}', f'lt{i_}'], ['gT'], 'pool')

                def evac_d(mi, msz, ps, psname):
                    i2 = mi % 2
                    tt(at[i2], ps, maskt, ALU.mult, [psname, 'maskt'], [f'at{i2}'])
                    tt(xall_[:, mi, :], xall_[:, mi, :], at[i2], ALU.add, [f'at{i2}', 'xall'], ['xall'], 'pool')
                linear(wb['ffn_down'][l], 'wb_ffn_down', 43, D, lambda kc: gT[:, kc, :], ['gT'], wt2, evac_d, G=256)
                if not last:
                    kb.dma('sxs', xview(xs)[:, :, t0:t0 + TB], xall_, reads=['xall'], writes=['xs'], q='pool')
                else:
                    rmsnorm_block(xall_, 'xall', 'fng', hfin, 'hfin', PS[0], 'ps0', sq, rstd)
                    kb.dma('sy', xview(yT)[:, :, t0:t0 + TB], hfin, reads=['hfin'], writes=['yT'], q='pool')
            kb.barrier()
        esl.close()
        xin = xout

    kb.barrier()
    es.close()
    kb.close()
    return nc


def host_inputs(inp, TP, seqs):
    common = {'vecs': pack_vecs(inp)}
    for k, v in const_mats().items():
        common['c_' + k] = v
    for nm, r, c in BIGW:
        common[nm] = np.ascontiguousarray(inp[nm], np.float32)
    common['rw_w_up'] = np.ascontiguousarray(inp['rw_w_up'].reshape(L * 2, 96, RWW), np.float32)
    common['rw_a_up'] = np.ascontiguousarray(inp['rw_a_up'].reshape(L * 2, 96, RWW), np.float32)
    common['rw_g_up'] = np.ascontiguousarray(inp['rw_g_up'], np.float32)
    common['rw_v_down'] = np.ascontiguousarray(inp['rw_v_down'][0], np.float32)
    common['rw_v_up'] = np.ascontiguousarray(inp['rw_v_up'][0], np.float32)
    maps = []
    for x, mem in seqs:
        T = x.shape[0]
        xt = np.zeros((D, TP), np.float32)
        xt[:, :T] = x.T
        mask = np.zeros((128, TP + 2), np.float32)
        mask[:, 1:1 + T] = 1.0
        m = dict(common)
        m['xT'] = xt
        m['memT'] = np.ascontiguousarray(mem.T, np.float32)
        m['mask'] = mask
        maps.append(m)
    return maps


def run(inp, debug=(), stop_after=None, ncores=8):
    xp, xs = np.asarray(inp['x_prompt']), np.asarray(inp['x_sample'])
    mp, ms = np.asarray(inp['mem_prompt']), np.asarray(inp['mem_sample'])
    seqs = [(xp[b], mp[b]) for b in range(xp.shape[0])] + [(xs[b], ms[b]) for b in range(xs.shape[0])]
    Tmax = max(s[0].shape[0] for s in seqs)
    TP = ((Tmax + TB - 1) // TB) * TB
    while len(seqs) < ncores:
        seqs.append((np.zeros((TB, D), np.float32), np.zeros((NMEM, D), np.float32)))
    inp = {k: np.asarray(v) for k, v in inp.items()}
    maps = host_inputs(inp, TP, seqs)
    nc = build(TP, debug=debug, stop_after=stop_after)
    res = run_bass_kernel_spmd(nc, maps, core_ids=list(range(ncores)))
    return res, seqs, TP


def kernel(**inputs):
    res, seqs, TP = run(inputs)
    xp, xs = inputs['x_prompt'], inputs['x_sample']
    outs = []
    i = 0
    for arr in (xp, xs):
        B, T, _ = arr.shape
        o = np.zeros((B, T, D), np.float32)
        for b in range(B):
            o[b] = res.results[i]['yT'][:, :T].T
            i += 1
        outs.append(o)
    return tuple(outs)
```

```python
import numpy as np
from contextlib import ExitStack
import concourse.bass as bass
import concourse.mybir as mybir
from concourse.bass_utils import run_bass_kernel_spmd

F32 = mybir.dt.float32
BF16 = mybir.dt.bfloat16
AF = mybir.ActivationFunctionType
ALU = mybir.AluOpType

D = 2048
L = 2
NMEM = 256
RWW = 768
RWC = 2944
HGC = 3840
XAC = 512
GC = 6144
INC = RWC + HGC + XAC + GC
DFF = 5504
TB = 512
CH = 64
DECAY_SCALE = 0.6065306597126334


class KB:
    def __init__(self, nc):
        self.nc = nc
        self.eng = {'pe': nc.tensor, 'act': nc.scalar, 'dve': nc.vector, 'pool': nc.gpsimd, 'sp': nc.sync}
        self.sem = {}
        self.cnt = {}
        self.inc = {}
        self.seen = {e: {} for e in self.eng}
        self.bufs = {}
        self._stack = []
        self.free = []
        for e in self.eng:
            self._mk(e, 1)

    def _mk(self, name, inc):
        if inc == 16 and self.free:
            self.sem[name], self.cnt[name] = self.free.pop()
        else:
            self.nsem = getattr(self, 'nsem', 0) + 1
            cm = self.nc.semaphore(f"sem{self.nsem}")
            self.sem[name] = cm.__enter__()
            self._stack.append(cm)
            self.cnt[name] = 0
        self.inc[name] = inc

    def close(self):
        for cm in reversed(self._stack):
            cm.__exit__(None, None, None)

    def _deps(self, e, reads, writes):
        waits = {}

        def need(w):
            if w is not None and w[0] != e:
                waits[w[0]] = max(waits.get(w[0], 0), w[1])
        for b in reads:
            st = self.bufs.get(b)
            if st:
                for e2, n in st['w'].items():
                    need((e2, n))
        for b in writes:
            st = self.bufs.get(b)
            if st:
                for e2, n in st['w'].items():
                    need((e2, n))
                for e2, n in st['r'].items():
                    need((e2, n))
        return waits

    def _wait(self, issuer, waits):
        for e2, n in waits.items():
            if n > self.seen[issuer].get(e2, 0):
                self.eng[issuer].wait_ge(self.sem[e2], n * self.inc[e2])
                self.seen[issuer][e2] = n

    def _mark(self, e, reads, writes):
        n = self.cnt[e]
        for b in reads:
            st = self.bufs.setdefault(b, {'w': {}, 'r': {}})
            st['r'][e] = n
        for b in writes:
            st = self.bufs.setdefault(b, {'w': {}, 'r': {}})
            st['w'][e] = n

    GLOBALS = ('c_', 'ps', 'ident_b', 'ones_b', 'vecs', 'rwsin', 'hgin', 'proj', 'rw_', 'hg_')
    ns = ''

    def _nm(self, names):
        if not self.ns:
            return list(names)
        return [b if b.startswith(self.GLOBALS) else self.ns + b for b in names]

    def op(self, e, fn, reads=(), writes=()):
        reads, writes = self._nm(reads), self._nm(writes)
        writes = list(writes) + [b for b in reads if b.startswith('ps') and b not in writes]
        self._wait(e, self._deps(e, reads, writes))
        ins = fn(self.eng[e])
        self.cnt[e] += 1
        ins.then_inc(self.sem[e], 1)
        self._mark(e, reads, writes)
        return ins

    def dma(self, stream, out, in_, reads=(), writes=(), q='sp', **kw):
        reads, writes = self._nm(reads), self._nm(writes)
        stream = self.ns + stream
        if stream not in self.sem:
            self._mk(stream, 16)
        self._wait(q, self._deps(stream, reads, writes))
        ins = self.eng[q].dma_start(out=out, in_=in_, **kw)
        self.cnt[stream] += 1
        ins.then_inc(self.sem[stream], 16)
        self._mark(stream, reads, writes)
        return ins

    def barrier(self):
        waits = {s: c for s, c in self.cnt.items() if c > 0}
        for e in self.eng:
            self._wait(e, {s: c for s, c in waits.items() if s != e})
        self.bufs = {}
        for s in [s for s in self.sem if self.inc[s] == 16]:
            self.free.append((self.sem.pop(s), self.cnt.pop(s)))
            self.inc.pop(s)
            for e in self.eng:
                self.seen[e].pop(s, None)


class VecPack:
    def __init__(self):
        self.cols = []
        self.idx = {}

    def add(self, name, arr):
        arr = np.asarray(arr, np.float32).reshape(-1)
        n = (arr.size + 127) // 128
        self.idx[name] = (len(self.cols), n)
        for c in range(n):
            col = np.zeros(128, np.float32)
            seg = arr[c * 128:(c + 1) * 128]
            col[:seg.size] = seg
            self.cols.append(col)

    def array(self):
        return np.stack(self.cols, axis=1).copy()


RW_GROUPS = ([(i * 128, 128) for i in range(18)]
             + [(2304, 96), (2400, 96), (2496, 96), (2592, 96), (2688, 128), (2816, 128)])


def vec_layout():
    vp = VecPack()
    z = np.zeros
    for l in range(L):
        vp.add(f'n1g{l}', z(D)); vp.add(f'n2g{l}', z(D)); vp.add(f'mng{l}', z(D))
        for gi, (r0, sz) in enumerate(RW_GROUPS):
            for j in range(3):
                vp.add(f'rwc{l}_{gi}_{j}', z(sz))
        for d in range(2):
            vp.add(f'w0{l}_{d}', z(RWW)); vp.add(f'a0{l}_{d}', z(RWW))
        for nm in ('kk', 'ka', 'rk', 'lnw', 'lnb', 'v0', 'lb0', 'lb1', 'hgn'):
            vp.add(f'{nm}{l}', z(RWW))
        for j in range(3):
            vp.add(f'fc{l}_{j}', z(2 * DFF))
    vp.add('fng', z(D))
    return vp


def pack_vecs(inp):
    vp = VecPack()
    for l in range(L):
        vp.add(f'n1g{l}', inp['norm1_g'][l]); vp.add(f'n2g{l}', inp['norm2_g'][l]); vp.add(f'mng{l}', inp['mem_norm_g'][l])
        for gi, (r0, sz) in enumerate(RW_GROUPS):
            for j in range(3):
                vp.add(f'rwc{l}_{gi}_{j}', inp['rw_conv'][l, j, r0:r0 + sz])
        for d in range(2):
            vp.add(f'w0{l}_{d}', inp['rw_w0'][l, d]); vp.add(f'a0{l}_{d}', inp['rw_a0'][l, d])
        vp.add(f'kk{l}', inp['rw_k_k'][l]); vp.add(f'ka{l}', inp['rw_k_a'][l]); vp.add(f'rk{l}', inp['rw_r_k'][l])
        vp.add(f'lnw{l}', inp['rw_lnx_w'][l]); vp.add(f'lnb{l}', inp['rw_lnx_b'][l])
        vp.add(f'v0{l}', inp['rw_v0'][l - 1] if l > 0 else np.zeros(RWW, np.float32))
        vp.add(f'lb0{l}', inp['hg_lb_logits'][0]); vp.add(f'lb1{l}', inp['hg_lb_logits'][l])
        vp.add(f'hgn{l}', inp['hg_norm_g'][l])
        for j in range(3):
            vp.add(f'fc{l}_{j}', inp['ffn_conv'][l, j])
    vp.add('fng', inp['final_norm_g'])
    return vp.array()


def const_mats():
    i = np.arange(128)
    h = i // 64
    t = i % 64
    same = (h[:, None] == h[None, :])
    c = {}
    c['ident'] = np.eye(128, dtype=np.float32)
    c['ones'] = np.ones((128, 128), np.float32)
    c['bones'] = same.astype(np.float32)
    c['mlt'] = (same & (t[:, None] < t[None, :])).astype(np.float32)
    c['mgt'] = (same & (t[:, None] > t[None, :])).astype(np.float32)
    t64 = np.arange(64)
    c['mle'] = (t[:, None] <= t64[None, :]).astype(np.float32)
    c['mge'] = (t[:, None] >= t64[None, :]).astype(np.float32)
    c['tle'] = (t64[:, None] <= t64[None, :]).astype(np.float32)
    c['tge'] = (t64[:, None] >= t64[None, :]).astype(np.float32)
    r = np.ones((128, TB), np.float32)
    r[:, ::CH] = 0.0
    c['rst'] = r
    return c


class RowSplit:
    def __init__(self, mk, name, R, C, step=3840):
        self.pieces = [(r0, mk(f'{name}_{r0}', [min(step, R - r0), C])) for r0 in range(0, R, step)]

    def __getitem__(self, key):
        rs, cs = key
        for base, ap in reversed(self.pieces):
            if rs.start >= base:
                assert rs.stop - base <= ap.shape[0]
                return ap[rs.start - base:rs.stop - base, cs]
        raise IndexError


BIGW = [('w_in', D, INC), ('w_mem_kv', D, 2 * XAC), ('w_branch_out', D, D), ('w_o', D, D),
        ('ffn_up', D, 2 * DFF), ('ffn_down', DFF, D)]


def build(TP, debug=(), stop_after=None):
    _uid = [0]

    def un(name):
        _uid[0] += 1
        return f's{_uid[0]}_{name}'
    NBLK = TP // TB
    nc = bass.Bass("TRN2", target_bir_lowering=False)
    VL = vec_layout()
    NV = len(VL.cols)
    dbg = set(debug)

    def din(name, shape, dt=F32):
        return nc.dram_tensor(name, list(shape), dt, kind="ExternalInput").ap()

    def dscr(name, shape, dt=F32):
        kind = "ExternalOutput" if name in dbg else "Internal"
        return nc.dram_tensor(name, list(shape), dt, kind=kind).ap()

    xT = din('xT', [D, TP])
    memT = din('memT', [D, NMEM])
    maskd = din('mask', [128, TP + 2])
    vecs_d = din('vecs', [128, NV])
    cm_d = {k: din('c_' + k, v.shape) for k, v in const_mats().items()}
    wd = {}
    for nm, r, c in BIGW:
        wd[nm] = din(nm, [L, r, c])
    w_up_d = din('rw_w_up', [L * 2, 96, RWW])
    a_up_d = din('rw_a_up', [L * 2, 96, RWW])
    g_up_d = din('rw_g_up', [L, 256, RWW])
    v_down_d = din('rw_v_down', [D, 64])
    v_up_d = din('rw_v_up', [64, RWW])
    yT = nc.dram_tensor('yT', [D, TP], F32, kind="ExternalOutput").ap()

    wb = {nm: dscr('b_' + nm, [L, r, c], BF16) for nm, r, c in BIGW}
    proj = RowSplit(dscr, 'proj', INC, TP + 2)

    kb = KB(nc)
    es = ExitStack()

    def sb(name, shape, dt=F32):
        return es.enter_context(nc.sbuf_tensor(un(name), list(shape), dt)).ap()

    PS = [es.enter_context(nc.psum_tensor(f'ps{i}', [128, 512], F32)).ap() for i in range(8)]

    vecs = sb('vecs', [128, NV])
    kb.dma('ldc', vecs, vecs_d, writes=['vecs'])
    cm = {}
    for k, d_ap in cm_d.items():
        cm[k] = sb('c_' + k, d_ap.shape)
        kb.dma('ldc', cm[k], d_ap, writes=['c_' + k])
    ones_b = sb('ones_b', [128, 128], BF16)
    ident_b = sb('ident_b', [128, 128], BF16)
    kb.op('dve', lambda e: e.tensor_copy(out=ones_b, in_=cm['ones']), reads=['c_ones'], writes=['ones_b'])
    kb.op('dve', lambda e: e.tensor_copy(out=ident_b, in_=cm['ident']), reads=['c_ident'], writes=['ident_b'])
    zt = sb('zt', [128, 2])
    kb.op('dve', lambda e: e.memset(zt, 0.0), writes=['zt'])
    kb.barrier()

    def V(name, c=0, n=1):
        c0, nn = VL.idx[name]
        return vecs[:, c0 + c:c0 + c + n]

    with ExitStack() as es0:
        CW = 4096
        stg = [es0.enter_context(nc.sbuf_tensor(un(f'p0f{i}'), [128, CW], F32)).ap() for i in range(3)]
        stb = [es0.enter_context(nc.sbuf_tensor(un(f'p0b{i}'), [128, CW], BF16)).ap() for i in range(3)]
        it = 0
        for nm, r, c in BIGW:
            tot = L * r * c
            per = tot // 128
            assert per * 128 == tot
            src = wd[nm].rearrange("l r c -> (l r c)").rearrange("(p f) -> p f", p=128)
            dst = wb[nm].rearrange("l r c -> (l r c)").rearrange("(p f) -> p f", p=128)
            o = 0
            while o < per:
                w = min(CW, per - o)
                s = it % 3
                kb.dma(f'p0l{s}', stg[s][:, :w], src[:, o:o + w], writes=[f'p0f{s}'])
                ce = ('dve', 'act', 'pool')[it % 3]
                if ce == 'act':
                    kb.op('act', lambda e: e.copy(out=stb[s][:, :w], in_=stg[s][:, :w]), reads=[f'p0f{s}'], writes=[f'p0b{s}'])
                else:
                    kb.op(ce, lambda e: e.tensor_copy(out=stb[s][:, :w], in_=stg[s][:, :w]), reads=[f'p0f{s}'], writes=[f'p0b{s}'])
                kb.dma(f'p0s{s}', dst[:, o:o + w], stb[s][:, :w], reads=[f'p0b{s}'], writes=['wb_' + nm], q='pool')
                o += w
                it += 1
        for r0 in range(0, RWC, 128):
            kb.dma('zp', proj[r0:r0 + 128, 0:1], zt[:, 0:1], reads=['zt'], writes=['proj'], q='pool', allow_slow_non_contiguous=True)
            kb.dma('zp', proj[r0:r0 + 128, TP + 1:TP + 2], zt[:, 0:1], reads=['zt'], writes=['proj'], q='pool', allow_slow_non_contiguous=True)
        kb.barrier()

    def rsqrt_(ap, name):
        kb.op('act', lambda e: e.activation(out=ap, in_=ap, func=AF.Sqrt), reads=[name], writes=[name])
        kb.op('dve', lambda e: e.reciprocal(out=ap, in_=ap), reads=[name], writes=[name])

    def rmsnorm_block(xall, xname, gname, hT, hname, ps, psname, sq, rstd, N=TB):
        for c in range(16):
            s = c % 2
            kb.op('act', lambda e: e.activation(out=sq[s][:, :N], in_=xall[:, c, :N], func=AF.Square),
                  reads=[xname], writes=[f'sq{s}'])
            kb.op('pe', lambda e: e.matmul(ps[:, :N], lhsT=ones_b, rhs=sq[s][:, :N], start=(c == 0), stop=(c == 15)),
                  reads=[f'sq{s}', 'ones_b'], writes=[psname])
        kb.op('dve', lambda e: e.tensor_scalar(out=rstd[:, :N], in0=ps[:, :N], scalar1=1.0 / D, scalar2=1e-6, op0=ALU.mult, op1=ALU.add),
              reads=[psname], writes=['rstd'])
        rsqrt_(rstd[:, :N], 'rstd')
        for c in range(16):
            kb.op('dve', lambda e: e.scalar_tensor_tensor(out=hT[:, c, :N], in0=xall[:, c, :N], scalar=V(gname, c), in1=rstd[:, :N],
                                                          op0=ALU.mult, op1=ALU.mult),
                  reads=[xname, 'rstd', 'vecs'], writes=[hname])

    lin_state = {'g': 0, 'ps': 0}

    def linear(Wb2d, wname, KC, M, rhs_fn, rhs_names, wtiles, evac, N=TB, G=512):
        Wv = Wb2d.rearrange("(kc p) m -> p kc m", p=128)
        for g0 in range(0, M, G):
            gw = min(G, M - g0)
            s = lin_state['g'] % len(wtiles)
            lin_state['g'] += 1
            wt = wtiles[s]
            kb.dma(f'lw{s}', wt[:, :KC, :gw], Wv[:, :, g0:g0 + gw], reads=[wname], writes=[f'wt{s}'])
            for j0 in range(0, gw, 128):
                msz = min(128, gw - j0)
                pi = 4 + lin_state['ps'] % 4
                lin_state['ps'] += 1
                ps = PS[pi]
                for kc in range(KC):
                    kb.op('pe', lambda e: e.matmul(ps[:msz, :N], lhsT=wt[:, kc, j0:j0 + msz], rhs=rhs_fn(kc),
                                                   start=(kc == 0), stop=(kc == KC - 1)),
                          reads=[f'wt{s}'] + rhs_names, writes=[f'ps{pi}'])
                evac((g0 + j0) // 128, msz, ps, f'ps{pi}')


    RWN = ['r', 'v', 'kk', 'lw0', 'lw1', 'ka0', 'ka1', 'kd0', 'kd1', 'bonus', 'g', 'y0', 'y1']
    rws = {n: dscr('rw_' + n, [RWW, TP]) for n in RWN}
    vfirst = dscr('vfirst', [RWW, TP])
    vdd = dscr('vd', [64, TP])
    HGN = ['q', 'lf0', 'lf1', 'kf0', 'kf1', 'o0', 'o1']
    hgs = {n: dscr('hg_' + n, [RWW, TP]) for n in HGN}
    xm = dscr('xm', [D, TP])
    xs = dscr('xs', [D, TP])
    ud = RowSplit(dscr, 'u', 2 * DFF, TP + 2)
    for r0 in range(0, 2 * DFF, 128):
        kb.dma('zp', ud[r0:r0 + 128, 0:1], zt[:, 0:1], reads=['zt'], writes=['u'], q='pool', allow_slow_non_contiguous=True)
        kb.dma('zp', ud[r0:r0 + 128, TP + 1:TP + 2], zt[:, 0:1], reads=['zt'], writes=['u'], q='pool', allow_slow_non_contiguous=True)

    def xview(ap):
        return ap.rearrange("(c p) t -> p c t", p=128)

    psr = {'n': 0}

    def pr(w=128):
        i = psr['n'] % 8
        psr['n'] += 1
        return PS[i][:, :w], f'ps{i}'

    cp_rr = {'n': 0}

    def cpy(out, in_, reads, writes, eng=None):
        if eng is None:
            eng = ('dve', 'act')[cp_rr['n'] % 2]
            cp_rr['n'] += 1
        if eng == 'act':
            kb.op('act', lambda e: e.copy(out=out, in_=in_), reads=reads, writes=writes)
        else:
            kb.op(eng, lambda e: e.tensor_copy(out=out, in_=in_), reads=reads, writes=writes)

    def tt(out, a, b, op, reads, writes, eng='dve'):
        kb.op(eng, lambda e: e.tensor_tensor(out=out, in0=a, in1=b, op=op), reads=reads, writes=writes)

    def conv3(out, oname, win, wname_, vname, N=TB, wts=None):
        rows = out.shape[0]
        if wts is not None:
            w0_, w1_, w2_ = wts
            kb.op('act', lambda e: e.activation(out=out, in_=win[:rows, 1:N + 1], func=AF.Copy, scale=w1_[:rows]),
                  reads=[wname_, 'vecs'], writes=[oname])
            kb.op('dve', lambda e: e.scalar_tensor_tensor(out=out, in0=win[:rows, 0:N], scalar=w0_[:rows], in1=out,
                                                          op0=ALU.mult, op1=ALU.add), reads=[wname_, oname, 'vecs'], writes=[oname])
            kb.op('dve', lambda e: e.scalar_tensor_tensor(out=out, in0=win[:rows, 2:N + 2], scalar=w2_[:rows], in1=out,
                                                          op0=ALU.mult, op1=ALU.add), reads=[wname_, oname, 'vecs'], writes=[oname])
            return
        kb.op('act', lambda e: e.activation(out=out, in_=win[:rows, 1:N + 1], func=AF.Copy, scale=V(vname + '_1')[:rows]),
              reads=[wname_, 'vecs'], writes=[oname])
        kb.op('dve', lambda e: e.scalar_tensor_tensor(out=out, in0=win[:rows, 0:N], scalar=V(vname + '_0')[:rows], in1=out,
                                                      op0=ALU.mult, op1=ALU.add), reads=[wname_, oname, 'vecs'], writes=[oname])
        kb.op('dve', lambda e: e.scalar_tensor_tensor(out=out, in0=win[:rows, 2:N + 2], scalar=V(vname + '_2')[:rows], in1=out,
                                                      op0=ALU.mult, op1=ALU.add), reads=[wname_, oname, 'vecs'], writes=[oname])

    def drive(gens):
        act = list(enumerate(gens))
        while act:
            for k, g in list(act):
                kb.ns = f'{k}:'
                try:
                    next(g)
                except StopIteration:
                    act.remove((k, g))
        kb.ns = ''

    xin = xT
    for l in range(L):
        last = (l == L - 1)
        xout = xs
        esl = ExitStack()

        def sbl(name, shape, dt=F32):
            return esl.enter_context(nc.sbuf_tensor(un(name), list(shape), dt)).ap()
        wup_b = sbl('wup_b', [96, 2, RWW], BF16)
        aup_b = sbl('aup_b', [96, 2, RWW], BF16)
        gup_b = sbl('gup_b', [128, 2, RWW], BF16)
        vup_b = sbl('vup_b', [64, RWW], BF16)
        vdn_b = sbl('vdn_b', [128, 16, 64], BF16)
        lbv = sbl('lbv', [128, 6])
        omlb = sbl('omlb', [128, 6])
        omka = sbl('omka', [128, 6])
        mkT_b = sbl('mkT_b', [128, 4, NMEM], BF16)
        mv_b = sbl('mv_b', [128, 2, XAC], BF16)
        with ExitStack() as e0:
            tmpf = e0.enter_context(nc.sbuf_tensor(un('tmpf'), [128, 2 * RWW], F32)).ap()
            memx = e0.enter_context(nc.sbuf_tensor(un('memx'), [128, 16, NMEM], F32)).ap()
            memh = e0.enter_context(nc.sbuf_tensor(un('memh'), [128, 16, NMEM], BF16)).ap()
            sqm = [e0.enter_context(nc.sbuf_tensor(un(f'sqm{i}'), [128, TB], BF16)).ap() for i in range(2)]
            rstdm = e0.enter_context(nc.sbuf_tensor(un('rstdm'), [128, TB], F32)).ap()
            wkv = e0.enter_context(nc.sbuf_tensor(un('wkv'), [128, 16, 2 * XAC], BF16)).ap()
            for d in range(2):
                kb.dma('lsm', tmpf[:96, :RWW], w_up_d[l * 2 + d], writes=['tmpf'])
                cpy(wup_b[:, d, :], tmpf[:96, :RWW], ['tmpf'], ['wup_b'], 'dve')
                kb.dma('lsm', tmpf[:96, :RWW], a_up_d[l * 2 + d], writes=['tmpf'])
                cpy(aup_b[:, d, :], tmpf[:96, :RWW], ['tmpf'], ['aup_b'], 'dve')
            kb.dma('lsm', tmpf.rearrange("p (a c) -> p a c", a=2), g_up_d[l].rearrange("(a p) c -> p a c", p=128), writes=['tmpf'])
            cpy(gup_b, tmpf.rearrange("p (a c) -> p a c", a=2), ['tmpf'], ['gup_b'], 'dve')
            kb.dma('lsm', tmpf[:64, :RWW], v_up_d, writes=['tmpf'])
            cpy(vup_b, tmpf[:64, :RWW], ['tmpf'], ['vup_b'], 'dve')
            kb.dma('lsm', tmpf[:, :1024].rearrange("p (c m) -> p c m", c=16), v_down_d.rearrange("(c p) m -> p c m", p=128), writes=['tmpf'])
            cpy(vdn_b, tmpf[:, :1024].rearrange("p (c m) -> p c m", c=16), ['tmpf'], ['vdn_b'], 'dve')
            if l == 0:
                kb.op('dve', lambda e: e.memset(lbv, 0.0), writes=['lbv'])
            else:
                tt(lbv, V(f'lb1{l}', 0, 6), V(f'lb0{l}', 0, 6), ALU.subtract, ['vecs'], ['lbv'])
                kb.op('act', lambda e: e.activation(out=lbv, in_=lbv, func=AF.Sigmoid), reads=['lbv'], writes=['lbv'])
            kb.op('dve', lambda e: e.tensor_scalar(out=omlb, in0=lbv, scalar1=-1.0, scalar2=1.0, op0=ALU.mult, op1=ALU.add),
                  reads=['lbv'], writes=['omlb'])
            kb.op('dve', lambda e: e.tensor_scalar(out=omka, in0=V(f'ka{l}', 0, 6), scalar1=-1.0, scalar2=1.0, op0=ALU.mult, op1=ALU.add),
                  reads=['vecs'], writes=['omka'])
            kb.dma('lmem', memx, xview(memT), writes=['memx'])
            rmsnorm_block(memx, 'memx', f'mng{l}', memh, 'memh', PS[0], 'ps0', sqm, rstdm, N=NMEM)
            kb.dma('lsm2', wkv, wb['w_mem_kv'][l].rearrange("(kc p) m -> p kc m", p=128), reads=['wb_w_mem_kv'], writes=['wkv'])
            for hd in range(4):
                ps, pn = pr()
                ps = PS[1][:, :NMEM]
                for kc in range(16):
                    kb.op('pe', lambda e: e.matmul(ps, lhsT=wkv[:, kc, hd * 128:(hd + 1) * 128], rhs=memh[:, kc, :],
                                                   start=(kc == 0), stop=(kc == 15)), reads=['wkv', 'memh'], writes=['ps1'])
                cpy(mkT_b[:, hd, :], ps, ['ps1'], ['mkT_b'])
            for mc in range(2):
                ps = PS[2]
                for kc in range(16):
                    kb.op('pe', lambda e: e.matmul(ps, lhsT=memh[:, kc, mc * 128:(mc + 1) * 128], rhs=wkv[:, kc, XAC:2 * XAC],
                                                   start=(kc == 0), stop=(kc == 15)), reads=['wkv', 'memh'], writes=['ps2'])
                cpy(mv_b[:, mc, :], ps, ['ps2'], ['mv_b'])
            kb.barrier()

        with ExitStack() as e1:
            def sb1(name, shape, dt=F32):
                return e1.enter_context(nc.sbuf_tensor(un(name), list(shape), dt)).ap()
            xall = [sb1(f'xall{i}', [128, 16, TB]) for i in range(2)]
            hT = sb1('hT', [128, 16, TB], BF16)
            sq = [sb1(f'sq{i}', [128, TB], BF16) for i in range(2)]
            rstd = sb1('rstd', [128, TB])
            wtiles = [sb1(f'wt{i}', [128, 16, 512], BF16) for i in range(2)]
            stage = [sb1(f'stage{i}', [128, TB]) for i in range(4)]
            ev = {'n': 0}
            for blk in range(NBLK):
                t0 = blk * TB
                xsl = blk % 2
                kb.dma(f'lx{xsl}', xall[xsl], xview(xin)[:, :, t0:t0 + TB], reads=['xin'], writes=[f'xall{xsl}'])
                rmsnorm_block(xall[xsl], f'xall{xsl}', f'n1g{l}', hT, 'hT', PS[0], 'ps0', sq, rstd)

                def evac(mi, msz, ps, psname):
                    s_ = ev['n'] % 4
                    ev['n'] += 1
                    if mi >= (RWC + HGC + XAC) // 128:
                        kb.op('act', lambda e: e.activation(out=stage[s_][:msz], in_=ps[:msz], func=AF.Sigmoid),
                              reads=[psname], writes=[f'stage{s_}'])
                    else:
                        cpy(stage[s_][:msz], ps[:msz], [psname], [f'stage{s_}'])
                    kb.dma(f'st{s_}', proj[mi * 128:mi * 128 + msz, 1 + t0:1 + t0 + TB], stage[s_][:msz],
                           reads=[f'stage{s_}'], writes=['proj'], q='pool')
                linear(wb['w_in'][l], 'wb_w_in', 16, INC, lambda kc: hT[:, kc, :], ['hT'], wtiles, evac)
                if l > 0:
                    ps = PS[1]
                    for kc in range(16):
                        kb.op('pe', lambda e: e.matmul(ps[:64], lhsT=vdn_b[:, kc, :], rhs=hT[:, kc, :], start=(kc == 0), stop=(kc == 15)),
                              reads=['vdn_b', 'hT'], writes=['ps1'])
                    s_ = ev['n'] % 4
                    ev['n'] += 1
                    cpy(stage[s_][:64], ps[:64], ['ps1'], [f'stage{s_}'])
                    kb.dma(f'st{s_}', vdd[:, t0:t0 + TB], stage[s_][:64], reads=[f'stage{s_}'], writes=['vd'], q='pool')
            kb.barrier()
        if stop_after == f'P1_{l}':
            esl.close()
            break

        with ExitStack() as e2:
            def sb2(name, shape, dt=F32):
                return e2.enter_context(nc.sbuf_tensor(un(name), list(shape), dt)).ap()
            W_ = TB + 2
            win = [sb2(f'win{i}', [128, W_]) for i in range(4)]
            wn = {'n': 0}

            def loadwin(row0, rows, t0):
                s_ = wn['n'] % 4
                wn['n'] += 1
                kb.dma(f'lwin{s_}', win[s_][:rows], proj[row0:row0 + rows, t0:t0 + W_], reads=['proj'], writes=[f'win{s_}'])
                return win[s_], f'win{s_}'
            cvt = sb2('cvt', [128, TB])
            twd = sb2('twd', [96, 2, TB], BF16)
            tad = sb2('tad', [96, 2, TB], BF16)
            sgd = sb2('sgd', [128, 2, TB], BF16)
            vdf = sb2('vdf', [64, TB])
            vdb_ = sb2('vdb', [64, TB], BF16)
            maskt = sb2('maskt', [128, TB])
            names = ['r', 'k', 'v', 'kk', 'lw0', 'lw1', 'a0', 'a1', 'ka0', 'ka1', 'kd0', 'kd1', 'bonus', 'g', 't1', 't2', 'vf', 'sg']
            T2 = [{n: sb2(f'{n}_{i}', [128, TB]) for n in names} for i in range(2)]
            hgt = [{n: sb2(f'h{n}_{i}', [128, TB]) for n in ['q', 'fz0', 'fz1', 'f0', 'f1', 'lf0', 'lf1', 'kf0', 'kf1']} for i in range(2)]
            for blk in range(NBLK):
                t0 = blk * TB
                kb.dma('lmask', maskt, maskd[:, 1 + t0:1 + t0 + TB], writes=['maskt'])
                for d in range(2):
                    w_, wn_ = loadwin(2304 + 96 * d, 96, t0)
                    conv3(cvt[:96], 'cvt', w_, wn_, f'rwc{l}_{18 + d}')
                    kb.op('act', lambda e: e.activation(out=twd[:, d, :], in_=cvt[:96], func=AF.Tanh), reads=['cvt'], writes=['twd'])
                    w_, wn_ = loadwin(2496 + 96 * d, 96, t0)
                    conv3(cvt[:96], 'cvt', w_, wn_, f'rwc{l}_{20 + d}')
                    cpy(tad[:, d, :], cvt[:96], ['cvt'], ['tad'], 'dve')
                for a in range(2):
                    w_, wn_ = loadwin(2688 + 128 * a, 128, t0)
                    conv3(cvt, 'cvt', w_, wn_, f'rwc{l}_{22 + a}')
                    kb.op('act', lambda e: e.activation(out=sgd[:, a, :], in_=cvt, func=AF.Sigmoid), reads=['cvt'], writes=['sgd'])
                if l > 0:
                    kb.dma('lvd', vdf, vdd[:, t0:t0 + TB], reads=['vd'], writes=['vdf'])
                    cpy(vdb_, vdf, ['vdf'], ['vdb'], 'dve')
                for c in range(6):
                    i_ = c % 2
                    Tt = T2[i_]

                    def nm(n):
                        return f'{n}_{i_}'
                    cs = slice(c * 128, (c + 1) * 128)
                    for gi, n in ((c, 'r'), (6 + c, 'k'), (12 + c, 'v')):
                        w_, wn_ = loadwin(gi * 128, 128, t0)
                        conv3(Tt[n], nm(n), w_, wn_, f'rwc{l}_{gi}')
                    tt(Tt['k'], Tt['k'], maskt, ALU.mult, [nm('k'), 'maskt'], [nm('k')])
                    tt(Tt['v'], Tt['v'], maskt, ALU.mult, [nm('v'), 'maskt'], [nm('v')])
                    if l == 0:
                        kb.dma(f'sv{i_}', vfirst[cs, t0:t0 + TB], Tt['v'], reads=[nm('v')], writes=['vfirst'], q='pool')
                    else:
                        ps, pn = PS[1], 'ps1'
                        kb.op('pe', lambda e: e.matmul(ps, lhsT=vup_b[:, cs], rhs=vdb_, start=True, stop=True), reads=['vup_b', 'vdb'], writes=[pn])
                        kb.op('act', lambda e: e.activation(out=Tt['sg'], in_=ps, func=AF.Sigmoid, bias=V(f'v0{l}', c)), reads=[pn, 'vecs'], writes=[nm('sg')])
                        kb.dma(f'lvf{i_}', Tt['vf'], vfirst[cs, t0:t0 + TB], reads=['vfirst'], writes=[nm('vf')])
                        tt(Tt['vf'], Tt['vf'], Tt['v'], ALU.subtract, [nm('vf'), nm('v')], [nm('vf')])
                        tt(Tt['vf'], Tt['vf'], Tt['sg'], ALU.mult, [nm('vf'), nm('sg')], [nm('vf')])
                        tt(Tt['v'], Tt['v'], Tt['vf'], ALU.add, [nm('vf'), nm('v')], [nm('v')])
                    for d in range(2):
                        ps, pn = PS[2], 'ps2'
                        kb.op('pe', lambda e: e.matmul(ps, lhsT=wup_b[:, d, cs], rhs=twd[:, d, :], start=True, stop=True), reads=['wup_b', 'twd'], writes=[pn])
                        kb.op('act', lambda e: e.activation(out=Tt[f'lw{d}'], in_=ps, func=AF.Sigmoid, bias=V(f'w0{l}_{d}', c)), reads=[pn, 'vecs'], writes=[nm(f'lw{d}')])
                        kb.op('dve', lambda e: e.tensor_scalar_mul(out=Tt[f'lw{d}'], in0=Tt[f'lw{d}'], scalar1=-DECAY_SCALE), reads=[nm(f'lw{d}')], writes=[nm(f'lw{d}')])
                        ps, pn = PS[3], 'ps3'
                        kb.op('pe', lambda e: e.matmul(ps, lhsT=aup_b[:, d, cs], rhs=tad[:, d, :], start=True, stop=True), reads=['aup_b', 'tad'], writes=[pn])
                        kb.op('act', lambda e: e.activation(out=Tt[f'a{d}'], in_=ps, func=AF.Sigmoid, bias=V(f'a0{l}_{d}', c)), reads=[pn, 'vecs'], writes=[nm(f'a{d}')])
                        kb.op('dve', lambda e: e.tensor_scalar(out=Tt['t1'], in0=Tt[f'a{d}'], scalar1=V(f'ka{l}', c), scalar2=omka[:, c:c + 1], op0=ALU.mult, op1=ALU.add),
                              reads=[nm(f'a{d}'), 'vecs', 'omka'], writes=[nm('t1')])
                        tt(Tt[f'kd{d}'], Tt['t1'], Tt['k'], ALU.mult, [nm('t1'), nm('k')], [nm(f'kd{d}')])
                    kb.op('dve', lambda e: e.tensor_scalar_mul(out=Tt['kk'], in0=Tt['k'], scalar1=V(f'kk{l}', c)), reads=[nm('k'), 'vecs'], writes=[nm('kk')])
                    tt(Tt['t1'], Tt['kk'], Tt['kk'], ALU.mult, [nm('kk')], [nm('t1')])
                    ps, pn = PS[1], 'ps1'
                    kb.op('pe', lambda e: e.matmul(ps, lhsT=cm['bones'], rhs=Tt['t1'], start=True, stop=True), reads=['c_bones', nm('t1')], writes=[pn])
                    kb.op('dve', lambda e: e.tensor_scalar_max(out=Tt['t2'], in0=ps, scalar1=1e-24), reads=[pn], writes=[nm('t2')])
                    rsqrt_(Tt['t2'], nm('t2'))
                    tt(Tt['kk'], Tt['kk'], Tt['t2'], ALU.mult, [nm('kk'), nm('t2')], [nm('kk')])
                    for d in range(2):
                        tt(Tt[f'ka{d}'], Tt['kk'], Tt[f'a{d}'], ALU.mult, [nm('kk'), nm(f'a{d}')], [nm(f'ka{d}')])
                    tt(Tt['t1'], Tt['kd0'], Tt['kd1'], ALU.add, [nm('kd0'), nm('kd1')], [nm('t1')])
                    tt(Tt['t1'], Tt['t1'], Tt['r'], ALU.mult, [nm('t1'), nm('r')], [nm('t1')])
                    kb.op('dve', lambda e: e.tensor_scalar_mul(out=Tt['t1'], in0=Tt['t1'], scalar1=V(f'rk{l}', c)), reads=[nm('t1'), 'vecs'], writes=[nm('t1')])
                    ps, pn = PS[2], 'ps2'
                    kb.op('pe', lambda e: e.matmul(ps, lhsT=cm['bones'], rhs=Tt['t1'], start=True, stop=True), reads=['c_bones', nm('t1')], writes=[pn])
                    tt(Tt['bonus'], ps, Tt['v'], ALU.mult, [pn, nm('v')], [nm('bonus')])
                    ps, pn = PS[3], 'ps3'
                    for a in range(2):
                        kb.op('pe', lambda e: e.matmul(ps, lhsT=gup_b[:, a, cs], rhs=sgd[:, a, :], start=(a == 0), stop=(a == 1)), reads=['gup_b', 'sgd'], writes=[pn])
                    cpy(Tt['g'], ps, [pn], [nm('g')], 'act')
                    for n in ('r', 'v', 'kk', 'lw0', 'lw1', 'ka0', 'ka1', 'kd0', 'kd1', 'bonus', 'g'):
                        kb.dma(f's2_{n}{i_}', rws[n][cs, t0:t0 + TB], Tt[n], reads=[nm(n)], writes=['rw_' + n], q='pool')
                    Ht = hgt[i_]

                    def hn(n):
                        return f'h{n}_{i_}'
                    hb = RWC + c * 128
                    kb.dma(f'lh0{i_}', Ht['q'], proj[hb:hb + 128, 1 + t0:1 + t0 + TB], reads=['proj'], writes=[hn('q')])
                    kb.op('act', lambda e: e.activation(out=Ht['q'], in_=Ht['q'], func=AF.Silu), reads=[hn('q')], writes=[hn('q')])
                    kb.dma(f's2_hq{i_}', hgs['q'][cs, t0:t0 + TB], Ht['q'], reads=[hn('q')], writes=['hg_q'], q='pool')
                    for d in range(2):
                        fb = RWC + RWW * (1 + d) + c * 128
                        kb.dma(f'lh{1 + d}{i_}', Ht[f'fz{d}'], proj[fb:fb + 128, 1 + t0:1 + t0 + TB], reads=['proj'], writes=[hn(f'fz{d}')])
                        kb.op('act', lambda e: e.activation(out=Ht[f'f{d}'], in_=Ht[f'fz{d}'], func=AF.Exp, scale=-1.0), reads=[hn(f'fz{d}')], writes=[hn(f'f{d}')])
                        kb.op('dve', lambda e: e.tensor_scalar_add(out=Ht[f'f{d}'], in0=Ht[f'f{d}'], scalar1=1.0), reads=[hn(f'f{d}')], writes=[hn(f'f{d}')])
                        kb.op('dve', lambda e: e.reciprocal(out=Ht[f'f{d}'], in_=Ht[f'f{d}']), reads=[hn(f'f{d}')], writes=[hn(f'f{d}')])
                        kb.op('dve', lambda e: e.tensor_scalar(out=Ht[f'f{d}'], in0=Ht[f'f{d}'], scalar1=omlb[:, c:c + 1], scalar2=lbv[:, c:c + 1], op0=ALU.mult, op1=ALU.add),
                              reads=[hn(f'f{d}'), 'omlb', 'lbv'], writes=[hn(f'f{d}')])
                        kb.op('act', lambda e: e.activation(out=Ht[f'lf{d}'], in_=Ht[f'f{d}'], func=AF.Ln), reads=[hn(f'f{d}')], writes=[hn(f'lf{d}')])
                        kb.op('dve', lambda e: e.tensor_scalar(out=Ht[f'kf{d}'], in0=Ht[f'f{d}'], scalar1=-1.0, scalar2=1.0, op0=ALU.mult, op1=ALU.add),
                              reads=[hn(f'f{d}')], writes=[hn(f'kf{d}')])
                        kb.dma(f's2_hlf{d}{i_}', hgs[f'lf{d}'][cs, t0:t0 + TB], Ht[f'lf{d}'], reads=[hn(f'lf{d}')], writes=[f'hg_lf{d}'], q='pool')
                        kb.dma(f's2_hkf{d}{i_}', hgs[f'kf{d}'][cs, t0:t0 + TB], Ht[f'kf{d}'], reads=[hn(f'kf{d}')], writes=[f'hg_kf{d}'], q='pool')
            kb.barrier()
        if stop_after == f'P2a_{l}':
            esl.close()
            break

        def v3(ap):
            return ap.rearrange("p (c t) -> p c t", t=CH)
        NCH = TB // CH
        with ExitStack() as e3:
            def sb3(name, shape, dt=F32):
                return e3.enter_context(nc.sbuf_tensor(un(name), list(shape), dt)).ap()
            def rw_factory(k):
                kb.ns = f'{k}:'
                LN_ = ['lw', 'ka', 'kd', 'kk', 'r', 'v']
                ld = [{n: sb3(f'l{n}_{i}', [128, TB]) for n in LN_} for i in range(1)]
                bcum = sb3('bcum', [128, TB]); bcur_t = sb3('bcur', [128, TB]); excl = sb3('excl', [128, TB])
                eex = sb3('eex', [128, TB]); enb = sb3('enb', [128, TB])
                ebs = [sb3(f'eb{i}', [128, TB]) for i in range(1)]
                EX = {n: [sb3(f'E{n}_{i}', [128, NCH, 2, CH], BF16) for i in range(1)] for n in 'abkrv'}
                rbar = [sb3(f'rbar{i}', [128, TB], BF16) for i in range(1)]
                ybuf = [sb3(f'ybuf{i}', [128, TB]) for i in range(1)]
                S = sb3('S', [128, 128]); Sb = sb3('Sb', [128, 128], BF16)
                Pm = [sb3(f'Pm{i}', [128, 128]) for i in range(2)]
                PTm = [sb3(f'PTm{i}', [128, 128]) for i in range(2)]
                TTm = [sb3(f'TTm{i}', [128, 128]) for i in range(2)]
                AakT = sb3('AakT', [128, 128], BF16); ArbT = sb3('ArbT', [128, CH], BF16); ArkT = sb3('ArkT', [128, CH], BF16)
                Btok = sb3('Btok', [128, 128], BF16); Ktok = sb3('Ktok', [128, 128], BF16); Vbd = sb3('Vbd', [128, 128], BF16)
                Xf = sb3('Xf', [128, 128]); Ub = sb3('Ub', [128, 128], BF16)
                for n in 'abkrv':
                    for i in range(1):
                        kb.op('pool', lambda e: e.memset(EX[n][i].rearrange("p c h t -> p (c h t)"), 0.0), writes=[f'E{n}_{i}'])
                kb.ns = ''
                bi = 0

                def chain(c, d):
                    nonlocal bi
                    cs = slice(c * 128, (c + 1) * 128)
                    kb.op('dve', lambda e: e.memset(S, 0.0), reads=[], writes=['S'])
                    kb.op('dve', lambda e: e.memset(Sb, 0.0), reads=[], writes=['Sb'])
                    mAT, mA, mInc = (cm['mlt'], cm['mgt'], cm['mle']) if d == 0 else (cm['mgt'], cm['mlt'], cm['mge'])
                    mATn, mAn, mIncn = ('c_mlt', 'c_mgt', 'c_mle') if d == 0 else ('c_mgt', 'c_mlt', 'c_mge')
                    blocks = list(range(NBLK)) if d == 0 else list(reversed(range(NBLK)))
                    for blk in blocks:
                        i_ = 0
                        bi += 1
                        t0 = blk * TB
                        Lt = ld[i_]
                        for n, src in (('lw', rws[f'lw{d}']), ('ka', rws[f'ka{d}']), ('kd', rws[f'kd{d}']), ('kk', rws['kk']), ('r', rws['r']), ('v', rws['v'])):
                            kb.dma(f'l3{n}{i_}', Lt[n], src[cs, t0:t0 + TB], reads=['rwsin'], writes=[f'l{n}_{i_}'])

                        def ln(n):
                            return f'l{n}_{i_}'
                        kb.op('dve', lambda e: e.tensor_tensor_scan(out=bcum, data0=cm['rst'], data1=Lt['lw'], initial=0.0, op0=ALU.mult, op1=ALU.add),
                              reads=['c_rst', ln('lw')], writes=['bcum'])
                        if d == 0:
                            bcur, bcn = bcum, 'bcum'
                            tt(excl, bcum, Lt['lw'], ALU.subtract, ['bcum', ln('lw')], ['excl'])
                            yield
                        else:
                            tot = v3(bcum)[:, :, CH - 1:CH].broadcast_to([128, NCH, CH])
                            tt(v3(excl), tot, v3(bcum), ALU.subtract, ['bcum'], ['excl'])
                            yield
                            tt(bcur_t, excl, Lt['lw'], ALU.add, ['excl', ln('lw')], ['bcur'])
                            yield
                            bcur, bcn = bcur_t, 'bcur'
                        eb = ebs[i_]
                        kb.op('act', lambda e: e.activation(out=eex, in_=excl, func=AF.Exp), reads=['excl'], writes=['eex'])
                        kb.op('act', lambda e: e.activation(out=eb, in_=bcur, func=AF.Exp), reads=[bcn], writes=[f'eb{i_}'])
                        kb.op('act', lambda e: e.activation(out=enb, in_=bcur, func=AF.Exp, scale=-1.0), reads=[bcn], writes=['enb'])
                        for hf in range(2):
                            rs = slice(64 * hf, 64 * hf + 64)
                            kb.op('dve', lambda e: e.scalar_tensor_tensor(out=EX['a'][i_][rs, :, hf, :], in0=v3(Lt['kk'])[rs], scalar=-1.0, in1=v3(eex)[rs],
                                                                          op0=ALU.mult, op1=ALU.mult), reads=[ln('kk'), 'eex'], writes=[f'Ea_{i_}'])
                            tt(EX['r'][i_][rs, :, hf, :], v3(Lt['r'])[rs], v3(eb)[rs], ALU.mult, [ln('r'), f'eb{i_}'], [f'Er_{i_}'], 'pool')
                            yield
                            tt(EX['b'][i_][rs, :, hf, :], v3(Lt['ka'])[rs], v3(enb)[rs], ALU.mult, [ln('ka'), 'enb'], [f'Eb_{i_}'])
                            yield
                            tt(EX['k'][i_][rs, :, hf, :], v3(Lt['kd'])[rs], v3(enb)[rs], ALU.mult, [ln('kd'), 'enb'], [f'Ek_{i_}'], 'pool')
                            yield
                            kb.op('act', lambda e: e.copy(out=EX['v'][i_][rs, :, hf, :], in_=v3(Lt['v'])[rs]), reads=[ln('v')], writes=[f'Ev_{i_}'])
                        tt(rbar[i_], Lt['r'], eb, ALU.mult, [ln('r'), f'eb{i_}'], [f'rbar{i_}'], 'pool')
                        yield
                        chunks = list(range(NCH)) if d == 0 else list(reversed(range(NCH)))
                        for ch in chunks:
                            Ea, Eb, Ek, Er, Ev = (EX[n][i_][:, ch].rearrange("p h t -> p (h t)") for n in 'abkrv')
                            En = {n: f'E{n}_{i_}' for n in 'abkrv'}
                            rb = rbar[i_][:, ch * CH:(ch + 1) * CH]
                            ec = eb[:, ch * CH + CH - 1:ch * CH + CH] if d == 0 else eb[:, ch * CH:ch * CH + 1]
                            ps, pn = pr()
                            kb.op('pe', lambda e: e.matmul(ps, lhsT=Eb, rhs=Ea, start=True, stop=True), reads=[En['b'], En['a']], writes=[pn])
                            tt(PTm[0], ps, mAT, ALU.mult, [pn, mATn], ['PTm0'])
                            yield
                            ps, pn = pr()
                            kb.op('pe', lambda e: e.matmul(ps, lhsT=Ea, rhs=Eb, start=True, stop=True), reads=[En['b'], En['a']], writes=[pn])
                            tt(Pm[0], ps, mA, ALU.mult, [pn, mAn], ['Pm0'])
                            yield
                            tt(TTm[0], PTm[0], cm['ident'], ALU.add, ['PTm0', 'c_ident'], ['TTm0'], 'pool')
                            yield
                            g_ = 0
                            for j in range(5):
                                g2 = 1 - g_
                                ps, pn = pr()
                                kb.op('pe', lambda e: e.matmul(ps, lhsT=PTm[g_], rhs=Pm[g_], start=True, stop=True), reads=[f'PTm{g_}', f'Pm{g_}'], writes=[pn])
                                cpy(Pm[g2], ps, [pn], [f'Pm{g2}'])
                                yield
                                if j < 4:
                                    ps2, pn2 = pr()
                                    kb.op('pe', lambda e: e.matmul(ps2, lhsT=Pm[g_], rhs=PTm[g_], start=True, stop=True), reads=[f'PTm{g_}', f'Pm{g_}'], writes=[pn2])
                                    cpy(PTm[g2], ps2, [pn2], [f'PTm{g2}'])
                                    yield
                                ps3, pn3 = pr()
                                kb.op('pe', lambda e: e.matmul(ps3, lhsT=Pm[g2], rhs=TTm[g_], start=True, stop=True), reads=[f'Pm{g2}', f'TTm{g_}'], writes=[pn3])
                                tt(TTm[g2], ps3, TTm[g_], ALU.add, [pn3, f'TTm{g_}'], [f'TTm{g2}'])
                                yield
                                g_ = g2
                            TT, TTn = TTm[g_], f'TTm{g_}'
                            ps, pn = pr()
                            kb.op('pe', lambda e: e.matmul(ps, lhsT=Ek, rhs=Ea, start=True, stop=True), reads=[En['k'], En['a']], writes=[pn])
                            tt(AakT, ps, mAT, ALU.mult, [pn, mATn], ['AakT'])
                            yield
                            ps, pn = pr(CH)
                            kb.op('pe', lambda e: e.matmul(ps, lhsT=Eb, rhs=rb, start=True, stop=True), reads=[En['b'], f'rbar{i_}'], writes=[pn])
                            tt(ArbT, ps, mInc, ALU.mult, [pn, mIncn], ['ArbT'])
                            yield
                            ps, pn = pr(CH)
                            kb.op('pe', lambda e: e.matmul(ps, lhsT=Ek, rhs=rb, start=True, stop=True), reads=[En['k'], f'rbar{i_}'], writes=[pn])
                            tt(ArkT, ps, mInc, ALU.mult, [pn, mIncn], ['ArkT'])
                            yield
                            for src, sn, dst, dn in ((Eb, En['b'], Btok, 'Btok'), (Ek, En['k'], Ktok, 'Ktok'), (Ev, En['v'], Vbd, 'Vbd')):
                                ps, pn = pr()
                                kb.op('pe', lambda e: e.matmul(ps, lhsT=src, rhs=ident_b, start=True, stop=True), reads=[sn, 'ident_b'], writes=[pn])
                                cpy(dst, ps, [pn], [dn])
                                yield
                            ps, pn = pr()
                            kb.op('pe', lambda e: e.matmul(ps, lhsT=Ea, rhs=Sb, start=True, stop=False), reads=[En['a'], 'Sb'], writes=[pn])
                            kb.op('pe', lambda e: e.matmul(ps, lhsT=AakT, rhs=Vbd, start=False, stop=True), reads=['AakT', 'Vbd'], writes=[pn])
                            cpy(Xf, ps, [pn], ['Xf'])
                            yield
                            ps, pn = pr()
                            kb.op('pe', lambda e: e.matmul(ps, lhsT=TT, rhs=Xf, start=True, stop=True), reads=[TTn, 'Xf'], writes=[pn])
                            cpy(Ub, ps, [pn], ['Ub'])
                            yield
                            ps, pn = pr(CH)
                            kb.op('pe', lambda e: e.matmul(ps, lhsT=Sb, rhs=rb, start=True, stop=False), reads=['Sb', f'rbar{i_}'], writes=[pn])
                            kb.op('pe', lambda e: e.matmul(ps, lhsT=Ub, rhs=ArbT, start=False, stop=False), reads=['Ub', 'ArbT'], writes=[pn])
                            kb.op('pe', lambda e: e.matmul(ps, lhsT=Vbd, rhs=ArkT, start=False, stop=True), reads=['Vbd', 'ArkT'], writes=[pn])
                            cpy(ybuf[i_][:, ch * CH:(ch + 1) * CH], ps, [pn], [f'ybuf{i_}'])
                            yield
                            ps, pn = pr()
                            kb.op('pe', lambda e: e.matmul(ps, lhsT=Btok, rhs=Ub, start=True, stop=False), reads=['Btok', 'Ub'], writes=[pn])
                            kb.op('pe', lambda e: e.matmul(ps, lhsT=Ktok, rhs=Vbd, start=False, stop=True), reads=['Ktok', 'Vbd'], writes=[pn])
                            tt(S, S, ps, ALU.add, ['S', pn], ['S'])
                            yield
                            kb.op('dve', lambda e: e.tensor_scalar_mul(out=S, in0=S, scalar1=ec), reads=['S', f'eb{i_}'], writes=['S'])
                            cpy(Sb, S, ['S'], ['Sb'], 'act')
                            yield
                            yield
                        kb.dma(f's3y{i_}', rws[f'y{d}'][cs, t0:t0 + TB], ybuf[i_], reads=[f'ybuf{i_}'], writes=['rw_y'], q='pool')
                return chain
            rw_chains = [rw_factory(k) for k in range(3)]
            rw_units = [(c, d) for c in range(6) for d in range(2)]
            for u0 in range(0, 12, 3):
                drive([rw_chains[k](*rw_units[u0 + k]) for k in range(3)])
            kb.barrier()
        if stop_after == f'P2b_{l}':
            esl.close()
            break

        with ExitStack() as e4:
            def sb4(name, shape, dt=F32):
                return e4.enter_context(nc.sbuf_tensor(un(name), list(shape), dt)).ap()
            def hg_factory(k):
                ld = [{n: sb4(f'g{n}_{i}', [128, TB]) for n in ['q', 'lf', 'kf', 'iv']} for i in range(2)]
                bcum = sb4('hbcum', [128, TB]); bcur_t = sb4('hbcur', [128, TB]); bm = sb4('hbm', [128, TB]); bc = sb4('hbc', [128, TB])
                ex1 = sb4('hex1', [128, TB]); ex2 = sb4('hex2', [128, TB]); ex3 = sb4('hex3', [128, TB])
                ebs = [sb4(f'heb{i}', [128, TB]) for i in range(2)]
                qn = [sb4(f'qn{i}', [128, TB], BF16) for i in range(2)]
                kn = [sb4(f'kn{i}', [128, TB], BF16) for i in range(2)]
                qh = [sb4(f'qh{i}', [128, TB], BF16) for i in range(2)]
                kh = [sb4(f'kh{i}', [128, TB], BF16) for i in range(2)]
                ivb = [sb4(f'ivb{i}', [128, TB], BF16) for i in range(2)]
                obuf = [sb4(f'obuf{i}', [128, TB]) for i in range(2)]
                S = sb4('hS', [128, 128]); Sb = sb4('hSb', [128, 128], BF16)
                scb = sb4('scb', [CH, CH], BF16); Vtok = sb4('Vtok', [CH, 128], BF16); Ktk = sb4('Ktk', [CH, 128], BF16)
                bi = 0

                def chain(c, d):
                    nonlocal bi
                    cs = slice(c * 128, (c + 1) * 128)
                    ivrow = RWC + 3 * RWW + c * 128
                    kb.op('dve', lambda e: e.memset(S, 0.0), writes=['hS'])
                    kb.op('dve', lambda e: e.memset(Sb, 0.0), writes=['hSb'])
                    msk, mskn = (cm['tle'], 'c_tle') if d == 0 else (cm['tge'], 'c_tge')
                    blocks = list(range(NBLK)) if d == 0 else list(reversed(range(NBLK)))
                    for blk in blocks:
                        i_ = bi % 2
                        bi += 1
                        t0 = blk * TB
                        Lt = ld[i_]
                        kb.dma(f'l4q{i_}', Lt['q'], hgs['q'][cs, t0:t0 + TB], reads=['hgin'], writes=[f'gq_{i_}'])
                        kb.dma(f'l4lf{i_}', Lt['lf'], hgs[f'lf{d}'][cs, t0:t0 + TB], reads=['hgin'], writes=[f'glf_{i_}'])
                        kb.dma(f'l4kf{i_}', Lt['kf'], hgs[f'kf{d}'][cs, t0:t0 + TB], reads=['hgin'], writes=[f'gkf_{i_}'])
                        kb.dma(f'l4iv{i_}', Lt['iv'], proj[ivrow:ivrow + 128, 1 + t0:1 + t0 + TB], reads=['proj'], writes=[f'giv_{i_}'])

                        def gn(n):
                            return f'g{n}_{i_}'
                        kb.op('dve', lambda e: e.tensor_tensor_scan(out=bcum, data0=cm['rst'], data1=Lt['lf'], initial=0.0, op0=ALU.mult, op1=ALU.add),
                              reads=['c_rst', gn('lf')], writes=['hbcum'])
                        if d == 0:
                            bcur, bcn = bcum, 'hbcum'
                            cidx = CH - 1
                        else:
                            tot = v3(bcum)[:, :, CH - 1:CH].broadcast_to([128, NCH, CH])
                            tt(v3(bcur_t), tot, v3(bcum), ALU.subtract, ['hbcum'], ['hbcur'])
                            yield
                            tt(bcur_t, bcur_t, Lt['lf'], ALU.add, ['hbcur', gn('lf')], ['hbcur'])
                            yield
                            bcur, bcn = bcur_t, 'hbcur'
                            cidx = 0
                        ref = v3(bcur)[:, :, 31:32].broadcast_to([128, NCH, CH])
                        tt(v3(bm), v3(bcur), ref, ALU.subtract, [bcn], ['hbm'])
                        bC = v3(bcur)[:, :, cidx:cidx + 1].broadcast_to([128, NCH, CH])
                        tt(v3(bc), v3(bcur), bC, ALU.subtract, [bcn], ['hbc'], 'pool')
                        eb = ebs[i_]
                        kb.op('act', lambda e: e.activation(out=ex1, in_=bm, func=AF.Exp), reads=['hbm'], writes=['hex1'])
                        kb.op('act', lambda e: e.activation(out=ex2, in_=bm, func=AF.Exp, scale=-1.0), reads=['hbm'], writes=['hex2'])
                        kb.op('act', lambda e: e.activation(out=eb, in_=bcur, func=AF.Exp), reads=[bcn], writes=[f'heb{i_}'])
                        kb.op('act', lambda e: e.activation(out=ex3, in_=bc, func=AF.Exp, scale=-1.0), reads=['hbc'], writes=['hex3'])
                        tt(qn[i_], Lt['q'], ex1, ALU.mult, [gn('q'), 'hex1'], [f'qn{i_}'])
                        tt(kn[i_], Lt['kf'], ex2, ALU.mult, [gn('kf'), 'hex2'], [f'kn{i_}'], 'pool')
                        tt(qh[i_], Lt['q'], eb, ALU.mult, [gn('q'), f'heb{i_}'], [f'qh{i_}'])
                        tt(kh[i_], Lt['kf'], ex3, ALU.mult, [gn('kf'), 'hex3'], [f'kh{i_}'], 'pool')
                        cpy(ivb[i_], Lt['iv'], [gn('iv')], [f'ivb{i_}'], 'act')
                        yield
                        chunks = list(range(NCH)) if d == 0 else list(reversed(range(NCH)))
                        for ch in chunks:
                            sl = slice(ch * CH, (ch + 1) * CH)
                            ec = eb[:, ch * CH + cidx:ch * CH + cidx + 1]
                            ps, pn = pr(CH)
                            kb.op('pe', lambda e: e.matmul(ps[:CH], lhsT=kn[i_][:, sl], rhs=qn[i_][:, sl], start=True, stop=True), reads=[f'kn{i_}', f'qn{i_}'], writes=[pn])
                            tt(scb, ps[:CH], msk, ALU.mult, [pn, mskn], ['scb'])
                            yield
                            ps, pn = pr()
                            kb.op('pe', lambda e: e.matmul(ps[:CH], lhsT=ivb[i_][:, sl], rhs=ident_b, start=True, stop=True), reads=[f'ivb{i_}', 'ident_b'], writes=[pn])
                            cpy(Vtok, ps[:CH], [pn], ['Vtok'])
                            yield
                            ps, pn = pr()
                            kb.op('pe', lambda e: e.matmul(ps[:CH], lhsT=kh[i_][:, sl], rhs=ident_b, start=True, stop=True), reads=[f'kh{i_}', 'ident_b'], writes=[pn])
                            cpy(Ktk, ps[:CH], [pn], ['Ktk'])
                            yield
                            ps, pn = pr(CH)
                            kb.op('pe', lambda e: e.matmul(ps, lhsT=Sb, rhs=qh[i_][:, sl], start=True, stop=False), reads=['hSb', f'qh{i_}'], writes=[pn])
                            kb.op('pe', lambda e: e.matmul(ps, lhsT=Vtok, rhs=scb, start=False, stop=True), reads=['Vtok', 'scb'], writes=[pn])
                            cpy(obuf[i_][:, sl], ps, [pn], [f'obuf{i_}'])
                            yield
                            ps, pn = pr()
                            kb.op('pe', lambda e: e.matmul(ps, lhsT=Ktk, rhs=Vtok, start=True, stop=True), reads=['Ktk', 'Vtok'], writes=[pn])
                            kb.op('dve', lambda e: e.scalar_tensor_tensor(out=S, in0=S, scalar=ec, in1=ps, op0=ALU.mult, op1=ALU.add),
                                  reads=['hS', pn, f'heb{i_}'], writes=['hS'])
                            cpy(Sb, S, ['hS'], ['hSb'], 'act')
                            yield
                            yield
                        kb.dma(f's4o{i_}', hgs[f'o{d}'][cs, t0:t0 + TB], obuf[i_], reads=[f'obuf{i_}'], writes=['hg_o'], q='pool')
                return chain
            hg_chains = [hg_factory(k) for k in range(3)]
            units = [(c, d) for c in range(6) for d in range(2)]
            for u0 in range(0, 12, 3):
                drive([hg_chains[k](*units[u0 + k]) for k in range(3)])
            kb.barrier()
        if stop_after == f'P2c_{l}':
            esl.close()
            break

        with ExitStack() as e5:
            def sb5(name, shape, dt=F32):
                return e5.enter_context(nc.sbuf_tensor(un(name), list(shape), dt)).ap()
            xall_ = sb5('xall', [128, 16, TB])
            hT = sb5('hT', [128, 16, TB], BF16)
            sq = [sb5(f'sq{i}', [128, TB], BF16) for i in range(2)]
            rstd = sb5('rstd', [128, TB])
            wtiles = [sb5(f'wt{i}', [128, 16, 512], BF16) for i in range(2)]
            stage = [sb5(f'stage{i}', [128, TB]) for i in range(4)]
            yrw = sb5('yrw', [128, 6, TB], BF16)
            yhg = sb5('yhg', [128, 6, TB], BF16)
            yxa = sb5('yxa', [128, 4, TB], BF16)
            mixed = sb5('mixed', [128, 16, TB], BF16)
            maskt = sb5('maskt', [128, TB])
            WK = [{n: sb5(f'k{n}_{i}', [128, TB]) for n in ['a', 'b', 'c', 'd', 'y', 'yc', 't', 'rs']} for i in range(2)]
            pT = [sb5(f'pT{i}', [128, TB], BF16) for i in range(2)]
            qb = sb5('qb', [128, TB], BF16)
            ev = {'n': 0}
            wkn = {'n': 0}
            GB = RWC + HGC + XAC
            for blk in range(NBLK):
                t0 = blk * TB
                kb.dma('lmask', maskt, maskd[:, 1 + t0:1 + t0 + TB], writes=['maskt'])
                kb.dma('lx0', xall_, xview(xin)[:, :, t0:t0 + TB], reads=['xin'], writes=['xall'])
                for c in range(6):
                    cs = slice(c * 128, (c + 1) * 128)
                    i_ = wkn['n'] % 2
                    wkn['n'] += 1
                    W = WK[i_]

                    def kn_(n):
                        return f'k{n}_{i_}'
                    for n, src in (('a', rws['y0']), ('b', rws['y1']), ('c', rws['bonus']), ('d', rws['g'])):
                        kb.dma(f'l5{n}{i_}', W[n], src[cs, t0:t0 + TB], reads=['rwsin'], writes=[kn_(n)])
                    tt(W['y'], W['a'], W['b'], ALU.add, [kn_('a'), kn_('b')], [kn_('y')])
                    ps, pn = PS[1], 'ps1'
                    kb.op('pe', lambda e: e.matmul(ps, lhsT=cm['bones'], rhs=W['y'], start=True, stop=True), reads=['c_bones', kn_('y')], writes=[pn])
                    kb.op('dve', lambda e: e.scalar_tensor_tensor(out=W['yc'], in0=ps, scalar=-1.0 / 64, in1=W['y'], op0=ALU.mult, op1=ALU.add),
                          reads=[pn, kn_('y')], writes=[kn_('yc')])
                    tt(W['t'], W['yc'], W['yc'], ALU.mult, [kn_('yc')], [kn_('t')], 'pool')
                    ps, pn = PS[2], 'ps2'
                    kb.op('pe', lambda e: e.matmul(ps, lhsT=cm['bones'], rhs=W['t'], start=True, stop=True), reads=['c_bones', kn_('t')], writes=[pn])
                    kb.op('dve', lambda e: e.tensor_scalar(out=W['rs'], in0=ps, scalar1=1.0 / 64, scalar2=64e-5, op0=ALU.mult, op1=ALU.add),
                          reads=[pn], writes=[kn_('rs')])
                    rsqrt_(W['rs'], kn_('rs'))
                    tt(W['yc'], W['yc'], W['rs'], ALU.mult, [kn_('yc'), kn_('rs')], [kn_('yc')])
                    kb.op('dve', lambda e: e.tensor_scalar(out=W['yc'], in0=W['yc'], scalar1=V(f'lnw{l}', c), scalar2=V(f'lnb{l}', c), op0=ALU.mult, op1=ALU.add),
                          reads=[kn_('yc'), 'vecs'], writes=[kn_('yc')])
                    tt(W['yc'], W['yc'], W['c'], ALU.add, [kn_('yc'), kn_('c')], [kn_('yc')], 'pool')
                    tt(yrw[:, c, :], W['yc'], W['d'], ALU.mult, [kn_('yc'), kn_('d')], ['yrw'])
                for c in range(6):
                    cs = slice(c * 128, (c + 1) * 128)
                    i_ = wkn['n'] % 2
                    wkn['n'] += 1
                    W = WK[i_]
                    ogr = RWC + 4 * RWW + c * 128
                    kb.dma(f'l5a{i_}', W['a'], hgs['o0'][cs, t0:t0 + TB], reads=['hgin'], writes=[kn_('a')])
                    kb.dma(f'l5b{i_}', W['b'], hgs['o1'][cs, t0:t0 + TB], reads=['hgin'], writes=[kn_('b')])
                    kb.dma(f'l5c{i_}', W['c'], proj[ogr:ogr + 128, 1 + t0:1 + t0 + TB], reads=['proj'], writes=[kn_('c')])
                    tt(W['y'], W['a'], W['b'], ALU.add, [kn_('a'), kn_('b')], [kn_('y')])
                    tt(W['t'], W['y'], W['y'], ALU.mult, [kn_('y')], [kn_('t')], 'pool')
                    ps, pn = PS[1], 'ps1'
                    kb.op('pe', lambda e: e.matmul(ps, lhsT=cm['ones'], rhs=W['t'], start=True, stop=True), reads=['c_ones', kn_('t')], writes=[pn])
                    kb.op('dve', lambda e: e.tensor_scalar(out=W['rs'], in0=ps, scalar1=1.0 / 128, scalar2=1e-6, op0=ALU.mult, op1=ALU.add),
                          reads=[pn], writes=[kn_('rs')])
                    rsqrt_(W['rs'], kn_('rs'))
                    tt(W['y'], W['y'], W['rs'], ALU.mult, [kn_('y'), kn_('rs')], [kn_('y')])
                    kb.op('act', lambda e: e.activation(out=W['d'], in_=W['c'], func=AF.Silu), reads=[kn_('c')], writes=[kn_('d')])
                    kb.op('dve', lambda e: e.scalar_tensor_tensor(out=yhg[:, c, :], in0=W['y'], scalar=V(f'hgn{l}', c), in1=W['d'], op0=ALU.mult, op1=ALU.mult),
                          reads=[kn_('y'), kn_('d'), 'vecs'], writes=['yhg'])
                for hd in range(4):
                    i_ = wkn['n'] % 2
                    wkn['n'] += 1
                    W = WK[i_]
                    qr = RWC + HGC + hd * 128
                    kb.dma(f'l5a{i_}', W['a'], proj[qr:qr + 128, 1 + t0:1 + t0 + TB], reads=['proj'], writes=[kn_('a')])
                    cpy(qb, W['a'], [kn_('a')], ['qb'], 'dve')
                    for mc in range(2):
                        ps, pn = PS[1 + mc], f'ps{1 + mc}'
                        kb.op('pe', lambda e: e.matmul(ps, lhsT=mkT_b[:, hd, mc * 128:(mc + 1) * 128], rhs=qb, start=True, stop=True), reads=['mkT_b', 'qb'], writes=[pn])
                        kb.op('act', lambda e: e.activation(out=pT[mc], in_=ps, func=AF.Exp, scale=float(128 ** -0.5)), reads=[pn], writes=[f'pT{mc}'])
                    ps, pn = PS[3], 'ps3'
                    for mc in range(2):
                        kb.op('pe', lambda e: e.matmul(ps, lhsT=ones_b, rhs=pT[mc], start=(mc == 0), stop=(mc == 1)), reads=['ones_b', f'pT{mc}'], writes=[pn])
                    kb.op('dve', lambda e: e.reciprocal(out=W['rs'], in_=ps), reads=[pn], writes=[kn_('rs')])
                    ps, pn = PS[1], 'ps1'
                    for mc in range(2):
                        kb.op('pe', lambda e: e.matmul(ps, lhsT=mv_b[:, mc, hd * 128:(hd + 1) * 128], rhs=pT[mc], start=(mc == 0), stop=(mc == 1)), reads=['mv_b', f'pT{mc}'], writes=[pn])
                    tt(yxa[:, hd, :], ps, W['rs'], ALU.mult, [pn, kn_('rs')], ['yxa'])
                Wv = wb['w_branch_out'][l].rearrange("(kc p) m -> p kc m", p=128)
                for g0 in range(0, D, 512):
                    s_ = lin_state['g'] % 2
                    lin_state['g'] += 1
                    wt = wtiles[s_]
                    kb.dma(f'lw{s_}', wt, Wv[:, :, g0:g0 + 512], reads=['wb_wbo'], writes=[f'wt{s_}'])
                    for j0 in range(0, 512, 128):
                        oc = (g0 + j0) // 128
                        i_ = wkn['n'] % 2
                        wkn['n'] += 1
                        W = WK[i_]
                        pss = []
                        for br, (k0, k1, src, srcn) in enumerate(((0, 6, yrw, 'yrw'), (6, 12, yhg, 'yhg'), (12, 16, yxa, 'yxa'))):
                            pi = 4 + lin_state['ps'] % 4
                            lin_state['ps'] += 1
                            ps = PS[pi]
                            for kc in range(k0, k1):
                                kb.op('pe', lambda e: e.matmul(ps, lhsT=wt[:, kc, j0:j0 + 128], rhs=src[:, kc - k0, :], start=(kc == k0), stop=(kc == k1 - 1)),
                                      reads=[f'wt{s_}', srcn], writes=[f'ps{pi}'])
                            pss.append((ps, f'ps{pi}'))
                            gr = GB + br * D + oc * 128
                            kb.dma(f'l5{"abc"[br]}{i_}', W['abc'[br]], proj[gr:gr + 128, 1 + t0:1 + t0 + TB], reads=['proj'], writes=[kn_('abc'[br])])
                        tt(W['y'], pss[0][0], W['a'], ALU.mult, [pss[0][1], kn_('a')], [kn_('y')])
                        tt(W['t'], pss[1][0], W['b'], ALU.mult, [pss[1][1], kn_('b')], [kn_('t')])
                        tt(W['y'], W['y'], W['t'], ALU.add, [kn_('y'), kn_('t')], [kn_('y')], 'pool')
                        tt(W['t'], pss[2][0], W['c'], ALU.mult, [pss[2][1], kn_('c')], [kn_('t')])
                        tt(mixed[:, oc, :], W['y'], W['t'], ALU.add, [kn_('y'), kn_('t')], ['mixed'], 'pool')

                def evac_o(mi, msz, ps, psname):
                    i2 = wkn['n'] % 2
                    wkn['n'] += 1
                    tt(WK[i2]['t'], ps, xall_[:, mi, :], ALU.add, [psname, 'xall'], [f'kt_{i2}'])
                    tt(xall_[:, mi, :], WK[i2]['t'], maskt, ALU.mult, [f'kt_{i2}', 'maskt'], ['xall'], 'pool')
                linear(wb['w_o'][l], 'wb_w_o', 16, D, lambda kc: mixed[:, kc, :], ['mixed'], wtiles, evac_o)
                kb.dma('sxm', xview(xm)[:, :, t0:t0 + TB], xall_, reads=['xall'], writes=['xm'], q='pool')
                rmsnorm_block(xall_, 'xall', f'n2g{l}', hT, 'hT', PS[0], 'ps0', sq, rstd)

                def evac_u(mi, msz, ps, psname):
                    s2 = ev['n'] % 4
                    ev['n'] += 1
                    cpy(stage[s2], ps, [psname], [f'stage{s2}'])
                    kb.dma(f'st{s2}', ud[mi * 128:(mi + 1) * 128, 1 + t0:1 + t0 + TB], stage[s2], reads=[f'stage{s2}'], writes=['u'], q='pool')
                linear(wb['ffn_up'][l], 'wb_ffn_up', 16, 2 * DFF, lambda kc: hT[:, kc, :], ['hT'], wtiles, evac_u)
            kb.barrier()
        if stop_after == f'P3_{l}':
            esl.close()
            break

        with ExitStack() as e6:
            def sb6(name, shape, dt=F32):
                return e6.enter_context(nc.sbuf_tensor(un(name), list(shape), dt)).ap()
            xall_ = sb6('xall', [128, 16, TB])
            gT = sb6('gT', [128, 43, TB], BF16)
            wt2 = [sb6(f'wt{i}', [128, 43, 256], BF16) for i in range(2)]
            W_ = TB + 2
            win = [sb6(f'win{i}', [128, W_]) for i in range(4)]
            at = [sb6(f'at{i}', [128, TB]) for i in range(2)]
            lt_ = [sb6(f'lt{i}', [128, TB]) for i in range(2)]
            maskt = sb6('maskt', [128, TB])
            if last:
                hfin = sb6('hfin', [128, 16, TB])
                sq = [sb6(f'sq{i}', [128, TB], BF16) for i in range(2)]
                rstd = sb6('rstd', [128, TB])
            wn = {'n': 0}
            for blk in range(NBLK):
                t0 = blk * TB
                kb.dma('lx0', xall_, xview(xm)[:, :, t0:t0 + TB], reads=['xm'], writes=['xall'])
                kb.dma('lmask', maskt, maskd[:, 1 + t0:1 + t0 + TB], writes=['maskt'])
                for j in range(43):
                    i_ = j % 2
                    outs = []
                    for tile_, tn, chn in ((at[i_], f'at{i_}', j), (lt_[i_], f'lt{i_}', 43 + j)):
                        s_ = wn['n'] % 4
                        wn['n'] += 1
                        kb.dma(f'lwin{s_}', win[s_], ud[chn * 128:(chn + 1) * 128, t0:t0 + W_], reads=['u'], writes=[f'win{s_}'])
                        conv3(tile_, tn, win[s_], f'win{s_}', None, wts=[V(f'fc{l}_{k_}', chn) for k_ in range(3)])
                    kb.op('act', lambda e: e.activation(out=at[i_], in_=at[i_], func=AF.Gelu), reads=[f'at{i_}'], writes=[f'at{i_}'])
                    tt(gT[:, j, :], at[i_], lt_[i_], ALU.mult, [f'at{i_}', f'lt{i_}'], ['gT'], 'pool')

                def evac_d(mi, msz, ps, psname):
                    i2 = mi % 2
                    tt(at[i2], ps, maskt, ALU.mult, [psname, 'maskt'], [f'at{i2}'])
                    tt(xall_[:, mi, :], xall_[:, mi, :], at[i2], ALU.add, [f'at{i2}', 'xall'], ['xall'], 'pool')
                linear(wb['ffn_down'][l], 'wb_ffn_down', 43, D, lambda kc: gT[:, kc, :], ['gT'], wt2, evac_d, G=256)
                if not last:
                    kb.dma('sxs', xview(xs)[:, :, t0:t0 + TB], xall_, reads=['xall'], writes=['xs'], q='pool')
                else:
                    rmsnorm_block(xall_, 'xall', 'fng', hfin, 'hfin', PS[0], 'ps0', sq, rstd)
                    kb.dma('sy', xview(yT)[:, :, t0:t0 + TB], hfin, reads=['hfin'], writes=['yT'], q='pool')
            kb.barrier()
        esl.close()
        xin = xout

    kb.barrier()
    es.close()
    kb.close()
    return nc


def host_inputs(inp, TP, seqs):
    common = {'vecs': pack_vecs(inp)}
    for k, v in const_mats().items():
        common['c_' + k] = v
    for nm, r, c in BIGW:
        common[nm] = np.ascontiguousarray(inp[nm], np.float32)
    common['rw_w_up'] = np.ascontiguousarray(inp['rw_w_up'].reshape(L * 2, 96, RWW), np.float32)
    common['rw_a_up'] = np.ascontiguousarray(inp['rw_a_up'].reshape(L * 2, 96, RWW), np.float32)
    common['rw_g_up'] = np.ascontiguousarray(inp['rw_g_up'], np.float32)
    common['rw_v_down'] = np.ascontiguousarray(inp['rw_v_down'][0], np.float32)
    common['rw_v_up'] = np.ascontiguousarray(inp['rw_v_up'][0], np.float32)
    maps = []
    for x, mem in seqs:
        T = x.shape[0]
        xt = np.zeros((D, TP), np.float32)
        xt[:, :T] = x.T
        mask = np.zeros((128, TP + 2), np.float32)
        mask[:, 1:1 + T] = 1.0
        m = dict(common)
        m['xT'] = xt
        m['memT'] = np.ascontiguousarray(mem.T, np.float32)
        m['mask'] = mask
        maps.append(m)
    return maps


def run(inp, debug=(), stop_after=None, ncores=8):
    xp, xs = np.asarray(inp['x_prompt']), np.asarray(inp['x_sample'])
    mp, ms = np.asarray(inp['mem_prompt']), np.asarray(inp['mem_sample'])
    seqs = [(xp[b], mp[b]) for b in range(xp.shape[0])] + [(xs[b], ms[b]) for b in range(xs.shape[0])]
    Tmax = max(s[0].shape[0] for s in seqs)
    TP = ((Tmax + TB - 1) // TB) * TB
    while len(seqs) < ncores:
        seqs.append((np.zeros((TB, D), np.float32), np.zeros((NMEM, D), np.float32)))
    inp = {k: np.asarray(v) for k, v in inp.items()}
    maps = host_inputs(inp, TP, seqs)
    nc = build(TP, debug=debug, stop_after=stop_after)
    res = run_bass_kernel_spmd(nc, maps, core_ids=list(range(ncores)))
    return res, seqs, TP


def kernel(**inputs):
    res, seqs, TP = run(inputs)
    xp, xs = inputs['x_prompt'], inputs['x_sample']
    outs = []
    i = 0
    for arr in (xp, xs):
        B, T, _ = arr.shape
        o = np.zeros((B, T, D), np.float32)
        for b in range(B):
            o[b] = res.results[i]['yT'][:, :T].T
            i += 1
        outs.append(o)
    return tuple(outs)
```

```python
import numpy as np
from contextlib import ExitStack
import concourse.bass as bass
import concourse.mybir as mybir
from concourse.bass_utils import run_bass_kernel_spmd

F32 = mybir.dt.float32
BF16 = mybir.dt.bfloat16
AF = mybir.ActivationFunctionType
ALU = mybir.AluOpType

D = 2048
L = 2
NMEM = 256
RWW = 768
RWC = 2944
HGC = 3840
XAC = 512
GC = 6144
INC = RWC + HGC + XAC + GC
DFF = 5504
TB = 512
CH = 64
DECAY_SCALE = 0.6065306597126334


class KB:
    def __init__(self, nc):
        self.nc = nc
        self.eng = {'pe': nc.tensor, 'act': nc.scalar, 'dve': nc.vector, 'pool': nc.gpsimd, 'sp': nc.sync}
        self.sem = {}
        self.cnt = {}
        self.inc = {}
        self.seen = {e: {} for e in self.eng}
        self.bufs = {}
        self._stack = []
        self.free = []
        for e in self.eng:
            self._mk(e, 1)

    def _mk(self, name, inc):
        if inc == 16 and self.free:
            self.sem[name], self.cnt[name] = self.free.pop()
        else:
            self.nsem = getattr(self, 'nsem', 0) + 1
            cm = self.nc.semaphore(f"sem{self.nsem}")
            self.sem[name] = cm.__enter__()
            self._stack.append(cm)
            self.cnt[name] = 0
        self.inc[name] = inc

    def close(self):
        for cm in reversed(self._stack):
            cm.__exit__(None, None, None)

    def _deps(self, e, reads, writes):
        waits = {}

        def need(w):
            if w is not None and w[0] != e:
                waits[w[0]] = max(waits.get(w[0], 0), w[1])
        for b in reads:
            st = self.bufs.get(b)
            if st:
                for e2, n in st['w'].items():
                    need((e2, n))
        for b in writes:
            st = self.bufs.get(b)
            if st:
                for e2, n in st['w'].items():
                    need((e2, n))
                for e2, n in st['r'].items():
                    need((e2, n))
        return waits

    def _wait(self, issuer, waits):
        for e2, n in waits.items():
            if n > self.seen[issuer].get(e2, 0):
                self.eng[issuer].wait_ge(self.sem[e2], n * self.inc[e2])
                self.seen[issuer][e2] = n

    def _mark(self, e, reads, writes):
        n = self.cnt[e]
        for b in reads:
            st = self.bufs.setdefault(b, {'w': {}, 'r': {}})
            st['r'][e] = n
        for b in writes:
            st = self.bufs.setdefault(b, {'w': {}, 'r': {}})
            st['w'][e] = n

    GLOBALS = ('c_', 'ps', 'ident_b', 'ones_b', 'vecs', 'rwsin', 'hgin', 'proj', 'rw_', 'hg_')
    ns = ''

    def _nm(self, names):
        if not self.ns:
            return list(names)
        return [b if b.startswith(self.GLOBALS) else self.ns + b for b in names]

    def op(self, e, fn, reads=(), writes=()):
        reads, writes = self._nm(reads), self._nm(writes)
        writes = list(writes) + [b for b in reads if b.startswith('ps') and b not in writes]
        self._wait(e, self._deps(e, reads, writes))
        ins = fn(self.eng[e])
        self.cnt[e] += 1
        ins.then_inc(self.sem[e], 1)
        self._mark(e, reads, writes)
        return ins

    def dma(self, stream, out, in_, reads=(), writes=(), q='sp', **kw):
        reads, writes = self._nm(reads), self._nm(writes)
        stream = self.ns + stream
        if stream not in self.sem:
            self._mk(stream, 16)
        self._wait(q, self._deps(stream, reads, writes))
        ins = self.eng[q].dma_start(out=out, in_=in_, **kw)
        self.cnt[stream] += 1
        ins.then_inc(self.sem[stream], 16)
        self._mark(stream, reads, writes)
        return ins

    def barrier(self):
        waits = {s: c for s, c in self.cnt.items() if c > 0}
        for e in self.eng:
            self._wait(e, {s: c for s, c in waits.items() if s != e})
        self.bufs = {}
        for s in [s for s in self.sem if self.inc[s] == 16]:
            self.free.append((self.sem.pop(s), self.cnt.pop(s)))
            self.inc.pop(s)
            for e in self.eng:
                self.seen[e].pop(s, None)


class VecPack:
    def __init__(self):
        self.cols = []
        self.idx = {}

    def add(self, name, arr):
        arr = np.asarray(arr, np.float32).reshape(-1)
        n = (arr.size + 127) // 128
        self.idx[name] = (len(self.cols), n)
        for c in range(n):
            col = np.zeros(128, np.float32)
            seg = arr[c * 128:(c + 1) * 128]
            col[:seg.size] = seg
            self.cols.append(col)

    def array(self):
        return np.stack(self.cols, axis=1).copy()


RW_GROUPS = ([(i * 128, 128) for i in range(18)]
             + [(2304, 96), (2400, 96), (2496, 96), (2592, 96), (2688, 128), (2816, 128)])


def vec_layout():
    vp = VecPack()
    z = np.zeros
    for l in range(L):
        vp.add(f'n1g{l}', z(D)); vp.add(f'n2g{l}', z(D)); vp.add(f'mng{l}', z(D))
        for gi, (r0, sz) in enumerate(RW_GROUPS):
            for j in range(3):
                vp.add(f'rwc{l}_{gi}_{j}', z(sz))
        for d in range(2):
            vp.add(f'w0{l}_{d}', z(RWW)); vp.add(f'a0{l}_{d}', z(RWW))
        for nm in ('kk', 'ka', 'rk', 'lnw', 'lnb', 'v0', 'lb0', 'lb1', 'hgn'):
            vp.add(f'{nm}{l}', z(RWW))
        for j in range(3):
            vp.add(f'fc{l}_{j}', z(2 * DFF))
    vp.add('fng', z(D))
    return vp


def pack_vecs(inp):
    vp = VecPack()
    for l in range(L):
        vp.add(f'n1g{l}', inp['norm1_g'][l]); vp.add(f'n2g{l}', inp['norm2_g'][l]); vp.add(f'mng{l}', inp['mem_norm_g'][l])
        for gi, (r0, sz) in enumerate(RW_GROUPS):
            for j in range(3):
                vp.add(f'rwc{l}_{gi}_{j}', inp['rw_conv'][l, j, r0:r0 + sz])
        for d in range(2):
            vp.add(f'w0{l}_{d}', inp['rw_w0'][l, d]); vp.add(f'a0{l}_{d}', inp['rw_a0'][l, d])
        vp.add(f'kk{l}', inp['rw_k_k'][l]); vp.add(f'ka{l}', inp['rw_k_a'][l]); vp.add(f'rk{l}', inp['rw_r_k'][l])
        vp.add(f'lnw{l}', inp['rw_lnx_w'][l]); vp.add(f'lnb{l}', inp['rw_lnx_b'][l])
        vp.add(f'v0{l}', inp['rw_v0'][l - 1] if l > 0 else np.zeros(RWW, np.float32))
        vp.add(f'lb0{l}', inp['hg_lb_logits'][0]); vp.add(f'lb1{l}', inp['hg_lb_logits'][l])
        vp.add(f'hgn{l}', inp['hg_norm_g'][l])
        for j in range(3):
            vp.add(f'fc{l}_{j}', inp['ffn_conv'][l, j])
    vp.add('fng', inp['final_norm_g'])
    return vp.array()


def const_mats():
    i = np.arange(128)
    h = i // 64
    t = i % 64
    same = (h[:, None] == h[None, :])
    c = {}
    c['ident'] = np.eye(128, dtype=np.float32)
    c['ones'] = np.ones((128, 128), np.float32)
    c['bones'] = same.astype(np.float32)
    c['mlt'] = (same & (t[:, None] < t[None, :])).astype(np.float32)
    c['mgt'] = (same & (t[:, None] > t[None, :])).astype(np.float32)
    t64 = np.arange(64)
    c['mle'] = (t[:, None] <= t64[None, :]).astype(np.float32)
    c['mge'] = (t[:, None] >= t64[None, :]).astype(np.float32)
    c['tle'] = (t64[:, None] <= t64[None, :]).astype(np.float32)
    c['tge'] = (t64[:, None] >= t64[None, :]).astype(np.float32)
    r = np.ones((128, TB), np.float32)
    r[:, ::CH] = 0.0
    c['rst'] = r
    return c


class RowSplit:
    def __init__(self, mk, name, R, C, step=3840):
        self.pieces = [(r0, mk(f'{name}_{r0}', [min(step, R - r0), C])) for r0 in range(0, R, step)]

    def __getitem__(self, key):
        rs, cs = key
        for base, ap in reversed(self.pieces):
            if rs.start >= base:
                assert rs.stop - base <= ap.shape[0]
                return ap[rs.start - base:rs.stop - base, cs]
        raise IndexError


BIGW = [('w_in', D, INC), ('w_mem_kv', D, 2 * XAC), ('w_branch_out', D, D), ('w_o', D, D),
        ('ffn_up', D, 2 * DFF), ('ffn_down', DFF, D)]


def build(TP, debug=(), stop_after=None):
    _uid = [0]

    def un(name):
        _uid[0] += 1
        return f's{_uid[0]}_{name}'
    NBLK = TP // TB
    nc = bass.Bass("TRN2", target_bir_lowering=False)
    VL = vec_layout()
    NV = len(VL.cols)
    dbg = set(debug)

    def din(name, shape, dt=F32):
        return nc.dram_tensor(name, list(shape), dt, kind="ExternalInput").ap()

    def dscr(name, shape, dt=F32):
        kind = "ExternalOutput" if name in dbg else "Internal"
        return nc.dram_tensor(name, list(shape), dt, kind=kind).ap()

    xT = din('xT', [D, TP])
    memT = din('memT', [D, NMEM])
    maskd = din('mask', [128, TP + 2])
    vecs_d = din('vecs', [128, NV])
    cm_d = {k: din('c_' + k, v.shape) for k, v in const_mats().items()}
    wd = {}
    for nm, r, c in BIGW:
        wd[nm] = din(nm, [L, r, c])
    w_up_d = din('rw_w_up', [L * 2, 96, RWW])
    a_up_d = din('rw_a_up', [L * 2, 96, RWW])
    g_up_d = din('rw_g_up', [L, 256, RWW])
    v_down_d = din('rw_v_down', [D, 64])
    v_up_d = din('rw_v_up', [64, RWW])
    yT = nc.dram_tensor('yT', [D, TP], F32, kind="ExternalOutput").ap()

    wb = {nm: dscr('b_' + nm, [L, r, c], BF16) for nm, r, c in BIGW}
    proj = RowSplit(dscr, 'proj', INC, TP + 2)

    kb = KB(nc)
    es = ExitStack()

    def sb(name, shape, dt=F32):
        return es.enter_context(nc.sbuf_tensor(un(name), list(shape), dt)).ap()

    PS = [es.enter_context(nc.psum_tensor(f'ps{i}', [128, 512], F32)).ap() for i in range(8)]

    vecs = sb('vecs', [128, NV])
    kb.dma('ldc', vecs, vecs_d, writes=['vecs'])
    cm = {}
    for k, d_ap in cm_d.items():
        cm[k] = sb('c_' + k, d_ap.shape)
        kb.dma('ldc', cm[k], d_ap, writes=['c_' + k])
    ones_b = sb('ones_b', [128, 128], BF16)
    ident_b = sb('ident_b', [128, 128], BF16)
    kb.op('dve', lambda e: e.tensor_copy(out=ones_b, in_=cm['ones']), reads=['c_ones'], writes=['ones_b'])
    kb.op('dve', lambda e: e.tensor_copy(out=ident_b, in_=cm['ident']), reads=['c_ident'], writes=['ident_b'])
    zt = sb('zt', [128, 2])
    kb.op('dve', lambda e: e.memset(zt, 0.0), writes=['zt'])
    kb.barrier()

    def V(name, c=0, n=1):
        c0, nn = VL.idx[name]
        return vecs[:, c0 + c:c0 + c + n]

    with ExitStack() as es0:
        CW = 4096
        stg = [es0.enter_context(nc.sbuf_tensor(un(f'p0f{i}'), [128, CW], F32)).ap() for i in range(3)]
        stb = [es0.enter_context(nc.sbuf_tensor(un(f'p0b{i}'), [128, CW], BF16)).ap() for i in range(3)]
        it = 0
        for nm, r, c in BIGW:
            tot = L * r * c
            per = tot // 128
            assert per * 128 == tot
            src = wd[nm].rearrange("l r c -> (l r c)").rearrange("(p f) -> p f", p=128)
            dst = wb[nm].rearrange("l r c -> (l r c)").rearrange("(p f) -> p f", p=128)
            o = 0
            while o < per:
                w = min(CW, per - o)
                s = it % 3
                kb.dma(f'p0l{s}', stg[s][:, :w], src[:, o:o + w], writes=[f'p0f{s}'])
                ce = ('dve', 'act', 'pool')[it % 3]
                if ce == 'act':
                    kb.op('act', lambda e: e.copy(out=stb[s][:, :w], in_=stg[s][:, :w]), reads=[f'p0f{s}'], writes=[f'p0b{s}'])
                else:
                    kb.op(ce, lambda e: e.tensor_copy(out=stb[s][:, :w], in_=stg[s][:, :w]), reads=[f'p0f{s}'], writes=[f'p0b{s}'])
                kb.dma(f'p0s{s}', dst[:, o:o + w], stb[s][:, :w], reads=[f'p0b{s}'], writes=['wb_' + nm], q='pool')
                o += w
                it += 1
        for r0 in range(0, RWC, 128):
            kb.dma('zp', proj[r0:r0 + 128, 0:1], zt[:, 0:1], reads=['zt'], writes=['proj'], q='pool', allow_slow_non_contiguous=True)
            kb.dma('zp', proj[r0:r0 + 128, TP + 1:TP + 2], zt[:, 0:1], reads=['zt'], writes=['proj'], q='pool', allow_slow_non_contiguous=True)
        kb.barrier()

    def rsqrt_(ap, name):
        kb.op('act', lambda e: e.activation(out=ap, in_=ap, func=AF.Sqrt), reads=[name], writes=[name])
        kb.op('dve', lambda e: e.reciprocal(out=ap, in_=ap), reads=[name], writes=[name])

    def rmsnorm_block(xall, xname, gname, hT, hname, ps, psname, sq, rstd, N=TB):
        for c in range(16):
            s = c % 2
            kb.op('act', lambda e: e.activation(out=sq[s][:, :N], in_=xall[:, c, :N], func=AF.Square),
                  reads=[xname], writes=[f'sq{s}'])
            kb.op('pe', lambda e: e.matmul(ps[:, :N], lhsT=ones_b, rhs=sq[s][:, :N], start=(c == 0), stop=(c == 15)),
                  reads=[f'sq{s}', 'ones_b'], writes=[psname])
        kb.op('dve', lambda e: e.tensor_scalar(out=rstd[:, :N], in0=ps[:, :N], scalar1=1.0 / D, scalar2=1e-6, op0=ALU.mult, op1=ALU.add),
              reads=[psname], writes=['rstd'])
        rsqrt_(rstd[:, :N], 'rstd')
        for c in range(16):
            kb.op('dve', lambda e: e.scalar_tensor_tensor(out=hT[:, c, :N], in0=xall[:, c, :N], scalar=V(gname, c), in1=rstd[:, :N],
                                                          op0=ALU.mult, op1=ALU.mult),
                  reads=[xname, 'rstd', 'vecs'], writes=[hname])

    lin_state = {'g': 0, 'ps': 0}

    def linear(Wb2d, wname, KC, M, rhs_fn, rhs_names, wtiles, evac, N=TB, G=512):
        Wv = Wb2d.rearrange("(kc p) m -> p kc m", p=128)
        for g0 in range(0, M, G):
            gw = min(G, M - g0)
            s = lin_state['g'] % len(wtiles)
            lin_state['g'] += 1
            wt = wtiles[s]
            kb.dma(f'lw{s}', wt[:, :KC, :gw], Wv[:, :, g0:g0 + gw], reads=[wname], writes=[f'wt{s}'])
            for j0 in range(0, gw, 128):
                msz = min(128, gw - j0)
                pi = 4 + lin_state['ps'] % 4
                lin_state['ps'] += 1
                ps = PS[pi]
                for kc in range(KC):
                    kb.op('pe', lambda e: e.matmul(ps[:msz, :N], lhsT=wt[:, kc, j0:j0 + msz], rhs=rhs_fn(kc),
                                                   start=(kc == 0), stop=(kc == KC - 1)),
                          reads=[f'wt{s}'] + rhs_names, writes=[f'ps{pi}'])
                evac((g0 + j0) // 128, msz, ps, f'ps{pi}')


    RWN = ['r', 'v', 'kk', 'lw0', 'lw1', 'ka0', 'ka1', 'kd0', 'kd1', 'bonus', 'g', 'y0', 'y1']
    rws = {n: dscr('rw_' + n, [RWW, TP]) for n in RWN}
    vfirst = dscr('vfirst', [RWW, TP])
    vdd = dscr('vd', [64, TP])
    HGN = ['q', 'lf0', 'lf1', 'kf0', 'kf1', 'o0', 'o1']
    hgs = {n: dscr('hg_' + n, [RWW, TP]) for n in HGN}
    xm = dscr('xm', [D, TP])
    xs = dscr('xs', [D, TP])
    ud = RowSplit(dscr, 'u', 2 * DFF, TP + 2)
    for r0 in range(0, 2 * DFF, 128):
        kb.dma('zp', ud[r0:r0 + 128, 0:1], zt[:, 0:1], reads=['zt'], writes=['u'], q='pool', allow_slow_non_contiguous=True)
        kb.dma('zp', ud[r0:r0 + 128, TP + 1:TP + 2], zt[:, 0:1], reads=['zt'], writes=['u'], q='pool', allow_slow_non_contiguous=True)

    def xview(ap):
        return ap.rearrange("(c p) t -> p c t", p=128)

    psr = {'n': 0}

    def pr(w=128):
        i = psr['n'] % 8
        psr['n'] += 1
        return PS[i][:, :w], f'ps{i}'

    cp_rr = {'n': 0}

    def cpy(out, in_, reads, writes, eng=None):
        if eng is None:
            eng = ('dve', 'act')[cp_rr['n'] % 2]
            cp_rr['n'] += 1
        if eng == 'act':
            kb.op('act', lambda e: e.copy(out=out, in_=in_), reads=reads, writes=writes)
        else:
            kb.op(eng, lambda e: e.tensor_copy(out=out, in_=in_), reads=reads, writes=writes)

    def tt(out, a, b, op, reads, writes, eng='dve'):
        kb.op(eng, lambda e: e.tensor_tensor(out=out, in0=a, in1=b, op=op), reads=reads, writes=writes)

    def conv3(out, oname, win, wname_, vname, N=TB, wts=None):
        rows = out.shape[0]
        if wts is not None:
            w0_, w1_, w2_ = wts
            kb.op('act', lambda e: e.activation(out=out, in_=win[:rows, 1:N + 1], func=AF.Copy, scale=w1_[:rows]),
                  reads=[wname_, 'vecs'], writes=[oname])
            kb.op('dve', lambda e: e.scalar_tensor_tensor(out=out, in0=win[:rows, 0:N], scalar=w0_[:rows], in1=out,
                                                          op0=ALU.mult, op1=ALU.add), reads=[wname_, oname, 'vecs'], writes=[oname])
            kb.op('dve', lambda e: e.scalar_tensor_tensor(out=out, in0=win[:rows, 2:N + 2], scalar=w2_[:rows], in1=out,
                                                          op0=ALU.mult, op1=ALU.add), reads=[wname_, oname, 'vecs'], writes=[oname])
            return
        kb.op('act', lambda e: e.activation(out=out, in_=win[:rows, 1:N + 1], func=AF.Copy, scale=V(vname + '_1')[:rows]),
              reads=[wname_, 'vecs'], writes=[oname])
        kb.op('dve', lambda e: e.scalar_tensor_tensor(out=out, in0=win[:rows, 0:N], scalar=V(vname + '_0')[:rows], in1=out,
                                                      op0=ALU.mult, op1=ALU.add), reads=[wname_, oname, 'vecs'], writes=[oname])
        kb.op('dve', lambda e: e.scalar_tensor_tensor(out=out, in0=win[:rows, 2:N + 2], scalar=V(vname + '_2')[:rows], in1=out,
                                                      op0=ALU.mult, op1=ALU.add), reads=[wname_, oname, 'vecs'], writes=[oname])

    def drive(gens):
        act = list(enumerate(gens))
        while act:
            for k, g in list(act):
                kb.ns = f'{k}:'
                try:
                    next(g)
                except StopIteration:
                    act.remove((k, g))
        kb.ns = ''

    xin = xT
    for l in range(L):
        last = (l == L - 1)
        xout = xs
        esl = ExitStack()

        def sbl(name, shape, dt=F32):
            return esl.enter_context(nc.sbuf_tensor(un(name), list(shape), dt)).ap()
        wup_b = sbl('wup_b', [96, 2, RWW], BF16)
        aup_b = sbl('aup_b', [96, 2, RWW], BF16)
        gup_b = sbl('gup_b', [128, 2, RWW], BF16)
        vup_b = sbl('vup_b', [64, RWW], BF16)
        vdn_b = sbl('vdn_b', [128, 16, 64], BF16)
        lbv = sbl('lbv', [128, 6])
        omlb = sbl('omlb', [128, 6])
        omka = sbl('omka', [128, 6])
        mkT_b = sbl('mkT_b', [128, 4, NMEM], BF16)
        mv_b = sbl('mv_b', [128, 2, XAC], BF16)
        with ExitStack() as e0:
            tmpf = e0.enter_context(nc.sbuf_tensor(un('tmpf'), [128, 2 * RWW], F32)).ap()
            memx = e0.enter_context(nc.sbuf_tensor(un('memx'), [128, 16, NMEM], F32)).ap()
            memh = e0.enter_context(nc.sbuf_tensor(un('memh'), [128, 16, NMEM], BF16)).ap()
            sqm = [e0.enter_context(nc.sbuf_tensor(un(f'sqm{i}'), [128, TB], BF16)).ap() for i in range(2)]
            rstdm = e0.enter_context(nc.sbuf_tensor(un('rstdm'), [128, TB], F32)).ap()
            wkv = e0.enter_context(nc.sbuf_tensor(un('wkv'), [128, 16, 2 * XAC], BF16)).ap()
            for d in range(2):
                kb.dma('lsm', tmpf[:96, :RWW], w_up_d[l * 2 + d], writes=['tmpf'])
                cpy(wup_b[:, d, :], tmpf[:96, :RWW], ['tmpf'], ['wup_b'], 'dve')
                kb.dma('lsm', tmpf[:96, :RWW], a_up_d[l * 2 + d], writes=['tmpf'])
                cpy(aup_b[:, d, :], tmpf[:96, :RWW], ['tmpf'], ['aup_b'], 'dve')
            kb.dma('lsm', tmpf.rearrange("p (a c) -> p a c", a=2), g_up_d[l].rearrange("(a p) c -> p a c", p=128), writes=['tmpf'])
            cpy(gup_b, tmpf.rearrange("p (a c) -> p a c", a=2), ['tmpf'], ['gup_b'], 'dve')
            kb.dma('lsm', tmpf[:64, :RWW], v_up_d, writes=['tmpf'])
            cpy(vup_b, tmpf[:64, :RWW], ['tmpf'], ['vup_b'], 'dve')
            kb.dma('lsm', tmpf[:, :1024].rearrange("p (c m) -> p c m", c=16), v_down_d.rearrange("(c p) m -> p c m", p=128), writes=['tmpf'])
            cpy(vdn_b, tmpf[:, :1024].rearrange("p (c m) -> p c m", c=16), ['tmpf'], ['vdn_b'], 'dve')
            if l == 0:
                kb.op('dve', lambda e: e.memset(lbv, 0.0), writes=['lbv'])
            else:
                tt(lbv, V(f'lb1{l}', 0, 6), V(f'lb0{l}', 0, 6), ALU.subtract, ['vecs'], ['lbv'])
                kb.op('act', lambda e: e.activation(out=lbv, in_=lbv, func=AF.Sigmoid), reads=['lbv'], writes=['lbv'])
            kb.op('dve', lambda e: e.tensor_scalar(out=omlb, in0=lbv, scalar1=-1.0, scalar2=1.0, op0=ALU.mult, op1=ALU.add),
                  reads=['lbv'], writes=['omlb'])
            kb.op('dve', lambda e: e.tensor_scalar(out=omka, in0=V(f'ka{l}', 0, 6), scalar1=-1.0, scalar2=1.0, op0=ALU.mult, op1=ALU.add),
                  reads=['vecs'], writes=['omka'])
            kb.dma('lmem', memx, xview(memT), writes=['memx'])
            rmsnorm_block(memx, 'memx', f'mng{l}', memh, 'memh', PS[0], 'ps0', sqm, rstdm, N=NMEM)
            kb.dma('lsm2', wkv, wb['w_mem_kv'][l].rearrange("(kc p) m -> p kc m", p=128), reads=['wb_w_mem_kv'], writes=['wkv'])
            for hd in range(4):
                ps, pn = pr()
                ps = PS[1][:, :NMEM]
                for kc in range(16):
                    kb.op('pe', lambda e: e.matmul(ps, lhsT=wkv[:, kc, hd * 128:(hd + 1) * 128], rhs=memh[:, kc, :],
                                                   start=(kc == 0), stop=(kc == 15)), reads=['wkv', 'memh'], writes=['ps1'])
                cpy(mkT_b[:, hd, :], ps, ['ps1'], ['mkT_b'])
            for mc in range(2):
                ps = PS[2]
                for kc in range(16):
                    kb.op('pe', lambda e: e.matmul(ps, lhsT=memh[:, kc, mc * 128:(mc + 1) * 128], rhs=wkv[:, kc, XAC:2 * XAC],
                                                   start=(kc == 0), stop=(kc == 15)), reads=['wkv', 'memh'], writes=['ps2'])
                cpy(mv_b[:, mc, :], ps, ['ps2'], ['mv_b'])
            kb.barrier()

        with ExitStack() as e1:
            def sb1(name, shape, dt=F32):
                return e1.enter_context(nc.sbuf_tensor(un(name), list(shape), dt)).ap()
            xall = [sb1(f'xall{i}', [128, 16, TB]) for i in range(2)]
            hT = sb1('hT', [128, 16, TB], BF16)
            sq = [sb1(f'sq{i}', [128, TB], BF16) for i in range(2)]
            rstd = sb1('rstd', [128, TB])
            wtiles = [sb1(f'wt{i}', [128, 16, 512], BF16) for i in range(2)]
            stage = [sb1(f'stage{i}', [128, TB]) for i in range(4)]
            ev = {'n': 0}
            for blk in range(NBLK):
                t0 = blk * TB
                xsl = blk % 2
                kb.dma(f'lx{xsl}', xall[xsl], xview(xin)[:, :, t0:t0 + TB], reads=['xin'], writes=[f'xall{xsl}'])
                rmsnorm_block(xall[xsl], f'xall{xsl}', f'n1g{l}', hT, 'hT', PS[0], 'ps0', sq, rstd)

                def evac(mi, msz, ps, psname):
                    s_ = ev['n'] % 4
                    ev['n'] += 1
                    if mi >= (RWC + HGC + XAC) // 128:
                        kb.op('act', lambda e: e.activation(out=stage[s_][:msz], in_=ps[:msz], func=AF.Sigmoid),
                              reads=[psname], writes=[f'stage{s_}'])
                    else:
                        cpy(stage[s_][:msz], ps[:msz], [psname], [f'stage{s_}'])
                    kb.dma(f'st{s_}', proj[mi * 128:mi * 128 + msz, 1 + t0:1 + t0 + TB], stage[s_][:msz],
                           reads=[f'stage{s_}'], writes=['proj'], q='pool')
                linear(wb['w_in'][l], 'wb_w_in', 16, INC, lambda kc: hT[:, kc, :], ['hT'], wtiles, evac)
                if l > 0:
                    ps = PS[1]
                    for kc in range(16):
                        kb.op('pe', lambda e: e.matmul(ps[:64], lhsT=vdn_b[:, kc, :], rhs=hT[:, kc, :], start=(kc == 0), stop=(kc == 15)),
                              reads=['vdn_b', 'hT'], writes=['ps1'])
                    s_ = ev['n'] % 4
                    ev['n'] += 1
                    cpy(stage[s_][:64], ps[:64], ['ps1'], [f'stage{s_}'])
                    kb.dma(f'st{s_}', vdd[:, t0:t0 + TB], stage[s_][:64], reads=[f'stage{s_}'], writes=['vd'], q='pool')
            kb.barrier()
        if stop_after == f'P1_{l}':
            esl.close()
            break

        with ExitStack() as e2:
            def sb2(name, shape, dt=F32):
                return e2.enter_context(nc.sbuf_tensor(un(name), list(shape), dt)).ap()
            W_ = TB + 2
            win = [sb2(f'win{i}', [128, W_]) for i in range(4)]
            wn = {'n': 0}

            def loadwin(row0, rows, t0):
                s_ = wn['n'] % 4
                wn['n'] += 1
                kb.dma(f'lwin{s_}', win[s_][:rows], proj[row0:row0 + rows, t0:t0 + W_], reads=['proj'], writes=[f'win{s_}'])
                return win[s_], f'win{s_}'
            cvt = sb2('cvt', [128, TB])
            twd = sb2('twd', [96, 2, TB], BF16)
            tad = sb2('tad', [96, 2, TB], BF16)
            sgd = sb2('sgd', [128, 2, TB], BF16)
            vdf = sb2('vdf', [64, TB])
            vdb_ = sb2('vdb', [64, TB], BF16)
            maskt = sb2('maskt', [128, TB])
            names = ['r', 'k', 'v', 'kk', 'lw0', 'lw1', 'a0', 'a1', 'ka0', 'ka1', 'kd0', 'kd1', 'bonus', 'g', 't1', 't2', 'vf', 'sg']
            T2 = [{n: sb2(f'{n}_{i}', [128, TB]) for n in names} for i in range(2)]
            hgt = [{n: sb2(f'h{n}_{i}', [128, TB]) for n in ['q', 'fz0', 'fz1', 'f0', 'f1', 'lf0', 'lf1', 'kf0', 'kf1']} for i in range(2)]
            for blk in range(NBLK):
                t0 = blk * TB
                kb.dma('lmask', maskt, maskd[:, 1 + t0:1 + t0 + TB], writes=['maskt'])
                for d in range(2):
                    w_, wn_ = loadwin(2304 + 96 * d, 96, t0)
                    conv3(cvt[:96], 'cvt', w_, wn_, f'rwc{l}_{18 + d}')
                    kb.op('act', lambda e: e.activation(out=twd[:, d, :], in_=cvt[:96], func=AF.Tanh), reads=['cvt'], writes=['twd'])
                    w_, wn_ = loadwin(2496 + 96 * d, 96, t0)
                    conv3(cvt[:96], 'cvt', w_, wn_, f'rwc{l}_{20 + d}')
                    cpy(tad[:, d, :], cvt[:96], ['cvt'], ['tad'], 'dve')
                for a in range(2):
                    w_, wn_ = loadwin(2688 + 128 * a, 128, t0)
                    conv3(cvt, 'cvt', w_, wn_, f'rwc{l}_{22 + a}')
                    kb.op('act', lambda e: e.activation(out=sgd[:, a, :], in_=cvt, func=AF.Sigmoid), reads=['cvt'], writes=['sgd'])
                if l > 0:
                    kb.dma('lvd', vdf, vdd[:, t0:t0 + TB], reads=['vd'], writes=['vdf'])
                    cpy(vdb_, vdf, ['vdf'], ['vdb'], 'dve')
                for c in range(6):
                    i_ = c % 2
                    Tt = T2[i_]

                    def nm(n):
                        return f'{n}_{i_}'
                    cs = slice(c * 128, (c + 1) * 128)
                    for gi, n in ((c, 'r'), (6 + c, 'k'), (12 + c, 'v')):
                        w_, wn_ = loadwin(gi * 128, 128, t0)
                        conv3(Tt[n], nm(n), w_, wn_, f'rwc{l}_{gi}')
                    tt(Tt['k'], Tt['k'], maskt, ALU.mult, [nm('k'), 'maskt'], [nm('k')])
                    tt(Tt['v'], Tt['v'], maskt, ALU.mult, [nm('v'), 'maskt'], [nm('v')])
                    if l == 0:
                        kb.dma(f'sv{i_}', vfirst[cs, t0:t0 + TB], Tt['v'], reads=[nm('v')], writes=['vfirst'], q='pool')
                    else:
                        ps, pn = PS[1], 'ps1'
                        kb.op('pe', lambda e: e.matmul(ps, lhsT=vup_b[:, cs], rhs=vdb_, start=True, stop=True), reads=['vup_b', 'vdb'], writes=[pn])
                        kb.op('act', lambda e: e.activation(out=Tt['sg'], in_=ps, func=AF.Sigmoid, bias=V(f'v0{l}', c)), reads=[pn, 'vecs'], writes=[nm('sg')])
                        kb.dma(f'lvf{i_}', Tt['vf'], vfirst[cs, t0:t0 + TB], reads=['vfirst'], writes=[nm('vf')])
                        tt(Tt['vf'], Tt['vf'], Tt['v'], ALU.subtract, [nm('vf'), nm('v')], [nm('vf')])
                        tt(Tt['vf'], Tt['vf'], Tt['sg'], ALU.mult, [nm('vf'), nm('sg')], [nm('vf')])
                        tt(Tt['v'], Tt['v'], Tt['vf'], ALU.add, [nm('vf'), nm('v')], [nm('v')])
                    for d in range(2):
                        ps, pn = PS[2], 'ps2'
                        kb.op('pe', lambda e: e.matmul(ps, lhsT=wup_b[:, d, cs], rhs=twd[:, d, :], start=True, stop=True), reads=['wup_b', 'twd'], writes=[pn])
                        kb.op('act', lambda e: e.activation(out=Tt[f'lw{d}'], in_=ps, func=AF.Sigmoid, bias=V(f'w0{l}_{d}', c)), reads=[pn, 'vecs'], writes=[nm(f'lw{d}')])
                        kb.op('dve', lambda e: e.tensor_scalar_mul(out=Tt[f'lw{d}'], in0=Tt[f'lw{d}'], scalar1=-DECAY_SCALE), reads=[nm(f'lw{d}')], writes=[nm(f'lw{d}')])
                        ps, pn = PS[3], 'ps3'
                        kb.op('pe', lambda e: e.matmul(ps, lhsT=aup_b[:, d, cs], rhs=tad[:, d, :], start=True, stop=True), reads=['aup_b', 'tad'], writes=[pn])
                        kb.op('act', lambda e: e.activation(out=Tt[f'a{d}'], in_=ps, func=AF.Sigmoid, bias=V(f'a0{l}_{d}', c)), reads=[pn, 'vecs'], writes=[nm(f'a{d}')])
                        kb.op('dve', lambda e: e.tensor_scalar(out=Tt['t1'], in0=Tt[f'a{d}'], scalar1=V(f'ka{l}', c), scalar2=omka[:, c:c + 1], op0=ALU.mult, op1=ALU.add),
                              reads=[nm(f'a{d}'), 'vecs', 'omka'], writes=[nm('t1')])
                        tt(Tt[f'kd{d}'], Tt['t1'], Tt['k'], ALU.mult, [nm('t1'), nm('k')], [nm(f'kd{d}')])
                    kb.op('dve', lambda e: e.tensor_scalar_mul(out=Tt['kk'], in0=Tt['k'], scalar1=V(f'kk{l}', c)), reads=[nm('k'), 'vecs'], writes=[nm('kk')])
                    tt(Tt['t1'], Tt['kk'], Tt['kk'], ALU.mult, [nm('kk')], [nm('t1')])
                    ps, pn = PS[1], 'ps1'
                    kb.op('pe', lambda e: e.matmul(ps, lhsT=cm['bones'], rhs=Tt['t1'], start=True, stop=True), reads=['c_bones', nm('t1')], writes=[pn])
                    kb.op('dve', lambda e: e.tensor_scalar_max(out=Tt['t2'], in0=ps, scalar1=1e-24), reads=[pn], writes=[nm('t2')])
                    rsqrt_(Tt['t2'], nm('t2'))
                    tt(Tt['kk'], Tt['kk'], Tt['t2'], ALU.mult, [nm('kk'), nm('t2')], [nm('kk')])
                    for d in range(2):
                        tt(Tt[f'ka{d}'], Tt['kk'], Tt[f'a{d}'], ALU.mult, [nm('kk'), nm(f'a{d}')], [nm(f'ka{d}')])
                    tt(Tt['t1'], Tt['kd0'], Tt['kd1'], ALU.add, [nm('kd0'), nm('kd1')], [nm('t1')])
                    tt(Tt['t1'], Tt['t1'], Tt['r'], ALU.mult, [nm('t1'), nm('r')], [nm('t1')])
                    kb.op('dve', lambda e: e.tensor_scalar_mul(out=Tt['t1'], in0=Tt['t1'], scalar1=V(f'rk{l}', c)), reads=[nm('t1'), 'vecs'], writes=[nm('t1')])
                    ps, pn = PS[2], 'ps2'
                    kb.op('pe', lambda e: e.matmul(ps, lhsT=cm['bones'], rhs=Tt['t1'], start=True, stop=True), reads=['c_bones', nm('t1')], writes=[pn])
                    tt(Tt['bonus'], ps, Tt['v'], ALU.mult, [pn, nm('v')], [nm('bonus')])
                    ps, pn = PS[3], 'ps3'
                    for a in range(2):
                        kb.op('pe', lambda e: e.matmul(ps, lhsT=gup_b[:, a, cs], rhs=sgd[:, a, :], start=(a == 0), stop=(a == 1)), reads=['gup_b', 'sgd'], writes=[pn])
                    cpy(Tt['g'], ps, [pn], [nm('g')], 'act')
                    for n in ('r', 'v', 'kk', 'lw0', 'lw1', 'ka0', 'ka1', 'kd0', 'kd1', 'bonus', 'g'):
                        kb.dma(f's2_{n}{i_}', rws[n][cs, t0:t0 + TB], Tt[n], reads=[nm(n)], writes=['rw_' + n], q='pool')
                    Ht = hgt[i_]

                    def hn(n):
                        return f'h{n}_{i_}'
                    hb = RWC + c * 128
                    kb.dma(f'lh0{i_}', Ht['q'], proj[hb:hb + 128, 1 + t0:1 + t0 + TB], reads=['proj'], writes=[hn('q')])
                    kb.op('act', lambda e: e.activation(out=Ht['q'], in_=Ht['q'], func=AF.Silu), reads=[hn('q')], writes=[hn('q')])
                    kb.dma(f's2_hq{i_}', hgs['q'][cs, t0:t0 + TB], Ht['q'], reads=[hn('q')], writes=['hg_q'], q='pool')
                    for d in range(2):
                        fb = RWC + RWW * (1 + d) + c * 128
                        kb.dma(f'lh{1 + d}{i_}', Ht[f'fz{d}'], proj[fb:fb + 128, 1 + t0:1 + t0 + TB], reads=['proj'], writes=[hn(f'fz{d}')])
                        kb.op('act', lambda e: e.activation(out=Ht[f'f{d}'], in_=Ht[f'fz{d}'], func=AF.Exp, scale=-1.0), reads=[hn(f'fz{d}')], writes=[hn(f'f{d}')])
                        kb.op('dve', lambda e: e.tensor_scalar_add(out=Ht[f'f{d}'], in0=Ht[f'f{d}'], scalar1=1.0), reads=[hn(f'f{d}')], writes=[hn(f'f{d}')])
                        kb.op('dve', lambda e: e.reciprocal(out=Ht[f'f{d}'], in_=Ht[f'f{d}']), reads=[hn(f'f{d}')], writes=[hn(f'f{d}')])
                        kb.op('dve', lambda e: e.tensor_scalar(out=Ht[f'f{d}'], in0=Ht[f'f{d}'], scalar1=omlb[:, c:c + 1], scalar2=lbv[:, c:c + 1], op0=ALU.mult, op1=ALU.add),
                              reads=[hn(f'f{d}'), 'omlb', 'lbv'], writes=[hn(f'f{d}')])
                        kb.op('act', lambda e: e.activation(out=Ht[f'lf{d}'], in_=Ht[f'f{d}'], func=AF.Ln), reads=[hn(f'f{d}')], writes=[hn(f'lf{d}')])
                        kb.op('dve', lambda e: e.tensor_scalar(out=Ht[f'kf{d}'], in0=Ht[f'f{d}'], scalar1=-1.0, scalar2=1.0, op0=ALU.mult, op1=ALU.add),
                              reads=[hn(f'f{d}')], writes=[hn(f'kf{d}')])
                        kb.dma(f's2_hlf{d}{i_}', hgs[f'lf{d}'][cs, t0:t0 + TB], Ht[f'lf{d}'], reads=[hn(f'lf{d}')], writes=[f'hg_lf{d}'], q='pool')
                        kb.dma(f's2_hkf{d}{i_}', hgs[f'kf{d}'][cs, t0:t0 + TB], Ht[f'kf{d}'], reads=[hn(f'kf{d}')], writes=[f'hg_kf{d}'], q='pool')
            kb.barrier()
        if stop_after == f'P2a_{l}':
            esl.close()
            break

        def v3(ap):
            return ap.rearrange("p (c t) -> p c t", t=CH)
        NCH = TB // CH
        with ExitStack() as e3:
            def sb3(name, shape, dt=F32):
                return e3.enter_context(nc.sbuf_tensor(un(name), list(shape), dt)).ap()
            def rw_factory(k):
                kb.ns = f'{k}:'
                LN_ = ['lw', 'ka', 'kd', 'kk', 'r', 'v']
                ld = [{n: sb3(f'l{n}_{i}', [128, TB]) for n in LN_} for i in range(1)]
                bcum = sb3('bcum', [128, TB]); bcur_t = sb3('bcur', [128, TB]); excl = sb3('excl', [128, TB])
                eex = sb3('eex', [128, TB]); enb = sb3('enb', [128, TB])
                ebs = [sb3(f'eb{i}', [128, TB]) for i in range(1)]
                EX = {n: [sb3(f'E{n}_{i}', [128, NCH, 2, CH], BF16) for i in range(1)] for n in 'abkrv'}
                rbar = [sb3(f'rbar{i}', [128, TB], BF16) for i in range(1)]
                ybuf = [sb3(f'ybuf{i}', [128, TB]) for i in range(1)]
                S = sb3('S', [128, 128]); Sb = sb3('Sb', [128, 128], BF16)
                Pm = [sb3(f'Pm{i}', [128, 128]) for i in range(2)]
                PTm = [sb3(f'PTm{i}', [128, 128]) for i in range(2)]
                TTm = [sb3(f'TTm{i}', [128, 128]) for i in range(2)]
                AakT = sb3('AakT', [128, 128], BF16); ArbT = sb3('ArbT', [128, CH], BF16); ArkT = sb3('ArkT', [128, CH], BF16)
                Btok = sb3('Btok', [128, 128], BF16); Ktok = sb3('Ktok', [128, 128], BF16); Vbd = sb3('Vbd', [128, 128], BF16)
                Xf = sb3('Xf', [128, 128]); Ub = sb3('Ub', [128, 128], BF16)
                for n in 'abkrv':
                    for i in range(1):
                        kb.op('pool', lambda e: e.memset(EX[n][i].rearrange("p c h t -> p (c h t)"), 0.0), writes=[f'E{n}_{i}'])
                kb.ns = ''
                bi = 0

                def chain(c, d):
                    nonlocal bi
                    cs = slice(c * 128, (c + 1) * 128)
                    kb.op('dve', lambda e: e.memset(S, 0.0), reads=[], writes=['S'])
                    kb.op('dve', lambda e: e.memset(Sb, 0.0), reads=[], writes=['Sb'])
                    mAT, mA, mInc = (cm['mlt'], cm['mgt'], cm['mle']) if d == 0 else (cm['mgt'], cm['mlt'], cm['mge'])
                    mATn, mAn, mIncn = ('c_mlt', 'c_mgt', 'c_mle') if d == 0 else ('c_mgt', 'c_mlt', 'c_mge')
                    blocks = list(range(NBLK)) if d == 0 else list(reversed(range(NBLK)))
                    for blk in blocks:
                        i_ = 0
                        bi += 1
                        t0 = blk * TB
                        Lt = ld[i_]
                        for n, src in (('lw', rws[f'lw{d}']), ('ka', rws[f'ka{d}']), ('kd', rws[f'kd{d}']), ('kk', rws['kk']), ('r', rws['r']), ('v', rws['v'])):
                            kb.dma(f'l3{n}{i_}', Lt[n], src[cs, t0:t0 + TB], reads=['rwsin'], writes=[f'l{n}_{i_}'])

                        def ln(n):
                            return f'l{n}_{i_}'
                        kb.op('dve', lambda e: e.tensor_tensor_scan(out=bcum, data0=cm['rst'], data1=Lt['lw'], initial=0.0, op0=ALU.mult, op1=ALU.add),
                              reads=['c_rst', ln('lw')], writes=['bcum'])
                        if d == 0:
                            bcur, bcn = bcum, 'bcum'
                            tt(excl, bcum, Lt['lw'], ALU.subtract, ['bcum', ln('lw')], ['excl'])
                            yield
                        else:
                            tot = v3(bcum)[:, :, CH - 1:CH].broadcast_to([128, NCH, CH])
                            tt(v3(excl), tot, v3(bcum), ALU.subtract, ['bcum'], ['excl'])
                            yield
                            tt(bcur_t, excl, Lt['lw'], ALU.add, ['excl', ln('lw')], ['bcur'])
                            yield
                            bcur, bcn = bcur_t, 'bcur'
                        eb = ebs[i_]
                        kb.op('act', lambda e: e.activation(out=eex, in_=excl, func=AF.Exp), reads=['excl'], writes=['eex'])
                        kb.op('act', lambda e: e.activation(out=eb, in_=bcur, func=AF.Exp), reads=[bcn], writes=[f'eb{i_}'])
                        kb.op('act', lambda e: e.activation(out=enb, in_=bcur, func=AF.Exp, scale=-1.0), reads=[bcn], writes=['enb'])
                        for hf in range(2):
                            rs = slice(64 * hf, 64 * hf + 64)
                            kb.op('dve', lambda e: e.scalar_tensor_tensor(out=EX['a'][i_][rs, :, hf, :], in0=v3(Lt['kk'])[rs], scalar=-1.0, in1=v3(eex)[rs],
                                                                          op0=ALU.mult, op1=ALU.mult), reads=[ln('kk'), 'eex'], writes=[f'Ea_{i_}'])
                            tt(EX['r'][i_][rs, :, hf, :], v3(Lt['r'])[rs], v3(eb)[rs], ALU.mult, [ln('r'), f'eb{i_}'], [f'Er_{i_}'], 'pool')
                            yield
                            tt(EX['b'][i_][rs, :, hf, :], v3(Lt['ka'])[rs], v3(enb)[rs], ALU.mult, [ln('ka'), 'enb'], [f'Eb_{i_}'])
                            yield
                            tt(EX['k'][i_][rs, :, hf, :], v3(Lt['kd'])[rs], v3(enb)[rs], ALU.mult, [ln('kd'), 'enb'], [f'Ek_{i_}'], 'pool')
                            yield
                            kb.op('act', lambda e: e.copy(out=EX['v'][i_][rs, :, hf, :], in_=v3(Lt['v'])[rs]), reads=[ln('v')], writes=[f'Ev_{i_}'])
                        tt(rbar[i_], Lt['r'], eb, ALU.mult, [ln('r'), f'eb{i_}'], [f'rbar{i_}'], 'pool')
                        yield
                        chunks = list(range(NCH)) if d == 0 else list(reversed(range(NCH)))
                        for ch in chunks:
                            Ea, Eb, Ek, Er, Ev = (EX[n][i_][:, ch].rearrange("p h t -> p (h t)") for n in 'abkrv')
                            En = {n: f'E{n}_{i_}' for n in 'abkrv'}
                            rb = rbar[i_][:, ch * CH:(ch + 1) * CH]
                            ec = eb[:, ch * CH + CH - 1:ch * CH + CH] if d == 0 else eb[:, ch * CH:ch * CH + 1]
                            ps, pn = pr()
                            kb.op('pe', lambda e: e.matmul(ps, lhsT=Eb, rhs=Ea, start=True, stop=True), reads=[En['b'], En['a']], writes=[pn])
                            tt(PTm[0], ps, mAT, ALU.mult, [pn, mATn], ['PTm0'])
                            yield
                            ps, pn = pr()
                            kb.op('pe', lambda e: e.matmul(ps, lhsT=Ea, rhs=Eb, start=True, stop=True), reads=[En['b'], En['a']], writes=[pn])
                            tt(Pm[0], ps, mA, ALU.mult, [pn, mAn], ['Pm0'])
                            yield
                            tt(TTm[0], PTm[0], cm['ident'], ALU.add, ['PTm0', 'c_ident'], ['TTm0'], 'pool')
                            yield
                            g_ = 0
                            for j in range(5):
                                g2 = 1 - g_
                                ps, pn = pr()
                                kb.op('pe', lambda e: e.matmul(ps, lhsT=PTm[g_], rhs=Pm[g_], start=True, stop=True), reads=[f'PTm{g_}', f'Pm{g_}'], writes=[pn])
                                cpy(Pm[g2], ps, [pn], [f'Pm{g2}'])
                                yield
                                if j < 4:
                                    ps2, pn2 = pr()
                                    kb.op('pe', lambda e: e.matmul(ps2, lhsT=Pm[g_], rhs=PTm[g_], start=True, stop=True), reads=[f'PTm{g_}', f'Pm{g_}'], writes=[pn2])
                                    cpy(PTm[g2], ps2, [pn2], [f'PTm{g2}'])
                                    yield
                                ps3, pn3 = pr()
                                kb.op('pe', lambda e: e.matmul(ps3, lhsT=Pm[g2], rhs=TTm[g_], start=True, stop=True), reads=[f'Pm{g2}', f'TTm{g_}'], writes=[pn3])
                                tt(TTm[g2], ps3, TTm[g_], ALU.add, [pn3, f'TTm{g_}'], [f'TTm{g2}'])
                                yield
                                g_ = g2
                            TT, TTn = TTm[g_], f'TTm{g_}'
                            ps, pn = pr()
                            kb.op('pe', lambda e: e.matmul(ps, lhsT=Ek, rhs=Ea, start=True, stop=True), reads=[En['k'], En['a']], writes=[pn])
                            tt(AakT, ps, mAT, ALU.mult, [pn, mATn], ['AakT'])
                            yield
                            ps, pn = pr(CH)
                            kb.op('pe', lambda e: e.matmul(ps, lhsT=Eb, rhs=rb, start=True, stop=True), reads=[En['b'], f'rbar{i_}'], writes=[pn])
                            tt(ArbT, ps, mInc, ALU.mult, [pn, mIncn], ['ArbT'])
                            yield
                            ps, pn = pr(CH)
                            kb.op('pe', lambda e: e.matmul(ps, lhsT=Ek, rhs=rb, start=True, stop=True), reads=[En['k'], f'rbar{i_}'], writes=[pn])
                            tt(ArkT, ps, mInc, ALU.mult, [pn, mIncn], ['ArkT'])
                            yield
                            for src, sn, dst, dn in ((Eb, En['b'], Btok, 'Btok'), (Ek, En['k'], Ktok, 'Ktok'), (Ev, En['v'], Vbd, 'Vbd')):
                                ps, pn = pr()
                                kb.op('pe', lambda e: e.matmul(ps, lhsT=src, rhs=ident_b, start=True, stop=True), reads=[sn, 'ident_b'], writes=[pn])
                                cpy(dst, ps, [pn], [dn])
                                yield
                            ps, pn = pr()
                            kb.op('pe', lambda e: e.matmul(ps, lhsT=Ea, rhs=Sb, start=True, stop=False), reads=[En['a'], 'Sb'], writes=[pn])
                            kb.op('pe', lambda e: e.matmul(ps, lhsT=AakT, rhs=Vbd, start=False, stop=True), reads=['AakT', 'Vbd'], writes=[pn])
                            cpy(Xf, ps, [pn], ['Xf'])
                            yield
                            ps, pn = pr()
                            kb.op('pe', lambda e: e.matmul(ps, lhsT=TT, rhs=Xf, start=True, stop=True), reads=[TTn, 'Xf'], writes=[pn])
                            cpy(Ub, ps, [pn], ['Ub'])
                            yield
                            ps, pn = pr(CH)
                            kb.op('pe', lambda e: e.matmul(ps, lhsT=Sb, rhs=rb, start=True, stop=False), reads=['Sb', f'rbar{i_}'], writes=[pn])
                            kb.op('pe', lambda e: e.matmul(ps, lhsT=Ub, rhs=ArbT, start=False, stop=False), reads=['Ub', 'ArbT'], writes=[pn])
                            kb.op('pe', lambda e: e.matmul(ps, lhsT=Vbd, rhs=ArkT, start=False, stop=True), reads=['Vbd', 'ArkT'], writes=[pn])
                            cpy(ybuf[i_][:, ch * CH:(ch + 1) * CH], ps, [pn], [f'ybuf{i_}'])
                            yield
                            ps, pn = pr()
                            kb.op('pe', lambda e: e.matmul(ps, lhsT=Btok, rhs=Ub, start=True, stop=False), reads=['Btok', 'Ub'], writes=[pn])
                            kb.op('pe', lambda e: e.matmul(ps, lhsT=Ktok, rhs=Vbd, start=False, stop=True), reads=['Ktok', 'Vbd'], writes=[pn])
                            tt(S, S, ps, ALU.add, ['S', pn], ['S'])
                            yield
                            kb.op('dve', lambda e: e.tensor_scalar_mul(out=S, in0=S, scalar1=ec), reads=['S', f'eb{i_}'], writes=['S'])
                            cpy(Sb, S, ['S'], ['Sb'], 'act')
                            yield
                            yield
                        kb.dma(f's3y{i_}', rws[f'y{d}'][cs, t0:t0 + TB], ybuf[i_], reads=[f'ybuf{i_}'], writes=['rw_y'], q='pool')
                return chain
            rw_chains = [rw_factory(k) for k in range(4)]
            rw_units = [(c, d) for c in range(6) for d in range(2)]
            for u0 in range(0, 12, 4):
                drive([rw_chains[k](*rw_units[u0 + k]) for k in range(4)])
            kb.barrier()
        if stop_after == f'P2b_{l}':
            esl.close()
            break

        with ExitStack() as e4:
            def sb4(name, shape, dt=F32):
                return e4.enter_context(nc.sbuf_tensor(un(name), list(shape), dt)).ap()
            def hg_factory(k):
                ld = [{n: sb4(f'g{n}_{i}', [128, TB]) for n in ['q', 'lf', 'kf', 'iv']} for i in range(2)]
                bcum = sb4('hbcum', [128, TB]); bcur_t = sb4('hbcur', [128, TB]); bm = sb4('hbm', [128, TB]); bc = sb4('hbc', [128, TB])
                ex1 = sb4('hex1', [128, TB]); ex2 = sb4('hex2', [128, TB]); ex3 = sb4('hex3', [128, TB])
                ebs = [sb4(f'heb{i}', [128, TB]) for i in range(2)]
                qn = [sb4(f'qn{i}', [128, TB], BF16) for i in range(2)]
                kn = [sb4(f'kn{i}', [128, TB], BF16) for i in range(2)]
                qh = [sb4(f'qh{i}', [128, TB], BF16) for i in range(2)]
                kh = [sb4(f'kh{i}', [128, TB], BF16) for i in range(2)]
                ivb = [sb4(f'ivb{i}', [128, TB], BF16) for i in range(2)]
                obuf = [sb4(f'obuf{i}', [128, TB]) for i in range(2)]
                S = sb4('hS', [128, 128]); Sb = sb4('hSb', [128, 128], BF16)
                scb = sb4('scb', [CH, CH], BF16); Vtok = sb4('Vtok', [CH, 128], BF16); Ktk = sb4('Ktk', [CH, 128], BF16)
                bi = 0

                def chain(c, d):
                    nonlocal bi
                    cs = slice(c * 128, (c + 1) * 128)
                    ivrow = RWC + 3 * RWW + c * 128
                    kb.op('dve', lambda e: e.memset(S, 0.0), writes=['hS'])
                    kb.op('dve', lambda e: e.memset(Sb, 0.0), writes=['hSb'])
                    msk, mskn = (cm['tle'], 'c_tle') if d == 0 else (cm['tge'], 'c_tge')
                    blocks = list(range(NBLK)) if d == 0 else list(reversed(range(NBLK)))
                    for blk in blocks:
                        i_ = bi % 2
                        bi += 1
                        t0 = blk * TB
                        Lt = ld[i_]
                        kb.dma(f'l4q{i_}', Lt['q'], hgs['q'][cs, t0:t0 + TB], reads=['hgin'], writes=[f'gq_{i_}'])
                        kb.dma(f'l4lf{i_}', Lt['lf'], hgs[f'lf{d}'][cs, t0:t0 + TB], reads=['hgin'], writes=[f'glf_{i_}'])
                        kb.dma(f'l4kf{i_}', Lt['kf'], hgs[f'kf{d}'][cs, t0:t0 + TB], reads=['hgin'], writes=[f'gkf_{i_}'])
                        kb.dma(f'l4iv{i_}', Lt['iv'], proj[ivrow:ivrow + 128, 1 + t0:1 + t0 + TB], reads=['proj'], writes=[f'giv_{i_}'])

                        def gn(n):
                            return f'g{n}_{i_}'
                        kb.op('dve', lambda e: e.tensor_tensor_scan(out=bcum, data0=cm['rst'], data1=Lt['lf'], initial=0.0, op0=ALU.mult, op1=ALU.add),
                              reads=['c_rst', gn('lf')], writes=['hbcum'])
                        if d == 0:
                            bcur, bcn = bcum, 'hbcum'
                            cidx = CH - 1
                        else:
                            tot = v3(bcum)[:, :, CH - 1:CH].broadcast_to([128, NCH, CH])
                            tt(v3(bcur_t), tot, v3(bcum), ALU.subtract, ['hbcum'], ['hbcur'])
                            yield
                            tt(bcur_t, bcur_t, Lt['lf'], ALU.add, ['hbcur', gn('lf')], ['hbcur'])
                            yield
                            bcur, bcn = bcur_t, 'hbcur'
                            cidx = 0
                        ref = v3(bcur)[:, :, 31:32].broadcast_to([128, NCH, CH])
                        tt(v3(bm), v3(bcur), ref, ALU.subtract, [bcn], ['hbm'])
                        bC = v3(bcur)[:, :, cidx:cidx + 1].broadcast_to([128, NCH, CH])
                        tt(v3(bc), v3(bcur), bC, ALU.subtract, [bcn], ['hbc'], 'pool')
                        eb = ebs[i_]
                        kb.op('act', lambda e: e.activation(out=ex1, in_=bm, func=AF.Exp), reads=['hbm'], writes=['hex1'])
                        kb.op('act', lambda e: e.activation(out=ex2, in_=bm, func=AF.Exp, scale=-1.0), reads=['hbm'], writes=['hex2'])
                        kb.op('act', lambda e: e.activation(out=eb, in_=bcur, func=AF.Exp), reads=[bcn], writes=[f'heb{i_}'])
                        kb.op('act', lambda e: e.activation(out=ex3, in_=bc, func=AF.Exp, scale=-1.0), reads=['hbc'], writes=['hex3'])
                        tt(qn[i_], Lt['q'], ex1, ALU.mult, [gn('q'), 'hex1'], [f'qn{i_}'])
                        tt(kn[i_], Lt['kf'], ex2, ALU.mult, [gn('kf'), 'hex2'], [f'kn{i_}'], 'pool')
                        tt(qh[i_], Lt['q'], eb, ALU.mult, [gn('q'), f'heb{i_}'], [f'qh{i_}'])
                        tt(kh[i_], Lt['kf'], ex3, ALU.mult, [gn('kf'), 'hex3'], [f'kh{i_}'], 'pool')
                        cpy(ivb[i_], Lt['iv'], [gn('iv')], [f'ivb{i_}'], 'act')
                        yield
                        chunks = list(range(NCH)) if d == 0 else list(reversed(range(NCH)))
                        for ch in chunks:
                            sl = slice(ch * CH, (ch + 1) * CH)
                            ec = eb[:, ch * CH + cidx:ch * CH + cidx + 1]
                            ps, pn = pr(CH)
                            kb.op('pe', lambda e: e.matmul(ps[:CH], lhsT=kn[i_][:, sl], rhs=qn[i_][:, sl], start=True, stop=True), reads=[f'kn{i_}', f'qn{i_}'], writes=[pn])
                            tt(scb, ps[:CH], msk, ALU.mult, [pn, mskn], ['scb'])
                            yield
                            ps, pn = pr()
                            kb.op('pe', lambda e: e.matmul(ps[:CH], lhsT=ivb[i_][:, sl], rhs=ident_b, start=True, stop=True), reads=[f'ivb{i_}', 'ident_b'], writes=[pn])
                            cpy(Vtok, ps[:CH], [pn], ['Vtok'])
                            yield
                            ps, pn = pr()
                            kb.op('pe', lambda e: e.matmul(ps[:CH], lhsT=kh[i_][:, sl], rhs=ident_b, start=True, stop=True), reads=[f'kh{i_}', 'ident_b'], writes=[pn])
                            cpy(Ktk, ps[:CH], [pn], ['Ktk'])
                            yield
                            ps, pn = pr(CH)
                            kb.op('pe', lambda e: e.matmul(ps, lhsT=Sb, rhs=qh[i_][:, sl], start=True, stop=False), reads=['hSb', f'qh{i_}'], writes=[pn])
                            kb.op('pe', lambda e: e.matmul(ps, lhsT=Vtok, rhs=scb, start=False, stop=True), reads=['Vtok', 'scb'], writes=[pn])
                            cpy(obuf[i_][:, sl], ps, [pn], [f'obuf{i_}'])
                            yield
                            ps, pn = pr()
                            kb.op('pe', lambda e: e.matmul(ps, lhsT=Ktk, rhs=Vtok, start=True, stop=True), reads=['Ktk', 'Vtok'], writes=[pn])
                            kb.op('dve', lambda e: e.scalar_tensor_tensor(out=S, in0=S, scalar=ec, in1=ps, op0=ALU.mult, op1=ALU.add),
                                  reads=['hS', pn, f'heb{i_}'], writes=['hS'])
                            cpy(Sb, S, ['hS'], ['hSb'], 'act')
                            yield
                            yield
                        kb.dma(f's4o{i_}', hgs[f'o{d}'][cs, t0:t0 + TB], obuf[i_], reads=[f'obuf{i_}'], writes=['hg_o'], q='pool')
                return chain
            hg_chains = [hg_factory(k) for k in range(3)]
            units = [(c, d) for c in range(6) for d in range(2)]
            for u0 in range(0, 12, 3):
                drive([hg_chains[k](*units[u0 + k]) for k in range(3)])
            kb.barrier()
        if stop_after == f'P2c_{l}':
            esl.close()
            break

        with ExitStack() as e5:
            def sb5(name, shape, dt=F32):
                return e5.enter_context(nc.sbuf_tensor(un(name), list(shape), dt)).ap()
            xall_ = sb5('xall', [128, 16, TB])
            hT = sb5('hT', [128, 16, TB], BF16)
            sq = [sb5(f'sq{i}', [128, TB], BF16) for i in range(2)]
            rstd = sb5('rstd', [128, TB])
            wtiles = [sb5(f'wt{i}', [128, 16, 512], BF16) for i in range(2)]
            stage = [sb5(f'stage{i}', [128, TB]) for i in range(4)]
            yrw = sb5('yrw', [128, 6, TB], BF16)
            yhg = sb5('yhg', [128, 6, TB], BF16)
            yxa = sb5('yxa', [128, 4, TB], BF16)
            mixed = sb5('mixed', [128, 16, TB], BF16)
            maskt = sb5('maskt', [128, TB])
            WK = [{n: sb5(f'k{n}_{i}', [128, TB]) for n in ['a', 'b', 'c', 'd', 'y', 'yc', 't', 'rs']} for i in range(2)]
            pT = [sb5(f'pT{i}', [128, TB], BF16) for i in range(2)]
            qb = sb5('qb', [128, TB], BF16)
            ev = {'n': 0}
            wkn = {'n': 0}
            GB = RWC + HGC + XAC
            for blk in range(NBLK):
                t0 = blk * TB
                kb.dma('lmask', maskt, maskd[:, 1 + t0:1 + t0 + TB], writes=['maskt'])
                kb.dma('lx0', xall_, xview(xin)[:, :, t0:t0 + TB], reads=['xin'], writes=['xall'])
                for c in range(6):
                    cs = slice(c * 128, (c + 1) * 128)
                    i_ = wkn['n'] % 2
                    wkn['n'] += 1
                    W = WK[i_]

                    def kn_(n):
                        return f'k{n}_{i_}'
                    for n, src in (('a', rws['y0']), ('b', rws['y1']), ('c', rws['bonus']), ('d', rws['g'])):
                        kb.dma(f'l5{n}{i_}', W[n], src[cs, t0:t0 + TB], reads=['rwsin'], writes=[kn_(n)])
                    tt(W['y'], W['a'], W['b'], ALU.add, [kn_('a'), kn_('b')], [kn_('y')])
                    ps, pn = PS[1], 'ps1'
                    kb.op('pe', lambda e: e.matmul(ps, lhsT=cm['bones'], rhs=W['y'], start=True, stop=True), reads=['c_bones', kn_('y')], writes=[pn])
                    kb.op('dve', lambda e: e.scalar_tensor_tensor(out=W['yc'], in0=ps, scalar=-1.0 / 64, in1=W['y'], op0=ALU.mult, op1=ALU.add),
                          reads=[pn, kn_('y')], writes=[kn_('yc')])
                    tt(W['t'], W['yc'], W['yc'], ALU.mult, [kn_('yc')], [kn_('t')], 'pool')
                    ps, pn = PS[2], 'ps2'
                    kb.op('pe', lambda e: e.matmul(ps, lhsT=cm['bones'], rhs=W['t'], start=True, stop=True), reads=['c_bones', kn_('t')], writes=[pn])
                    kb.op('dve', lambda e: e.tensor_scalar(out=W['rs'], in0=ps, scalar1=1.0 / 64, scalar2=64e-5, op0=ALU.mult, op1=ALU.add),
                          reads=[pn], writes=[kn_('rs')])
                    rsqrt_(W['rs'], kn_('rs'))
                    tt(W['yc'], W['yc'], W['rs'], ALU.mult, [kn_('yc'), kn_('rs')], [kn_('yc')])
                    kb.op('dve', lambda e: e.tensor_scalar(out=W['yc'], in0=W['yc'], scalar1=V(f'lnw{l}', c), scalar2=V(f'lnb{l}', c), op0=ALU.mult, op1=ALU.add),
                          reads=[kn_('yc'), 'vecs'], writes=[kn_('yc')])
                    tt(W['yc'], W['yc'], W['c'], ALU.add, [kn_('yc'), kn_('c')], [kn_('yc')], 'pool')
                    tt(yrw[:, c, :], W['yc'], W['d'], ALU.mult, [kn_('yc'), kn_('d')], ['yrw'])
                for c in range(6):
                    cs = slice(c * 128, (c + 1) * 128)
                    i_ = wkn['n'] % 2
                    wkn['n'] += 1
                    W = WK[i_]
                    ogr = RWC + 4 * RWW + c * 128
                    kb.dma(f'l5a{i_}', W['a'], hgs['o0'][cs, t0:t0 + TB], reads=['hgin'], writes=[kn_('a')])
                    kb.dma(f'l5b{i_}', W['b'], hgs['o1'][cs, t0:t0 + TB], reads=['hgin'], writes=[kn_('b')])
                    kb.dma(f'l5c{i_}', W['c'], proj[ogr:ogr + 128, 1 + t0:1 + t0 + TB], reads=['proj'], writes=[kn_('c')])
                    tt(W['y'], W['a'], W['b'], ALU.add, [kn_('a'), kn_('b')], [kn_('y')])
                    tt(W['t'], W['y'], W['y'], ALU.mult, [kn_('y')], [kn_('t')], 'pool')
                    ps, pn = PS[1], 'ps1'
                    kb.op('pe', lambda e: e.matmul(ps, lhsT=cm['ones'], rhs=W['t'], start=True, stop=True), reads=['c_ones', kn_('t')], writes=[pn])
                    kb.op('dve', lambda e: e.tensor_scalar(out=W['rs'], in0=ps, scalar1=1.0 / 128, scalar2=1e-6, op0=ALU.mult, op1=ALU.add),
                          reads=[pn], writes=[kn_('rs')])
                    rsqrt_(W['rs'], kn_('rs'))
                    tt(W['y'], W['y'], W['rs'], ALU.mult, [kn_('y'), kn_('rs')], [kn_('y')])
                    kb.op('act', lambda e: e.activation(out=W['d'], in_=W['c'], func=AF.Silu), reads=[kn_('c')], writes=[kn_('d')])
                    kb.op('dve', lambda e: e.scalar_tensor_tensor(out=yhg[:, c, :], in0=W['y'], scalar=V(f'hgn{l}', c), in1=W['d'], op0=ALU.mult, op1=ALU.mult),
                          reads=[kn_('y'), kn_('d'), 'vecs'], writes=['yhg'])
                for hd in range(4):
                    i_ = wkn['n'] % 2
                    wkn['n'] += 1
                    W = WK[i_]
                    qr = RWC + HGC + hd * 128
                    kb.dma(f'l5a{i_}', W['a'], proj[qr:qr + 128, 1 + t0:1 + t0 + TB], reads=['proj'], writes=[kn_('a')])
                    cpy(qb, W['a'], [kn_('a')], ['qb'], 'dve')
                    for mc in range(2):
                        ps, pn = PS[1 + mc], f'ps{1 + mc}'
                        kb.op('pe', lambda e: e.matmul(ps, lhsT=mkT_b[:, hd, mc * 128:(mc + 1) * 128], rhs=qb, start=True, stop=True), reads=['mkT_b', 'qb'], writes=[pn])
                        kb.op('act', lambda e: e.activation(out=pT[mc], in_=ps, func=AF.Exp, scale=float(128 ** -0.5)), reads=[pn], writes=[f'pT{mc}'])
                    ps, pn = PS[3], 'ps3'
                    for mc in range(2):
                        kb.op('pe', lambda e: e.matmul(ps, lhsT=ones_b, rhs=pT[mc], start=(mc == 0), stop=(mc == 1)), reads=['ones_b', f'pT{mc}'], writes=[pn])
                    kb.op('dve', lambda e: e.reciprocal(out=W['rs'], in_=ps), reads=[pn], writes=[kn_('rs')])
                    ps, pn = PS[1], 'ps1'
                    for mc in range(2):
                        kb.op('pe', lambda e: e.matmul(ps, lhsT=mv_b[:, mc, hd * 128:(hd + 1) * 128], rhs=pT[mc], start=(mc == 0), stop=(mc == 1)), reads=['mv_b', f'pT{mc}'], writes=[pn])
                    tt(yxa[:, hd, :], ps, W['rs'], ALU.mult, [pn, kn_('rs')], ['yxa'])
                Wv = wb['w_branch_out'][l].rearrange("(kc p) m -> p kc m", p=128)
                for g0 in range(0, D, 512):
                    s_ = lin_state['g'] % 2
                    lin_state['g'] += 1
                    wt = wtiles[s_]
                    kb.dma(f'lw{s_}', wt, Wv[:, :, g0:g0 + 512], reads=['wb_wbo'], writes=[f'wt{s_}'])
                    for j0 in range(0, 512, 128):
                        oc = (g0 + j0) // 128
                        i_ = wkn['n'] % 2
                        wkn['n'] += 1
                        W = WK[i_]
                        pss = []
                        for br, (k0, k1, src, srcn) in enumerate(((0, 6, yrw, 'yrw'), (6, 12, yhg, 'yhg'), (12, 16, yxa, 'yxa'))):
                            pi = 4 + lin_state['ps'] % 4
                            lin_state['ps'] += 1
                            ps = PS[pi]
                            for kc in range(k0, k1):
                                kb.op('pe', lambda e: e.matmul(ps, lhsT=wt[:, kc, j0:j0 + 128], rhs=src[:, kc - k0, :], start=(kc == k0), stop=(kc == k1 - 1)),
                                      reads=[f'wt{s_}', srcn], writes=[f'ps{pi}'])
                            pss.append((ps, f'ps{pi}'))
                            gr = GB + br * D + oc * 128
                            kb.dma(f'l5{"abc"[br]}{i_}', W['abc'[br]], proj[gr:gr + 128, 1 + t0:1 + t0 + TB], reads=['proj'], writes=[kn_('abc'[br])])
                        tt(W['y'], pss[0][0], W['a'], ALU.mult, [pss[0][1], kn_('a')], [kn_('y')])
                        tt(W['t'], pss[1][0], W['b'], ALU.mult, [pss[1][1], kn_('b')], [kn_('t')])
                        tt(W['y'], W['y'], W['t'], ALU.add, [kn_('y'), kn_('t')], [kn_('y')], 'pool')
                        tt(W['t'], pss[2][0], W['c'], ALU.mult, [pss[2][1], kn_('c')], [kn_('t')])
                        tt(mixed[:, oc, :], W['y'], W['t'], ALU.add, [kn_('y'), kn_('t')], ['mixed'], 'pool')

                def evac_o(mi, msz, ps, psname):
                    i2 = wkn['n'] % 2
                    wkn['n'] += 1
                    tt(WK[i2]['t'], ps, xall_[:, mi, :], ALU.add, [psname, 'xall'], [f'kt_{i2}'])
                    tt(xall_[:, mi, :], WK[i2]['t'], maskt, ALU.mult, [f'kt_{i2}', 'maskt'], ['xall'], 'pool')
                linear(wb['w_o'][l], 'wb_w_o', 16, D, lambda kc: mixed[:, kc, :], ['mixed'], wtiles, evac_o)
                kb.dma('sxm', xview(xm)[:, :, t0:t0 + TB], xall_, reads=['xall'], writes=['xm'], q='pool')
                rmsnorm_block(xall_, 'xall', f'n2g{l}', hT, 'hT', PS[0], 'ps0', sq, rstd)

                def evac_u(mi, msz, ps, psname):
                    s2 = ev['n'] % 4
                    ev['n'] += 1
                    cpy(stage[s2], ps, [psname], [f'stage{s2}'])
                    kb.dma(f'st{s2}', ud[mi * 128:(mi + 1) * 128, 1 + t0:1 + t0 + TB], stage[s2], reads=[f'stage{s2}'], writes=['u'], q='pool')
                linear(wb['ffn_up'][l], 'wb_ffn_up', 16, 2 * DFF, lambda kc: hT[:, kc, :], ['hT'], wtiles, evac_u)
            kb.barrier()
        if stop_after == f'P3_{l}':
            esl.close()
            break

        with ExitStack() as e6:
            def sb6(name, shape, dt=F32):
                return e6.enter_context(nc.sbuf_tensor(un(name), list(shape), dt)).ap()
            xall_ = sb6('xall', [128, 16, TB])
            gT = sb6('gT', [128, 43, TB], BF16)
            wt2 = [sb6(f'wt{i}', [128, 43, 256], BF16) for i in range(2)]
            W_ = TB + 2
            win = [sb6(f'win{i}', [128, W_]) for i in range(4)]
            at = [sb6(f'at{i}', [128, TB]) for i in range(2)]
            lt_ = [sb6(f'lt{i}', [128, TB]) for i in range(2)]
            maskt = sb6('maskt', [128, TB])
            if last:
                hfin = sb6('hfin', [128, 16, TB])
                sq = [sb6(f'sq{i}', [128, TB], BF16) for i in range(2)]
                rstd = sb6('rstd', [128, TB])
            wn = {'n': 0}
            for blk in range(NBLK):
                t0 = blk * TB
                kb.dma('lx0', xall_, xview(xm)[:, :, t0:t0 + TB], reads=['xm'], writes=['xall'])
                kb.dma('lmask', maskt, maskd[:, 1 + t0:1 + t0 + TB], writes=['maskt'])
                for j in range(43):
                    i_ = j % 2
                    outs = []
                    for tile_, tn, chn in ((at[i_], f'at{i_}', j), (lt_[i_], f'lt{i_}', 43 + j)):
                        s_ = wn['n'] % 4
                        wn['n'] += 1
                        kb.dma(f'lwin{s_}', win[s_], ud[chn * 128:(chn + 1) * 128, t0:t0 + W_], reads=['u'], writes=[f'win{s_}'])
                        conv3(tile_, tn, win[s_], f'win{s_}', None, wts=[V(f'fc{l}_{k_}', chn) for k_ in range(3)])
                    kb.op('act', lambda e: e.activation(out=at[i_], in_=at[i_], func=AF.Gelu), reads=[f'at{i_}'], writes=[f'at{i_}'])
                    tt(gT[:, j, :], at[i_], lt_[i_], ALU.mult, [f'at{i_}', f'lt{i_}'], ['gT'], 'pool')

                def evac_d(mi, msz, ps, psname):
                    i2 = mi % 2
                    tt(at[i2], ps, maskt, ALU.mult, [psname, 'maskt'], [f'at{i2}'])
                    tt(xall_[:, mi, :], xall_[:, mi, :], at[i2], ALU.add, [f'at{i2}', 'xall'], ['xall'], 'pool')
                linear(wb['ffn_down'][l], 'wb_ffn_down', 43, D, lambda kc: gT[:, kc, :], ['gT'], wt2, evac_d, G=256)
                if not last:
                    kb.dma('sxs', xview(xs)[:, :, t0:t0 + TB], xall_, reads=['xall'], writes=['xs'], q='pool')
                else:
                    rmsnorm_block(xall_, 'xall', 'fng', hfin, 'hfin', PS[0], 'ps0', sq, rstd)
                    kb.dma('sy', xview(yT)[:, :, t0:t0 + TB], hfin, reads=['hfin'], writes=['yT'], q='pool')
            kb.barrier()
        esl.close()
        xin = xout

    kb.barrier()
    es.close()
    kb.close()
    return nc


def host_inputs(inp, TP, seqs):
    common = {'vecs': pack_vecs(inp)}
    for k, v in const_mats().items():
        common['c_' + k] = v
    for nm, r, c in BIGW:
        common[nm] = np.ascontiguousarray(inp[nm], np.float32)
    common['rw_w_up'] = np.ascontiguousarray(inp['rw_w_up'].reshape(L * 2, 96, RWW), np.float32)
    common['rw_a_up'] = np.ascontiguousarray(inp['rw_a_up'].reshape(L * 2, 96, RWW), np.float32)
    common['rw_g_up'] = np.ascontiguousarray(inp['rw_g_up'], np.float32)
    common['rw_v_down'] = np.ascontiguousarray(inp['rw_v_down'][0], np.float32)
    common['rw_v_up'] = np.ascontiguousarray(inp['rw_v_up'][0], np.float32)
    maps = []
    for x, mem in seqs:
        T = x.shape[0]
        xt = np.zeros((D, TP), np.float32)
        xt[:, :T] = x.T
        mask = np.zeros((128, TP + 2), np.float32)
        mask[:, 1:1 + T] = 1.0
        m = dict(common)
        m['xT'] = xt
        m['memT'] = np.ascontiguousarray(mem.T, np.float32)
        m['mask'] = mask
        maps.append(m)
    return maps


def run(inp, debug=(), stop_after=None, ncores=8):
    xp, xs = np.asarray(inp['x_prompt']), np.asarray(inp['x_sample'])
    mp, ms = np.asarray(inp['mem_prompt']), np.asarray(inp['mem_sample'])
    seqs = [(xp[b], mp[b]) for b in range(xp.shape[0])] + [(xs[b], ms[b]) for b in range(xs.shape[0])]
    Tmax = max(s[0].shape[0] for s in seqs)
    TP = ((Tmax + TB - 1) // TB) * TB
    while len(seqs) < ncores:
        seqs.append((np.zeros((TB, D), np.float32), np.zeros((NMEM, D), np.float32)))
    inp = {k: np.asarray(v) for k, v in inp.items()}
    maps = host_inputs(inp, TP, seqs)
    nc = build(TP, debug=debug, stop_after=stop_after)
    res = run_bass_kernel_spmd(nc, maps, core_ids=list(range(ncores)))
    return res, seqs, TP


def kernel(**inputs):
    res, seqs, TP = run(inputs)
    xp, xs = inputs['x_prompt'], inputs['x_sample']
    outs = []
    i = 0
    for arr in (xp, xs):
        B, T, _ = arr.shape
        o = np.zeros((B, T, D), np.float32)
        for b in range(B):
            o[b] = res.results[i]['yT'][:, :T].T
            i += 1
        outs.append(o)
    return tuple(outs)
```
